# Optimizing a Trainium2 kernel written in Bass

```python
import math, functools
import jax, jax.numpy as jnp
from jax import lax
import numpy as np

D_MODEL = 1024
BATCH = 4
SEQ = 8192
DEPTH = 2
DEC_BATCH = 16
DEC_SEQ = 2048
PAST_LEN = 128

HEAD_DIM = 64
ATTN_WIDTH = D_MODEL // 2
N_Q_HEADS = ATTN_WIDTH // HEAD_DIM
N_KV_HEADS = 2
KV_WIDTH = N_KV_HEADS * HEAD_DIM
WINDOW = 128
BLOCK = 128
ROPE_THETA = 10000.0
POOL_WINDOWS = (2, 4, 8, 16)
N_POOL_GROUPS = len(POOL_WINDOWS)
POOL_WIDTH = D_MODEL // 2
POOL_GROUP_DIM = POOL_WIDTH // N_POOL_GROUPS
IN_PROJ_WIDTH = ATTN_WIDTH + 2 * KV_WIDTH + POOL_WIDTH
MIX_WIDTH = ATTN_WIDTH + POOL_WIDTH
CONV_DIM = D_MODEL
CONV_WIDTH = 31
D_FF = -(-8 * D_MODEL // (3 * 256)) * 256
N_EVEN = (DEPTH + 1) // 2
N_ODD = DEPTH // 2
EPS = 1e-6
NEG_INF = -1e30

kernel_name = 'hybrid_window_gqa_pool_conformer_encoder'


def _rmsnorm(x, g):
    xf = x.astype(jnp.float32)
    y = xf * lax.rsqrt(jnp.mean(xf * xf, axis=-1, keepdims=True) + EPS)
    return (y * g.astype(jnp.float32)).astype(x.dtype)


def _rope(x, pos):
    half = HEAD_DIM // 2
    inv_freq = ROPE_THETA ** (-jnp.arange(0, half, dtype=jnp.float32) * 2.0 / HEAD_DIM)
    ang = pos.astype(jnp.float32)[:, None] * inv_freq[None, :]
    cos = jnp.cos(ang)[None, :, None, :]
    sin = jnp.sin(ang)[None, :, None, :]
    xf = x.astype(jnp.float32)
    x1, x2 = xf[..., :half], xf[..., half:]
    out = jnp.concatenate([x1 * cos - x2 * sin, x2 * cos + x1 * sin], axis=-1)
    return out.astype(x.dtype)


def _windowed_gqa(q, k, v, sink):
    B, S = q.shape[0], q.shape[1]
    nb = S // BLOCK
    G = N_Q_HEADS // N_KV_HEADS
    qb = q.reshape(B, nb, BLOCK, N_KV_HEADS, G, HEAD_DIM)

    def band(t):
        tp = jnp.pad(t, ((0, 0), (BLOCK, BLOCK), (0, 0), (0, 0)))
        tp = tp.reshape(B, nb + 2, BLOCK, N_KV_HEADS, HEAD_DIM)
        return jnp.concatenate([tp[:, :-2], tp[:, 1:-1], tp[:, 2:]], axis=2)

    kb, vb = band(k), band(v)
    scores = jnp.einsum('bnqhgd,bnjhd->bnhgqj', qb, kb,
                        preferred_element_type=jnp.float32) * (HEAD_DIM ** -0.5)
    blk = jnp.arange(nb)[:, None, None]
    qpos = blk * BLOCK + jnp.arange(BLOCK)[None, :, None]
    kpos = blk * BLOCK - BLOCK + jnp.arange(3 * BLOCK)[None, None, :]
    mask = (jnp.abs(kpos - qpos) <= WINDOW) & (kpos >= 0) & (kpos < S)
    scores = jnp.where(mask[None, :, None, None], scores, NEG_INF)
    sink_l = sink.astype(jnp.float32).reshape(N_KV_HEADS, G)[None, None, :, :, None, None]
    m = jnp.maximum(jnp.max(scores, axis=-1, keepdims=True), sink_l)
    p = jnp.exp(scores - m)
    p = p / (jnp.sum(p, axis=-1, keepdims=True) + jnp.exp(sink_l - m))
    out = jnp.einsum('bnhgqj,bnjhd->bnqhgd', p.astype(v.dtype), vb)
    return out.reshape(B, S, N_Q_HEADS * HEAD_DIM)


def _multiscale_pool(u, w_pool, pool_scale):
    B, S = u.shape[0], u.shape[1]
    pos = jnp.arange(S)
    outs = []
    for gi, w in enumerate(POOL_WINDOWS):
        half = w // 2
        ug = u[..., gi * POOL_GROUP_DIM:(gi + 1) * POOL_GROUP_DIM].astype(jnp.float32)
        up = jnp.pad(ug, ((0, 0), (half, half), (0, 0)))
        cs = jnp.pad(jnp.cumsum(up, axis=1), ((0, 0), (1, 0), (0, 0)))
        wsum = cs[:, w:w + S] - cs[:, :S]
        cnt = (jnp.minimum(pos + half, S) - jnp.maximum(pos - half, 0)).astype(jnp.float32)
        outs.append(wsum / cnt[None, :, None] - ug)
    pooled = jnp.stack(outs, axis=2)
    mixed = jnp.einsum('bsgc,gcd->bsgd', pooled, w_pool.astype(jnp.float32))
    mixed = mixed.reshape(B, S, POOL_WIDTH) * pool_scale.astype(jnp.float32)
    return mixed.astype(u.dtype)


def _attn_pool_mixer(h, w_in, sink, w_pool, pool_scale, w_out):
    B, S, _ = h.shape
    z = h @ w_in
    q = z[..., :ATTN_WIDTH].reshape(B, S, N_Q_HEADS, HEAD_DIM)
    k = z[..., ATTN_WIDTH:ATTN_WIDTH + KV_WIDTH].reshape(B, S, N_KV_HEADS, HEAD_DIM)
    v = z[..., ATTN_WIDTH + KV_WIDTH:ATTN_WIDTH + 2 * KV_WIDTH].reshape(B, S, N_KV_HEADS, HEAD_DIM)
    u = z[..., ATTN_WIDTH + 2 * KV_WIDTH:]
    pos = jnp.arange(S)
    a = _windowed_gqa(_rope(q, pos), _rope(k, pos), v, sink)
    p = _multiscale_pool(u, w_pool, pool_scale)
    return jnp.concatenate([a, p], axis=-1) @ w_out


def _conformer_conv(h, w_pw1, b_pw1, w_dw, b_dw, ln_g, ln_b, w_pw2, b_pw2):
    z = h @ w_pw1 + b_pw1
    g = z[..., :CONV_DIM] * jax.nn.sigmoid(z[..., CONV_DIM:])
    y = lax.conv_general_dilated(
        g, w_dw[:, None, :].astype(g.dtype), window_strides=(1,),
        padding=((CONV_WIDTH // 2, CONV_WIDTH // 2),),
        dimension_numbers=('NWC', 'WIO', 'NWC'),
        feature_group_count=CONV_DIM) + b_dw
    yf = y.astype(jnp.float32)
    mu = jnp.mean(yf, axis=-1, keepdims=True)
    var = jnp.mean(jnp.square(yf - mu), axis=-1, keepdims=True)
    yf = (yf - mu) * lax.rsqrt(var + EPS) * ln_g.astype(jnp.float32) + ln_b.astype(jnp.float32)
    y = jax.nn.silu(yf).astype(h.dtype)
    return y @ w_pw2 + b_pw2


def _swiglu(h, w_gate, w_up, w_down):
    return (jax.nn.silu(h @ w_gate) * (h @ w_up)) @ w_down


def _trunk(x, mix_pre_g, mix_post_g, ffn_pre_g, ffn_post_g,
           w_in, attn_sink, w_pool, pool_scale, w_out,
           conv_w_pw1, conv_b_pw1, conv_w_dw, conv_b_dw, conv_ln_g, conv_ln_b,
           conv_w_pw2, conv_b_pw2, ffn_w_gate, ffn_w_up, ffn_w_down):
    for layer in range(DEPTH):
        h = _rmsnorm(x, mix_pre_g[layer])
        if layer % 2 == 0:
            e = layer // 2
            m = _attn_pool_mixer(h, w_in[e], attn_sink[e], w_pool[e], pool_scale[e], w_out[e])
        else:
            o = layer // 2
            m = _conformer_conv(h, conv_w_pw1[o], conv_b_pw1[o], conv_w_dw[o], conv_b_dw[o],
                                conv_ln_g[o], conv_ln_b[o], conv_w_pw2[o], conv_b_pw2[o])
        x = x + _rmsnorm(m, mix_post_g[layer])
        h = _rmsnorm(x, ffn_pre_g[layer])
        f = _swiglu(h, ffn_w_gate[layer], ffn_w_up[layer], ffn_w_down[layer])
        x = x + _rmsnorm(f, ffn_post_g[layer])
    return x


def setup_inputs(seed: int = 0) -> dict:
    key = jax.random.key(seed)
    ks = jax.random.split(key, 24)
    f32 = jnp.float32
    nrm = lambda k, shape, s: jax.random.normal(k, shape, f32) * s
    gain = lambda k, shape: 1.0 + 0.05 * jax.random.normal(k, shape, f32)
    return {
        'x_prompt': jax.random.normal(ks[0], (BATCH, SEQ, D_MODEL), f32),
        'x_sample': jax.random.normal(ks[1], (DEC_BATCH, DEC_SEQ, D_MODEL), f32),
        'mix_pre_g': gain(ks[2], (DEPTH, D_MODEL)),
        'mix_post_g': gain(ks[3], (DEPTH, D_MODEL)),
        'ffn_pre_g': gain(ks[4], (DEPTH, D_MODEL)),
        'ffn_post_g': gain(ks[5], (DEPTH, D_MODEL)),
        'w_in': nrm(ks[6], (N_EVEN, D_MODEL, IN_PROJ_WIDTH), D_MODEL ** -0.5),
        'attn_sink': nrm(ks[7], (N_EVEN, N_Q_HEADS), 0.5),
        'w_pool': nrm(ks[8], (N_EVEN, N_POOL_GROUPS, POOL_GROUP_DIM, POOL_GROUP_DIM), POOL_GROUP_DIM ** -0.5),
        'pool_scale': gain(ks[9], (N_EVEN, POOL_WIDTH)),
        'w_out': nrm(ks[10], (N_EVEN, MIX_WIDTH, D_MODEL), MIX_WIDTH ** -0.5),
        'conv_w_pw1': nrm(ks[11], (N_ODD, D_MODEL, 2 * CONV_DIM), D_MODEL ** -0.5),
        'conv_b_pw1': nrm(ks[12], (N_ODD, 2 * CONV_DIM), 0.02),
        'conv_w_dw': nrm(ks[13], (N_ODD, CONV_WIDTH, CONV_DIM), CONV_WIDTH ** -0.5),
        'conv_b_dw': nrm(ks[14], (N_ODD, CONV_DIM), 0.02),
        'conv_ln_g': gain(ks[15], (N_ODD, CONV_DIM)),
        'conv_ln_b': nrm(ks[16], (N_ODD, CONV_DIM), 0.02),
        'conv_w_pw2': nrm(ks[17], (N_ODD, CONV_DIM, D_MODEL), CONV_DIM ** -0.5),
        'conv_b_pw2': nrm(ks[18], (N_ODD, D_MODEL), 0.02),
        'ffn_w_gate': nrm(ks[19], (DEPTH, D_MODEL, D_FF), D_MODEL ** -0.5),
        'ffn_w_up': nrm(ks[20], (DEPTH, D_MODEL, D_FF), D_MODEL ** -0.5),
        'ffn_w_down': nrm(ks[21], (DEPTH, D_FF, D_MODEL), D_FF ** -0.5),
    }


def reference(x_prompt, x_sample, mix_pre_g, mix_post_g, ffn_pre_g, ffn_post_g,
              w_in, attn_sink, w_pool, pool_scale, w_out,
              conv_w_pw1, conv_b_pw1, conv_w_dw, conv_b_dw, conv_ln_g, conv_ln_b,
              conv_w_pw2, conv_b_pw2, ffn_w_gate, ffn_w_up, ffn_w_down):
    trunk = functools.partial(
        _trunk, mix_pre_g=mix_pre_g, mix_post_g=mix_post_g, ffn_pre_g=ffn_pre_g,
        ffn_post_g=ffn_post_g, w_in=w_in, attn_sink=attn_sink, w_pool=w_pool,
        pool_scale=pool_scale, w_out=w_out, conv_w_pw1=conv_w_pw1, conv_b_pw1=conv_b_pw1,
        conv_w_dw=conv_w_dw, conv_b_dw=conv_b_dw, conv_ln_g=conv_ln_g, conv_ln_b=conv_ln_b,
        conv_w_pw2=conv_w_pw2, conv_b_pw2=conv_b_pw2, ffn_w_gate=ffn_w_gate,
        ffn_w_up=ffn_w_up, ffn_w_down=ffn_w_down)
    y_prompt = trunk(x_prompt)
    y_sample = trunk(x_sample)
    return (y_prompt, y_sample)
```

```python
import numpy as np
import ml_dtypes
import concourse.bass as bass
import concourse.mybir as mybir
from concourse.bass_utils import run_bass_kernel_spmd

F32 = mybir.dt.float32
BF16 = mybir.dt.bfloat16
AF = mybir.ActivationFunctionType
ALU = mybir.AluOpType

D = 1024
T = 512
NT = 16
KC = 8
DFF = 2816
FC = 22
import os
NW = 5
ACC_BANKS = tuple(int(c) for c in '01267')
UW = 2816
NU = 85
EPS = 1e-6
NTILES = NT

V_PRE0, V_PRE1, V_POST0, V_POST1 = 0, 8, 16, 24
V_FPRE0, V_FPRE1, V_FPOST0, V_FPOST1 = 32, 40, 48, 56
V_PSCALE = 64
V_BPW1 = 68
V_BDW = 84
V_LNG = 92
V_LNB = 100
V_BPW2 = 108
V_SINK = 116
V_FLAG = 120
V_WDW = 121
V_CORR = 369
NV = V_CORR + 128


def U_IN(i): return i
U_POOL = 8
def U_OUT(i): return 9 + i
def U_G(l, j): return 13 + l * 30 + 2 * j
def U_U(l, j): return 13 + l * 30 + 2 * j + 1
def U_D(l, m): return 13 + l * 30 + 22 + m
def U_PW1(i): return 73 + i
def U_PW2(i): return 81 + i


UNIT_N = [2048] * NU
UNIT_N[7] = 1024
UNIT_N[8] = 512
for _l in range(2):
    for _m in range(8):
        UNIT_N[13 + _l * 30 + 22 + _m] = 2816


class _Op:
    __slots__ = ("eng", "fn", "deps", "idx", "needs_inc", "sem", "semval", "is_dma")


class Rec:
    ENGS = ("sync", "pe", "act", "dve", "pool")

    def __init__(self):
        self.ops = {e: [] for e in self.ENGS}
        self.lastw = {}
        self.readers = {}
        self.dma_count = {}
        self.fence_deps = {e: [] for e in self.ENGS}

    def _add(self, eng, fn, reads, writes, sem=None):
        o = _Op()
        o.eng = eng; o.fn = fn; o.idx = None; o.needs_inc = False
        o.sem = sem; o.is_dma = sem is not None; o.semval = None
        deps = []
        if self.fence_deps[eng]:
            deps.extend(self.fence_deps[eng]); self.fence_deps[eng] = []
        for k in reads:
            w = self.lastw.get(k)
            if w is not None:
                deps.append(w)
        for k in writes:
            w = self.lastw.get(k)
            if w is not None:
                deps.append(w)
            rd = self.readers.get(k)
            if rd:
                deps.extend(rd.values())
        fl = []
        seen = set()
        for d in deps:
            if id(d) in seen:
                continue
            seen.add(id(d))
            if (not d.is_dma) and (not o.is_dma) and d.eng == eng and eng == "pe":
                continue
            fl.append(d)
        o.deps = fl
        for d in fl:
            d.needs_inc = True
        if o.is_dma:
            c = self.dma_count.get(sem, 0) + 1
            self.dma_count[sem] = c
            o.semval = 16 * c
        rk = ("dma", sem) if o.is_dma else eng
        for k in reads:
            self.readers.setdefault(k, {})[rk] = o
        for k in writes:
            self.lastw[k] = o
            self.readers[k] = {}
        self.ops[eng].append(o)
        return o

    def op(self, eng, fn, reads=(), writes=()):
        return self._add(eng, fn, reads, writes)

    def dma(self, eng, fn, reads, writes, sem):
        return self._add(eng, fn, reads, writes, sem=sem)

    def fence(self):
        deps = []
        for e in self.ENGS:
            for o in reversed(self.ops[e]):
                if not o.is_dma:
                    deps.append(o); break
        lastdma = {}
        for e in self.ENGS:
            for o in self.ops[e]:
                if o.is_dma:
                    lastdma[o.sem] = o
        deps.extend(lastdma.values())
        for e in self.ENGS:
            self.fence_deps[e] = list(deps)

    def finalize(self):
        for e in self.ENGS:
            c = 0
            for o in self.ops[e]:
                if o.is_dma:
                    continue
                if o.needs_inc:
                    c += 1
                    o.idx = c

    def emit(self, eng_name, eng, sems):
        seen = {}
        for o in self.ops[eng_name]:
            for d in o.deps:
                if d.is_dma:
                    key = ("dma", d.sem); val = d.semval; sh = sems[d.sem]
                else:
                    key = d.eng; val = d.idx; sh = sems["eng_" + d.eng]
                if seen.get(key, 0) >= val:
                    continue
                eng.wait_ge(sh, val)
                seen[key] = val
            ins = o.fn(eng)
            if o.is_dma:
                ins.then_inc(sems[o.sem], 16)
            elif o.needs_inc:
                ins.then_inc(sems["eng_" + eng_name], 1)


def build_program():
    nc = bass.Bass("TRN2", target_bir_lowering=False)
    R = Rec()

    xT = nc.dram_tensor("xT", [NT, 128, KC, T], F32, kind="ExternalInput").ap()
    yT = nc.dram_tensor("yT", [NT, 128, KC, T], F32, kind="ExternalOutput").ap()
    cosT = nc.dram_tensor("cosT", [NT, 128, T], F32, kind="ExternalInput").ap()
    sinT = nc.dram_tensor("sinT", [NT, 128, T], F32, kind="ExternalInput").ap()
    wsrc = nc.dram_tensor("wsrc", [NU, 128, UW], F32, kind="ExternalInput").ap()
    vecs_d = nc.dram_tensor("vecs", [128, NV], F32, kind="ExternalInput").ap()
    masks_d = nc.dram_tensor("masks", [4, 128, 128], BF16, kind="ExternalInput").ap()
    wsc = nc.dram_tensor("wsc", [NU, 128, UW], BF16, kind="Internal").ap()

    import contextlib
    from collections import deque
    es = contextlib.ExitStack()

    def sb(name, shape, dt):
        return es.enter_context(nc.sbuf_tensor(name, shape, dt))

    with es:
        fbuf = sb("fbuf", [128, KC, T], F32)
        ybuf = sb("ybuf", [128, KC, T], F32)
        xr = [sb("xr0", [128, KC, T], F32), sb("xr1", [128, KC, T], F32)]
        hb = sb("hb", [128, KC, T], BF16)
        sq = sb("sq", [128, 2, T], BF16)
        kz = [sb("kz0", [128, 3, T], BF16), sb("kz1", [128, 3, T], BF16)]
        vz = [sb("vz0", [128, 12, 128], BF16), sb("vz1", [128, 12, 128], BF16)]
        qb_ = sb("qb", [128, 2, 4, T], BF16)
        ub = sb("ub", [128, 3, 4, 528], BF16)
        mix = sb("mix", [128, KC, T], BF16)
        ptb = sb("ptb", [128, 3, T], BF16)
        actb = sb("actb", [128, FC, T], BF16)
        sgb = sb("sgb", [128, 2, T], BF16)
        gb = sb("gb", [128, 2, KC, 544], BF16)
        wring = sb("wring", [128, NW, UW], BF16)
        cosb = sb("cosb", [128, T], F32)
        sinb = sb("sinb", [128, T], F32)
        tmp = sb("tmp", [128, 4, T], F32)
        vec = sb("vec", [128, NV], F32)
        maskb = sb("maskb", [128, 4, 128], BF16)
        ones = sb("ones", [128, 128], BF16)
        onesz = sb("onesz", [128, 2, 128], BF16)
        esink = sb("esink", [128, 4, 128], F32)
        es4 = sb("es4", [128, 4], F32)
        bg2 = sb("bg2", [128, 8], F32)
        ttail = sb("ttail", [128, 4, KC, 16], F32)
        ps = [es.enter_context(nc.psum_tensor("ps%d" % i, [128, T], F32)) for i in range(8)]

        sem_names = (["eng_" + e for e in Rec.ENGS] + ["w%d" % i for i in range(NW)] +
                     ["xa", "cos", "sin", "xr0", "xr1", "st0", "st1", "consts", "consts2",
                      "p32_0", "p32_1", "p32_2", "p16_0", "p16_1", "p16_2", "p16_3"])
        sems = {n: es.enter_context(nc.semaphore(n)) for n in sem_names}

        def vcol(c):
            return vec[:, c:c + 1]

        def bcast_cols(col0, cstride, n_mid, n_last):
            base = vec[:, col0:col0 + 1]
            return bass.AP(base.tensor, base.offset, [list(base.ap[0]), [cstride, n_mid], [0, n_last]])

        def mask_bc(mi):
            base = maskb[:, mi, :]
            return bass.AP(base.tensor, base.offset, [list(base.ap[0]), [0, 4], [1, 128]])

        R.dma("sync", lambda e: e.dma_start(out=vec[:], in_=vecs_d), [], ["vec"], "consts")
        R.dma("sync", lambda e: e.dma_start(out=maskb[:], in_=masks_d.rearrange("m p q -> p m q")),
              [], ["maskb"], "consts2")
        R.op("pool", lambda e: e.memset(ones[:], 1.0), [], ["ones"])
        R.op("pool", lambda e: e.memset(onesz[:], 0.0), [], ["onesz"])
        R.op("pool", lambda e: e.memset(onesz[:, 0, 0:64], 1.0), [], ["onesz"])
        R.op("pool", lambda e: e.memset(onesz[:, 1, 64:128], 1.0), [], ["onesz"])
        R.op("pool", lambda e: e.memset(kz[0][:], 0.0), [], [("kz", 0), ("kz", 1), ("kz", 2)])
        R.op("pool", lambda e: e.memset(kz[1][:], 0.0), [], [("kz", 0), ("kz", 1), ("kz", 2)])
        R.op("pool", lambda e: e.memset(vz[0][:], 0.0), [], [("vz", 0), ("vz", 1), ("vz", 2)])
        R.op("pool", lambda e: e.memset(vz[1][:], 0.0), [], [("vz", 0), ("vz", 1), ("vz", 2)])
        R.op("pool", lambda e: e.memset(esink[:], 0.0), [], ["esink"])
        R.op("act", lambda e: e.activation(out=es4[:], in_=vec[:, V_SINK:V_SINK + 4], func=AF.Exp),
             ["vec"], ["es4"])
        for c in range(4):
            R.op("dve", (lambda c: lambda e: e.tensor_scalar(
                out=esink[:, c, :], in0=esink[:, c, :], scalar1=es4[:, c:c + 1], scalar2=None,
                op0=ALU.add))(c), ["es4", "esink"], ["esink"])
        R.op("dve", lambda e: e.tensor_tensor(out=bg2[:], in0=vec[:, V_BPW2:V_BPW2 + 8],
                                               in1=vec[:, V_POST1:V_POST1 + 8], op=ALU.mult),
             ["vec"], ["bg2"])

        st32 = [fbuf, xr[0], xr[1]]
        st32k = ["P32_0", "P32_1", "P32_2"]
        for u in range(NU):
            a = u % 3
            b = u % 4
            n = UNIT_N[u]
            src32 = st32[a][:].rearrange("p k t -> p (k t)")[:, 0:n]
            dst16 = actb[:].rearrange("p k t -> p (k t)")[:, b * UW:b * UW + n]
            R.dma("sync", (lambda u, src32, n: lambda e: e.dma_start(out=src32, in_=wsrc[u][:, 0:n]))(u, src32, n),
                  [], [st32k[a]], "p32_%d" % a)
            if u % 2 == 0:
                R.op("act", (lambda s_, d_: lambda e: e.copy(out=d_, in_=s_))(src32, dst16),
                     [st32k[a]], [("P16", b)])
            else:
                R.op("dve", (lambda s_, d_: lambda e: e.tensor_copy(out=d_, in_=s_))(src32, dst16),
                     [st32k[a]], [("P16", b)])
            R.dma("sync", (lambda u, d_, n: lambda e: e.dma_start(out=wsc[u][:, 0:n], in_=d_))(u, dst16, n),
                  [("P16", b)], [("wsc", u)], "p16_%d" % b)
        R.fence()

        state = {"acc": 0, "w": 0, "sq": 0, "S": 0, "pt": 0, "sg": 0}
        bgq = deque()
        tailq = deque()

        def drain_tail(n=None):
            k = 0
            while tailq and (n is None or k < n):
                tailq.popleft()()
                k += 1

        def drain(n=None):
            k = 0
            while bgq and (n is None or k < n):
                bgq.popleft()()
                k += 1

        def nextacc():
            b = ACC_BANKS[state["acc"] % len(ACC_BANKS)]
            state["acc"] += 1
            return b

        def wload(uid):
            n = UNIT_N[uid]
            slot = state["w"] % NW
            state["w"] += 1
            R.dma("sync", (lambda uid, slot, n: lambda e: e.dma_start(
                out=wring[:, slot, 0:n], in_=wsc[uid][:, 0:n]))(uid, slot, n),
                [("wsc", uid)], [("w", slot)], "w%d" % slot)
            return slot

        def mm(out, lhsT, rhs, start, stop, reads, writes):
            R.op("pe", lambda e: e.matmul(out, lhsT, rhs, start=start, stop=stop), reads, writes)

        def act(out, in_, func, reads, writes, bias=None, scale=None):
            kw = {}
            if bias is not None:
                kw["bias"] = bias
            if scale is not None:
                kw["scale"] = scale
            R.op("act", lambda e: e.activation(out=out, in_=in_, func=func, **kw), reads, writes)

        def tt(eng, out, in0, in1, op, reads, writes):
            R.op(eng, lambda e: e.tensor_tensor(out=out, in0=in0, in1=in1, op=op), reads, writes)

        def stt(eng, out, in0, scalar, in1, op0, op1, reads, writes):
            R.op(eng, lambda e: e.scalar_tensor_tensor(out=out, in0=in0, scalar=scalar, in1=in1,
                                                       op0=op0, op1=op1), reads, writes)

        def ts(eng, out, in0, s1, s2, op0, op1, reads, writes):
            if s2 is None:
                R.op(eng, lambda e: e.tensor_scalar(out=out, in0=in0, scalar1=s1, scalar2=None,
                                                    op0=op0), reads, writes)
            else:
                R.op(eng, lambda e: e.tensor_scalar(out=out, in0=in0, scalar1=s1, scalar2=s2,
                                                    op0=op0, op1=op1), reads, writes)

        def rstd_from(bank, out_tmp):
            o = tmp[:, out_tmp, :]
            act(o, ps[bank][:], AF.Ln, [("ps", bank)], [("tmp", out_tmp)], scale=1.0 / D, bias=EPS)
            act(o, o, AF.Exp, [("tmp", out_tmp)], [("tmp", out_tmp)], scale=-0.5)

        def stats_mm(bank, kc, src_ap, src_key, bias=None):
            i = state["sq"] % 2
            state["sq"] += 1
            act(sq[:, i, :], src_ap, AF.Square, [src_key], [("sq", i)], bias=bias)
            mm(ps[bank][:], ones[:], sq[:, i, :], kc == 0, kc == KC - 1,
               [("sq", i), "ones"], [("ps", bank)])

        def prenorm(src, srckey, gcol, rt, dst, dstkey):
            bank = 3
            for kc in range(KC):
                stats_mm(bank, kc, src[:, kc, :], srckey(kc))
            rstd_from(bank, rt)
            for kc in range(KC):
                stt("dve", dst[:, kc, :], src[:, kc, :], vcol(gcol + kc), tmp[:, rt, :],
                    ALU.mult, ALU.mult, [srckey(kc), ("tmp", rt), "vec"], [dstkey(kc)])

        hkey = lambda kc: ("h", kc)
        mkey = lambda kc: ("mix", kc)

        def proj_chunk(bank, slot, mw, sub, rhs_of, rhs_key, nk):
            for kc in range(nk):
                mm(ps[bank][:], wring[:, slot, kc * mw + sub * 128: kc * mw + sub * 128 + 128],
                   rhs_of(kc), kc == 0, kc == nk - 1, [("w", slot), rhs_key(kc)], [("ps", bank)])

        def postnorm_residual(xslot, gcol, produce, rt, bias_col=None, bgcol=None, ndrain=0, final=False):
            sbank = 3
            banks = {}
            pend = []

            def evac(m):
                bk = banks[m]
                if bias_col is None:
                    act(fbuf[:, m, :], ps[bk][:], AF.Identity, [("ps", bk), "vec"], [("fbuf", m)],
                        scale=vcol(gcol + m))
                    sqb = None
                else:
                    act(fbuf[:, m, :], ps[bk][:], AF.Identity, [("ps", bk), "vec", "bg2"],
                        [("fbuf", m)], scale=vcol(gcol + m), bias=bgcol[:, m:m + 1])
                    sqb = vcol(bias_col + m)
                i = state["sq"] % 2
                state["sq"] += 1
                act(sq[:, i, :], ps[bk][:], AF.Square, [("ps", bk), "vec"], [("sq", i)], bias=sqb)
                pend.append((m, i))

            def flush_one():
                m, i = pend.pop(0)
                mm(ps[sbank][:], ones[:], sq[:, i, :], m == 0, m == KC - 1,
                   [("sq", i), "ones"], [("ps", sbank)])

            for m in range(KC):
                banks[m] = produce(m)
                evac(m)
                if len(pend) > 1:
                    flush_one()
                if m < KC - 3:
                    drain(ndrain)
            while pend:
                flush_one()
            rstd_from(sbank, rt)
            for m in range(KC):
                tt("pool" if final else "dve", fbuf[:, m, :], fbuf[:, m, :], tmp[:, rt, :], ALU.mult,
                   [("fbuf", m), ("tmp", rt)], [("fbuf", m)])
                if m % 4 == 3 and not final:
                    tt("dve", xr[xslot][:, m, :], xr[xslot][:, m, :], fbuf[:, m, :], ALU.add,
                       [("fbuf", m), ("xr", xslot, m)], [("xr", xslot, m)])
                else:
                    tt("pool", xr[xslot][:, m, :], xr[xslot][:, m, :], fbuf[:, m, :], ALU.add,
                       [("fbuf", m), ("xr", xslot, m)], [("xr", xslot, m)])

        def ffn(layer, xslot, rt, ndrain=0, final=False):
            prenorm(xr[xslot], lambda kc: ("xr", xslot, kc), V_FPRE0 + 8 * layer, rt, hb, hkey)
            for j in range(11):
                sg_ = wload(U_G(layer, j))
                su_ = wload(U_U(layer, j))
                for sub in range(2):
                    ch = 2 * j + sub
                    bgt = nextacc()
                    bup = nextacc()
                    proj_chunk(bgt, sg_, 256, sub, lambda kc: hb[:, kc, :], hkey, KC)
                    proj_chunk(bup, su_, 256, sub, lambda kc: hb[:, kc, :], hkey, KC)
                    i = state["sg"] % 2
                    state["sg"] += 1
                    act(sgb[:, i, :], ps[bgt][:], AF.Silu, [("ps", bgt)], [("sg", i)])
                    tt("dve", actb[:, ch, :], sgb[:, i, :], ps[bup][:], ALU.mult,
                       [("sg", i), ("ps", bup)], [("act", ch)])
                    drain(ndrain)

            def produce(m):
                sl = wload(U_D(layer, m))
                bk = nextacc()
                proj_chunk(bk, sl, 128, 0, lambda kc: actb[:, kc, :], lambda kc: ("act", kc), FC)
                return bk
            postnorm_residual(xslot, V_FPOST0 + 8 * layer, produce, rt, ndrain=ndrain, final=final)

        def A_load(s):
            R.dma("act", lambda e: e.dma_start(out=fbuf[:], in_=xT[s]), [],
                  [("fbuf", m) for m in range(KC)], "xa")
            R.dma("act", lambda e: e.dma_start(out=cosb[:], in_=cosT[s]), [], ["cos"], "cos")
            R.dma("act", lambda e: e.dma_start(out=sinb[:], in_=sinT[s]), [], ["sin"], "sin")

        def A_prep(s):
            prenorm(fbuf, lambda kc: ("fbuf", kc), V_PRE0, 0, mix, mkey)

        def A_proj(s, mid_hook=None):
            qs = s % 2
            ks = s % 3
            hr = lambda kc: mix[:, kc, :]
            hk = mkey
            for c in range(5):
                sl = wload(U_IN(c))
                b0 = nextacc(); b1 = nextacc()
                proj_chunk(b0, sl, 256, 0, hr, hk, KC)
                proj_chunk(b1, sl, 256, 1, hr, hk, KC)
                ta = 2 * (c % 2); tb = ta + 1
                tt("dve", fbuf[:, ta, :], ps[b0][:], cosb[:], ALU.mult, [("ps", b0), "cos"], [("fbuf", ta)])
                tt("dve", fbuf[:, tb, :], ps[b1][:], sinb[:], ALU.mult, [("ps", b1), "sin"], [("fbuf", tb)])
                if c < 4:
                    tt("pool", qb_[:, qs, c, :], fbuf[:, ta, :], fbuf[:, tb, :], ALU.add,
                       [("fbuf", ta), ("fbuf", tb)], [("q", qs, c)])
                else:
                    tt("pool", kz[0][0:64, ks, :], fbuf[0:64, ta, :], fbuf[0:64, tb, :], ALU.add,
                       [("fbuf", ta), ("fbuf", tb)], [("kz", ks)])
                    tt("pool", kz[1][64:128, ks, :], fbuf[64:128, ta, :], fbuf[64:128, tb, :], ALU.add,
                       [("fbuf", ta), ("fbuf", tb)], [("kz", ks)])
                if mid_hook is not None:
                    mid_hook(c)
            sl = wload(U_IN(5))
            bv = nextacc()
            for blk in range(4):
                for kc in range(KC):
                    mm(ps[bv][:, blk * 128:(blk + 1) * 128], mix[:, kc, blk * 128:(blk + 1) * 128],
                       wring[:, sl, kc * 256: kc * 256 + 128], kc == 0, kc == KC - 1,
                       [("w", sl), ("mix", kc)], [("ps", bv)])
            pv3 = ps[bv][:].rearrange("p (b d) -> p b d", b=4)
            R.op("act", lambda e: e.copy(out=vz[0][:, ks * 4:ks * 4 + 4, 0:64], in_=pv3[:, :, 0:64]),
                 [("ps", bv)], [("vz", ks)])
            R.op("act", lambda e: e.copy(out=vz[1][:, ks * 4:ks * 4 + 4, 64:128], in_=pv3[:, :, 64:128]),
                 [("ps", bv)], [("vz", ks)])
            us = s % 3

            def u_chunk(sl, mw, sub, gi):
                bk = nextacc()
                proj_chunk(bk, sl, mw, sub, hr, hk, KC)
                R.op("act", lambda e: e.copy(out=ub[:, us, gi, 8:520], in_=ps[bk][:]),
                     [("ps", bk)], [("u", us, "c")])
            u_chunk(sl, 256, 1, 0)
            drain(4)
            sl = wload(U_IN(6))
            u_chunk(sl, 256, 0, 1)
            drain(4)
            u_chunk(sl, 256, 1, 2)
            drain(4)
            sl = wload(U_IN(7))
            u_chunk(sl, 128, 0, 3)
            drain(4)
            halo(ub, "u", s, 8, 8, 520)

        def halo(buf, name, s, hw, c0, c1, RS=3):
            sl = s % RS
            if s == 0:
                R.op("pool", lambda e: e.memset(buf[:, 0, :, c0 - hw:c0], 0.0), [], [(name, 0, "l")])
            if s == NT - 1:
                R.op("pool", lambda e: e.memset(buf[:, sl, :, c1:c1 + hw], 0.0), [], [(name, sl, "r")])
            if s > 0:
                dsl = (s - 1) % RS
                dst = buf[:, dsl, :, c1:c1 + hw]
                src = buf[:, sl, :, c0:c0 + hw]
                if s % 4 == 0:
                    ts("pool", dst, src, vcol(V_FLAG), None, ALU.mult, None,
                       [(name, sl, "c"), "vec"], [(name, dsl, "r")])
                else:
                    R.op("pool", lambda e: e.tensor_copy(out=dst, in_=src),
                         [(name, sl, "c")], [(name, dsl, "r")])
            if s < NT - 1:
                dsl = (s + 1) % RS
                dst2 = buf[:, dsl, :, c0 - hw:c0]
                src2 = buf[:, sl, :, c1 - hw:c1]
                if (s + 1) % 4 == 0:
                    ts("pool", dst2, src2, vcol(V_FLAG), None, ALU.mult, None,
                       [(name, sl, "c"), "vec"], [(name, dsl, "l")])
                else:
                    R.op("pool", lambda e: e.tensor_copy(out=dst2, in_=src2),
                         [(name, sl, "c")], [(name, dsl, "l")])

        CM = 496

        def enqueue_conv_main(t):
            gs = t % 2
            gk = [("g", gs, "c"), ("g", gs, "l")]
            for j in range(31):
                for i in range(KC):
                    def thunk(i=i, j=j):
                        src = gb[:, gs, i, j + 1: j + 1 + CM]
                        wc = vcol(V_WDW + i * 31 + j)
                        if j == 0:
                            ts("dve", ybuf[:, i, 0:CM], src, wc, vcol(V_BDW + i), ALU.mult, ALU.add,
                               gk + ["vec"], [("y", i)])
                        else:
                            stt("dve", ybuf[:, i, 0:CM], src, wc, ybuf[:, i, 0:CM], ALU.mult, ALU.add,
                                gk + ["vec", ("y", i)], [("y", i)])
                    bgq.append(thunk)

        def conv_tail(t):
            gs = t % 2
            gk = [("g", gs, "c"), ("g", gs, "r")]
            ykt = ["ytail"]
            yt = ybuf[:, :, CM:T]
            for j in range(31):
                def thunk(j=j):
                    src = gb[:, gs, :, CM + j + 1: CM + j + 1 + 16]
                    wbc = bcast_cols(V_WDW + j, 31, KC, 16)
                    tsl = j % 4
                    tk = ("ttail", tsl)
                    tt("dve", ttail[:, tsl], src, wbc, ALU.mult, gk + ["vec"], [tk])
                    if j == 0:
                        tt("dve", yt, ttail[:, tsl], bcast_cols(V_BDW, 1, KC, 16), ALU.add, [tk, "vec"],
                           ykt + [("y", i) for i in range(KC)])
                    else:
                        tt("dve", yt, yt, ttail[:, tsl], ALU.add, [tk] + ykt, ykt)
                tailq.append(thunk)

        def stageB(t):
            xs = t % 2
            qs = t % 2
            us = t % 3
            U = ub[:, us]
            fb = fbuf[:].rearrange("p k t -> p (k t)")
            P1 = fb[:, 0:4 * 528].rearrange("p (g c) -> p g c", g=4)
            P2 = fb[:, 4 * 528:7 * 528].rearrange("p (g c) -> p g c", g=3)
            fall = [("fbuf", m) for m in range(KC)]
            ukeys = [("u", us, "c"), ("u", us, "l"), ("u", us, "r")]
            res = [P1[:, 0, 7:519], P2[:, 0, 6:518], P1[:, 2, 4:516], P2[:, 2, 0:512]]

            def pool_p0():
                tt("dve", P1[:, 0:4, 0:527], U[:, :, 0:527], U[:, :, 1:528], ALU.add, ukeys, fall)
                tt("dve", P2[:, 0:3, 0:525], P1[:, 1:4, 0:525], P1[:, 1:4, 2:527], ALU.add, fall, fall)

            def pool_p1():
                tt("dve", P1[:, 2:4, 0:521], P2[:, 1:3, 0:521], P2[:, 1:3, 4:525], ALU.add, fall, fall)
                tt("dve", P2[:, 2:3, 0:513], P1[:, 3:4, 0:513], P1[:, 3:4, 8:521], ALU.add, fall, fall)
                if t % 4 == 0:
                    tb = 0 if t == 0 else 2
                    for gi in range(4):
                        cc = V_CORR + tb * 32 + gi * 8
                        tt("dve", res[gi][:, 0:8], res[gi][:, 0:8], vec[:, cc:cc + 8], ALU.mult,
                           fall + ["vec"], fall)
                if t % 4 == 3:
                    tb = 1 if t == NT - 1 else 3
                    for gi in range(4):
                        cc = V_CORR + tb * 32 + gi * 8
                        tt("dve", res[gi][:, 504:512], res[gi][:, 504:512], vec[:, cc:cc + 8], ALU.mult,
                           fall + ["vec"], fall)

            def pool_stt(gis):
                for gi in gis:
                    w = (2, 4, 8, 16)[gi]
                    stt("dve", hb[:, gi, :], res[gi], 1.0 / w, U[:, gi, 8:520], ALU.mult, ALU.subtract,
                        fall + ukeys, [("h", gi)])
            pool_pieces = [lambda: None, pool_p0, pool_p1, lambda: pool_stt((0, 1, 2, 3))]
            for qb in range(4):
                n = 4 * t + qb
                contribs = []
                for j in (n - 1, n, n + 1):
                    if 0 <= j < 64:
                        for g in range(2):
                            contribs.append((g, j))
                rhs_q = qb_[:, qs, :, qb * 128:(qb + 1) * 128]
                info = []
                bnum, bden = (6, 7)

                def emitS(i):
                    g, j = contribs[i]
                    sbk = 4 + state["S"] % 2
                    state["S"] += 1
                    ksl = (j // 4) % 3
                    ko = (j % 4) * 128
                    mm(ps[sbk][:].rearrange("p (c q) -> p c q", c=4), kz[g][:, ksl, ko:ko + 128], rhs_q,
                       True, True, [("kz", ksl)] + [("q", qs, c) for c in range(4)], [("ps", sbk)])
                    pi = state["pt"] % 3
                    state["pt"] += 1
                    act(ptb[:, pi, :], ps[sbk][:], AF.Exp, [("ps", sbk)], [("pt", pi)], scale=0.125)
                    if j != n:
                        if j < n:
                            mi = 2 if (n % 16 == 0) else 0
                        else:
                            mi = 3 if (n % 16 == 15) else 1
                        p3 = ptb[:, pi, :].rearrange("p (c q) -> p c q", c=4)
                        tt("dve", p3, p3, mask_bc(mi), ALU.mult, [("pt", pi), "maskb"], [("pt", pi)])
                    info.append(pi)

                def emitPV(i):
                    g, j = contribs[i]
                    pi = info[i]
                    vsl = ((j // 4) % 3) * 4 + (j % 4)
                    first = i == 0
                    last = i == len(contribs) - 1
                    mm(ps[bnum][:], vz[g][:, vsl, :], ptb[:, pi, :], first, last,
                       [("vz", (j // 4) % 3), ("pt", pi)], [("ps", bnum)])
                    mm(ps[bden][:], onesz[:, g, :], ptb[:, pi, :], first, last,
                       ["onesz", ("pt", pi)], [("ps", bden)])

                emitS(0)
                for i in range(len(contribs)):
                    if i + 1 < len(contribs):
                        emitS(i + 1)
                    emitPV(i)
                    if i == 2:
                        pool_pieces[qb]()
                tt("dve", tmp[:, 1, :], ps[bden][:], esink[:].rearrange("p c q -> p (c q)"), ALU.add,
                   [("ps", bden), "esink"], [("tmp", 1)])
                R.op("dve", lambda e: e.tensor_copy(out=tmp[:, 2, :], in_=ps[bnum][:]), [("ps", bnum)], [("tmp", 2)])
                act(tmp[:, 1, :], tmp[:, 1, :], AF.Ln, [("tmp", 1)], [("tmp", 1)])
                act(tmp[:, 1, :], tmp[:, 1, :], AF.Exp, [("tmp", 1)], [("tmp", 1)], scale=-1.0)
                tt("dve", mix[:, 0:4, qb * 128:(qb + 1) * 128],
                   tmp[:, 2, :].rearrange("p (c q) -> p c q", c=4),
                   tmp[:, 1, :].rearrange("p (c q) -> p c q", c=4), ALU.mult,
                   [("tmp", 2), ("tmp", 1)], [("mix", c) for c in range(4)])
            R.dma("act", lambda e: e.dma_start(out=xr[xs][:], in_=xT[t]), [],
                  [("xr", xs, m) for m in range(KC)], "xr%d" % xs)
            slp = wload(U_POOL)
            for gi, w in enumerate((2, 4, 8, 16)):
                bk = nextacc()
                mm(ps[bk][:], wring[:, slp, gi * 128:(gi + 1) * 128], hb[:, gi, :], True, True,
                   [("w", slp), ("h", gi)], [("ps", bk)])
                act(mix[:, 4 + gi, :], ps[bk][:], AF.Identity, [("ps", bk), "vec"], [("mix", 4 + gi)],
                    scale=vcol(V_PSCALE + gi))
            wslots = {}

            def produce_out(m):
                if m % 2 == 0:
                    wslots[0] = wload(U_OUT(m // 2))
                bk = nextacc()
                proj_chunk(bk, wslots[0], 256, m % 2, lambda kc: mix[:, kc, :], mkey, KC)
                return bk
            postnorm_residual(xs, V_POST0, produce_out, 0, ndrain=2)
            ffn(0, xs, 0, ndrain=3)
            prenorm(xr[xs], lambda kc: ("xr", xs, kc), V_PRE1, 0, hb, hkey)
            gs = t % 2
            for i in range(KC):
                if i == 1 and t + 2 < NT:
                    A_load(t + 2)
                if i == 4 and t + 2 < NT:
                    A_prep(t + 2)
                sl = wload(U_PW1(i))
                ba = nextacc(); bb = nextacc()
                proj_chunk(ba, sl, 256, 0, lambda kc: hb[:, kc, :], hkey, KC)
                proj_chunk(bb, sl, 256, 1, lambda kc: hb[:, kc, :], hkey, KC)
                si = state["sg"] % 2
                state["sg"] += 1
                act(sgb[:, si, :], ps[bb][:], AF.Sigmoid, [("ps", bb), "vec"], [("sg", si)],
                    bias=vcol(V_BPW1 + 8 + i))
                stt("dve", gb[:, gs, i, 16:528], ps[ba][:], vcol(V_BPW1 + i), sgb[:, si, :],
                    ALU.add, ALU.mult, [("ps", ba), ("sg", si), "vec"], [("g", gs, "c")])
                if i < 4:
                    drain(6)
            drain()
            halo(gb, "g", t, 16, 16, 528, RS=2)

        silu_q = deque()

        def run_silus():
            while silu_q:
                silu_q.popleft()()

        def C_ln(t):
            yk = lambda i: ("y", i)
            bsum = nextacc(); bsq = 3
            for i in range(KC):
                si = state["sg"] % 2
                state["sg"] += 1
                R.op("act", (lambda si, i: lambda e: e.copy(out=sgb[:, si, :], in_=ybuf[:, i, :]))(si, i),
                     [yk(i), "ytail"], [("sg", si)])
                mm(ps[bsum][:], ones[:], sgb[:, si, :], i == 0, i == KC - 1, [("sg", si), "ones"],
                   [("ps", bsum)])
                stats_mm(bsq, i, ybuf[:, i, :], yk(i))
            mu = tmp[:, 1, :]; msq = tmp[:, 2, :]; rs = tmp[:, 3, :]; nmr = tmp[:, 2, :]
            ts("dve", mu, ps[bsum][:], 1.0 / D, None, ALU.mult, None, [("ps", bsum)], [("tmp", 1)])
            tt("dve", msq, mu, mu, ALU.mult, [("tmp", 1)], [("tmp", 2)])
            stt("dve", rs, ps[bsq][:], 1.0 / D, msq, ALU.mult, ALU.subtract, [("ps", bsq), ("tmp", 2)],
                [("tmp", 3)])
            act(rs, rs, AF.Ln, [("tmp", 3)], [("tmp", 3)], bias=EPS)
            act(rs, rs, AF.Exp, [("tmp", 3)], [("tmp", 3)], scale=-0.5)
            stt("dve", nmr, mu, -1.0, rs, ALU.mult, ALU.mult, [("tmp", 1), ("tmp", 3)], [("tmp", 2)])
            for i in (0, 1, 5, 2, 3, 6, 4, 7):
                ne = "pool" if i >= 5 else "dve"
                tt(ne, ybuf[:, i, :], ybuf[:, i, :], rs, ALU.mult, [yk(i), ("tmp", 3)], [yk(i)])
                tt(ne, ybuf[:, i, :], ybuf[:, i, :], nmr, ALU.add, [yk(i), ("tmp", 2)], [yk(i)])
                silu_q.append((lambda i: lambda: act(hb[:, i, :], ybuf[:, i, :], AF.Silu, [("y", i), "vec"],
                                                     [("h", i)], scale=vcol(V_LNG + i),
                                                     bias=vcol(V_LNB + i)))(i))

        def C_main(t):
            xs = t % 2
            wslots = {}

            def produce_pw2(m):
                if m % 2 == 0:
                    wslots[0] = wload(U_PW2(m // 2))
                bk = nextacc()
                proj_chunk(bk, wslots[0], 256, m % 2, lambda kc: hb[:, kc, :], hkey, KC)
                return bk
            postnorm_residual(xs, V_POST1, produce_pw2, 0, bias_col=V_BPW2, bgcol=bg2, ndrain=3)
            ffn(1, xs, 0, ndrain=3, final=True)
            R.dma("pool", lambda e: e.dma_start(out=yT[t], in_=xr[xs][:]),
                  [("xr", xs, m) for m in range(KC)], [("yT", t)], "st%d" % xs)

        A_load(0); A_prep(0); A_proj(0)
        A_load(1); A_prep(1); A_proj(1)
        for s in range(1, NT + 2):
            tb_, tc_ = s - 1, s - 2
            if 0 <= tb_ < NT:
                stageB(tb_)
            has_c = 0 <= tc_ < NT
            if has_c:
                conv_tail(tc_)
                drain_tail(10)

            enq = {"done": False}

            def hook(c, tc_=tc_, tb_=tb_):
                if c < 3:
                    drain_tail(16)
                elif c == 3:
                    drain(); drain_tail()
                    C_ln(tc_)
                else:
                    drain(5)
            if s + 1 < NT:
                A_proj(s + 1, mid_hook=hook if has_c else None)
            elif has_c:
                drain(); drain_tail()
                C_ln(tc_)
            run_silus()
            if 0 <= tb_ < NT and not enq["done"]:
                enqueue_conv_main(tb_)
            if has_c:
                C_main(tc_)
        drain()

        R.finalize()
        lastst = {}
        for o in R.ops["pool"]:
            if o.is_dma and o.sem in ("st0", "st1"):
                lastst[o.sem] = o

        with nc.Block() as block:
            @block.sync
            def _(eng):
                R.emit("sync", eng, sems)
                for o in lastst.values():
                    eng.wait_ge(sems[o.sem], o.semval)

            @block.tensor
            def _(eng):
                R.emit("pe", eng, sems)

            @block.scalar
            def _(eng):
                R.emit("act", eng, sems)

            @block.vector
            def _(eng):
                R.emit("dve", eng, sems)

            @block.gpsimd
            def _(eng):
                R.emit("pool", eng, sems)
    return nc


def _unit_cols(W, cols, kcs):
    sub = W[:, cols]
    mw = sub.shape[1]
    return sub.reshape(kcs, 128, mw).transpose(1, 0, 2).reshape(128, kcs * mw)


def _build_wsrc(inp):
    wsrc = np.zeros((NU, 128, UW), np.float32)
    w_in = inp["w_in"][0]
    hd = 64

    def qcols(c, swap):
        d = np.arange(64)
        dd = (d + 32) % 64 if swap else d
        return np.concatenate([c * hd + dd, (4 + c) * hd + dd])

    def kcols(swap):
        d = np.arange(64)
        dd = (d + 32) % 64 if swap else d
        return np.concatenate([512 + dd, 512 + 64 + dd])
    chunks = []
    for c in range(4):
        chunks.append(qcols(c, False)); chunks.append(qcols(c, True))
    chunks.append(kcols(False)); chunks.append(kcols(True))
    chunks.append(np.arange(640, 768))
    for gi in range(4):
        chunks.append(768 + gi * 128 + np.arange(128))
    for u in range(8):
        cols = np.concatenate(chunks[2 * u: 2 * u + 2])
        a = _unit_cols(w_in, cols, 8)
        wsrc[U_IN(u), :, :a.shape[1]] = a
    wp = inp["w_pool"][0]
    wsrc[U_POOL, :, :512] = wp.transpose(1, 0, 2).reshape(128, 512)
    rowperm = []
    for c in range(4):
        rowperm.extend(list(c * 64 + np.arange(64)))
        rowperm.extend(list((4 + c) * 64 + np.arange(64)))
    rowperm.extend(list(512 + np.arange(512)))
    w_out = inp["w_out"][0][np.array(rowperm), :]
    for i in range(4):
        wsrc[U_OUT(i), :, :2048] = _unit_cols(w_out, np.arange(256 * i, 256 * i + 256), 8)
    for l in range(2):
        wg = inp["ffn_w_gate"][l]; wu = inp["ffn_w_up"][l]; wd = inp["ffn_w_down"][l]
        for j in range(11):
            cols = np.arange(256 * j, 256 * j + 256)
            wsrc[U_G(l, j), :, :2048] = _unit_cols(wg, cols, 8)
            wsrc[U_U(l, j), :, :2048] = _unit_cols(wu, cols, 8)
        for m in range(8):
            wsrc[U_D(l, m), :, :] = _unit_cols(wd, np.arange(128 * m, 128 * m + 128), 22)
    pw1 = inp["conv_w_pw1"][0]
    for i in range(8):
        cols = np.concatenate([np.arange(128 * i, 128 * i + 128), 1024 + np.arange(128 * i, 128 * i + 128)])
        wsrc[U_PW1(i), :, :2048] = _unit_cols(pw1, cols, 8)
    pw2 = inp["conv_w_pw2"][0]
    for i in range(4):
        wsrc[U_PW2(i), :, :2048] = _unit_cols(pw2, np.arange(256 * i, 256 * i + 256), 8)
    return wsrc


def _colvec(v):
    n = v.shape[0] // 128
    return v.reshape(n, 128).T


def _build_vecs(inp, is_prompt):
    vec = np.zeros((128, NV), np.float32)
    vec[:, V_PRE0:V_PRE0 + 8] = _colvec(inp["mix_pre_g"][0])
    vec[:, V_PRE1:V_PRE1 + 8] = _colvec(inp["mix_pre_g"][1])
    vec[:, V_POST0:V_POST0 + 8] = _colvec(inp["mix_post_g"][0])
    vec[:, V_POST1:V_POST1 + 8] = _colvec(inp["mix_post_g"][1])
    vec[:, V_FPRE0:V_FPRE0 + 8] = _colvec(inp["ffn_pre_g"][0])
    vec[:, V_FPRE1:V_FPRE1 + 8] = _colvec(inp["ffn_pre_g"][1])
    vec[:, V_FPOST0:V_FPOST0 + 8] = _colvec(inp["ffn_post_g"][0])
    vec[:, V_FPOST1:V_FPOST1 + 8] = _colvec(inp["ffn_post_g"][1])
    vec[:, V_PSCALE:V_PSCALE + 4] = _colvec(inp["pool_scale"][0])
    vec[:, V_BPW1:V_BPW1 + 16] = _colvec(inp["conv_b_pw1"][0])
    vec[:, V_BDW:V_BDW + 8] = _colvec(inp["conv_b_dw"][0])
    vec[:, V_LNG:V_LNG + 8] = _colvec(inp["conv_ln_g"][0])
    vec[:, V_LNB:V_LNB + 8] = _colvec(inp["conv_ln_b"][0])
    vec[:, V_BPW2:V_BPW2 + 8] = _colvec(inp["conv_b_pw2"][0])
    sink = inp["attn_sink"][0]
    for c in range(4):
        vec[0:64, V_SINK + c] = sink[c]
        vec[64:128, V_SINK + c] = sink[4 + c]
    vec[:, V_FLAG] = 1.0 if is_prompt else 0.0
    wdw = inp["conv_w_dw"][0]
    for i in range(8):
        vec[:, V_WDW + i * 31: V_WDW + (i + 1) * 31] = wdw[:, i * 128:(i + 1) * 128].T
    ledge = np.ones((4, 8), np.float32); redge = np.ones((4, 8), np.float32)
    for gi, w in enumerate((2, 4, 8, 16)):
        half = w // 2
        for i in range(8):
            if i < half:
                ledge[gi, i] = np.float32(w) / np.float32(i + half)
            r = 7 - i
            if r < half - 1:
                redge[gi, i] = np.float32(w) / np.float32(r + 1 + half)
    lint = np.ones((4, 8), np.float32) if is_prompt else ledge
    rint = np.ones((4, 8), np.float32) if is_prompt else redge
    for tb, tab in enumerate((ledge, redge, lint, rint)):
        vec[:, V_CORR + tb * 32: V_CORR + (tb + 1) * 32] = tab.reshape(1, 32)
    return vec


def _build_rope(pos):
    half = 32
    inv_freq = (np.float32(10000.0) ** (-np.arange(0, half, dtype=np.float32) * np.float32(2.0) / np.float32(64))).astype(np.float32)
    ang = pos.astype(np.float32)[:, None] * inv_freq[None, :]
    cos = np.cos(ang).astype(np.float32); sin = np.sin(ang).astype(np.float32)
    p = np.arange(128)
    f = p % 32
    sign = np.where((p % 64) < 32, -1.0, 1.0).astype(np.float32)
    cosT = cos[:, f].T
    sinT = (sin[:, f] * sign[None, :]).T
    S = pos.shape[0]
    cosT = cosT.reshape(128, NT, T).transpose(1, 0, 2)
    sinT = sinT.reshape(128, NT, T).transpose(1, 0, 2)
    return np.ascontiguousarray(cosT), np.ascontiguousarray(sinT)


def _to_featmajor(xc):
    return np.ascontiguousarray(xc.reshape(NT, T, KC, 128).transpose(0, 3, 2, 1))


def _from_featmajor(y):
    return np.ascontiguousarray(y.transpose(0, 3, 2, 1).reshape(NT * T, D))


_NC_CACHE = {}


def kernel(**inputs):
    inp = {k: np.asarray(v) for k, v in inputs.items()}
    xp = inp["x_prompt"].astype(np.float32, copy=False)
    xs = inp["x_sample"].astype(np.float32, copy=False)
    wsrc = _build_wsrc(inp)
    kl = np.arange(128)[:, None]; ql = np.arange(128)[None, :]
    mP = (kl >= ql).astype(np.float32); mN = (kl <= ql).astype(np.float32)
    zero = np.zeros_like(mP)
    in_maps = []
    for c in range(8):
        is_prompt = c < 4
        if is_prompt:
            xc = xp[c]
            pos = np.arange(8192)
        else:
            xc = xs[4 * (c - 4): 4 * (c - 4) + 4].reshape(8192, D)
            pos = np.arange(8192) % 2048
        cosT, sinT = _build_rope(pos)
        masks = np.stack([mP, mN, mP if is_prompt else zero, mN if is_prompt else zero]).astype(ml_dtypes.bfloat16)
        in_maps.append({
            "xT": _to_featmajor(xc),
            "cosT": cosT, "sinT": sinT,
            "wsrc": wsrc,
            "vecs": _build_vecs(inp, is_prompt),
            "masks": masks,
        })
    if "nc" not in _NC_CACHE:
        _NC_CACHE["nc"] = build_program()
    nc = _NC_CACHE["nc"]
    res = run_bass_kernel_spmd(nc, in_maps, core_ids=list(range(8)))
    outs = [np.asarray(r["yT"]) for r in res.results]
    y_prompt = np.stack([_from_featmajor(outs[c]) for c in range(4)]).astype(np.float32)
    ys = [_from_featmajor(outs[c]).reshape(4, 2048, D) for c in range(4, 8)]
    y_sample = np.concatenate(ys, axis=0).astype(np.float32)
    return (y_prompt, y_sample)
```

```python
import numpy as np
import ml_dtypes
import concourse.bass as bass
import concourse.mybir as mybir
from concourse.bass_utils import run_bass_kernel_spmd

F32 = mybir.dt.float32
BF16 = mybir.dt.bfloat16
AF = mybir.ActivationFunctionType
ALU = mybir.AluOpType

D = 1024
T = 512
NT = 16
KC = 8
DFF = 2816
FC = 22
import os
NW = 5
ACC_BANKS = tuple(int(c) for c in '01267')
UW = 2816
NU = 85
EPS = 1e-6
NTILES = NT

V_PRE0, V_PRE1, V_POST0, V_POST1 = 0, 8, 16, 24
V_FPRE0, V_FPRE1, V_FPOST0, V_FPOST1 = 32, 40, 48, 56
V_PSCALE = 64
V_BPW1 = 68
V_BDW = 84
V_LNG = 92
V_LNB = 100
V_BPW2 = 108
V_SINK = 116
V_FLAG = 120
V_WDW = 121
V_CORR = 369
NV = V_CORR + 128


def U_IN(i): return i
U_POOL = 8
def U_OUT(i): return 9 + i
def U_G(l, j): return 13 + l * 30 + 2 * j
def U_U(l, j): return 13 + l * 30 + 2 * j + 1
def U_D(l, m): return 13 + l * 30 + 22 + m
def U_PW1(i): return 73 + i
def U_PW2(i): return 81 + i


UNIT_N = [2048] * NU
UNIT_N[7] = 1024
UNIT_N[8] = 512
for _l in range(2):
    for _m in range(8):
        UNIT_N[13 + _l * 30 + 22 + _m] = 2816


class _Op:
    __slots__ = ("eng", "fn", "deps", "idx", "needs_inc", "sem", "semval", "is_dma")


class Rec:
    ENGS = ("sync", "pe", "act", "dve", "pool")

    def __init__(self):
        self.ops = {e: [] for e in self.ENGS}
        self.lastw = {}
        self.readers = {}
        self.dma_count = {}
        self.fence_deps = {e: [] for e in self.ENGS}

    def _add(self, eng, fn, reads, writes, sem=None):
        o = _Op()
        o.eng = eng; o.fn = fn; o.idx = None; o.needs_inc = False
        o.sem = sem; o.is_dma = sem is not None; o.semval = None
        deps = []
        if self.fence_deps[eng]:
            deps.extend(self.fence_deps[eng]); self.fence_deps[eng] = []
        for k in reads:
            w = self.lastw.get(k)
            if w is not None:
                deps.append(w)
        for k in writes:
            w = self.lastw.get(k)
            if w is not None:
                deps.append(w)
            rd = self.readers.get(k)
            if rd:
                deps.extend(rd.values())
        fl = []
        seen = set()
        for d in deps:
            if id(d) in seen:
                continue
            seen.add(id(d))
            if (not d.is_dma) and (not o.is_dma) and d.eng == eng and eng == "pe":
                continue
            fl.append(d)
        o.deps = fl
        for d in fl:
            d.needs_inc = True
        if o.is_dma:
            c = self.dma_count.get(sem, 0) + 1
            self.dma_count[sem] = c
            o.semval = 16 * c
        rk = ("dma", sem) if o.is_dma else eng
        for k in reads:
            self.readers.setdefault(k, {})[rk] = o
        for k in writes:
            self.lastw[k] = o
            self.readers[k] = {}
        self.ops[eng].append(o)
        return o

    def op(self, eng, fn, reads=(), writes=()):
        return self._add(eng, fn, reads, writes)

    def dma(self, eng, fn, reads, writes, sem):
        return self._add(eng, fn, reads, writes, sem=sem)

    def fence(self):
        deps = []
        for e in self.ENGS:
            for o in reversed(self.ops[e]):
                if not o.is_dma:
                    deps.append(o); break
        lastdma = {}
        for e in self.ENGS:
            for o in self.ops[e]:
                if o.is_dma:
                    lastdma[o.sem] = o
        deps.extend(lastdma.values())
        for e in self.ENGS:
            self.fence_deps[e] = list(deps)

    def finalize(self):
        for e in self.ENGS:
            c = 0
            for o in self.ops[e]:
                if o.is_dma:
                    continue
                if o.needs_inc:
                    c += 1
                    o.idx = c

    def emit(self, eng_name, eng, sems):
        seen = {}
        for o in self.ops[eng_name]:
            for d in o.deps:
                if d.is_dma:
                    key = ("dma", d.sem); val = d.semval; sh = sems[d.sem]
                else:
                    key = d.eng; val = d.idx; sh = sems["eng_" + d.eng]
                if seen.get(key, 0) >= val:
                    continue
                eng.wait_ge(sh, val)
                seen[key] = val
            ins = o.fn(eng)
            if o.is_dma:
                ins.then_inc(sems[o.sem], 16)
            elif o.needs_inc:
                ins.then_inc(sems["eng_" + eng_name], 1)


def build_program():
    nc = bass.Bass("TRN2", target_bir_lowering=False)
    R = Rec()

    xT = nc.dram_tensor("xT", [NT, 128, KC, T], F32, kind="ExternalInput").ap()
    yT = nc.dram_tensor("yT", [NT, 128, KC, T], F32, kind="ExternalOutput").ap()
    cosT = nc.dram_tensor("cosT", [NT, 128, T], F32, kind="ExternalInput").ap()
    sinT = nc.dram_tensor("sinT", [NT, 128, T], F32, kind="ExternalInput").ap()
    wsrc = nc.dram_tensor("wsrc", [NU, 128, UW], F32, kind="ExternalInput").ap()
    vecs_d = nc.dram_tensor("vecs", [128, NV], F32, kind="ExternalInput").ap()
    masks_d = nc.dram_tensor("masks", [4, 128, 128], BF16, kind="ExternalInput").ap()
    wsc = nc.dram_tensor("wsc", [NU, 128, UW], BF16, kind="Internal").ap()

    import contextlib
    from collections import deque
    es = contextlib.ExitStack()

    def sb(name, shape, dt):
        return es.enter_context(nc.sbuf_tensor(name, shape, dt))

    with es:
        fbuf = sb("fbuf", [128, KC, T], F32)
        ybuf = sb("ybuf", [128, KC, T], F32)
        xr = [sb("xr0", [128, KC, T], F32), sb("xr1", [128, KC, T], F32)]
        hb = sb("hb", [128, KC, T], BF16)
        sq = sb("sq", [128, 2, T], BF16)
        kz = [sb("kz0", [128, 3, T], BF16), sb("kz1", [128, 3, T], BF16)]
        vz = [sb("vz0", [128, 12, 128], BF16), sb("vz1", [128, 12, 128], BF16)]
        qb_ = sb("qb", [128, 2, 4, T], BF16)
        ub = sb("ub", [128, 3, 4, 528], BF16)
        mix = sb("mix", [128, KC, T], BF16)
        ptb = sb("ptb", [128, 3, T], BF16)
        actb = sb("actb", [128, FC, T], BF16)
        sgb = sb("sgb", [128, 2, T], BF16)
        gb = sb("gb", [128, 2, KC, 544], BF16)
        wring = sb("wring", [128, NW, UW], BF16)
        cosb = sb("cosb", [128, T], F32)
        sinb = sb("sinb", [128, T], F32)
        tmp = sb("tmp", [128, 4, T], F32)
        vec = sb("vec", [128, NV], F32)
        maskb = sb("maskb", [128, 4, 128], BF16)
        ones = sb("ones", [128, 128], BF16)
        onesz = sb("onesz", [128, 2, 128], BF16)
        esink = sb("esink", [128, 4, 128], F32)
        es4 = sb("es4", [128, 4], F32)
        bg2 = sb("bg2", [128, 8], F32)
        ttail = sb("ttail", [128, 4, KC, 16], F32)
        ps = [es.enter_context(nc.psum_tensor("ps%d" % i, [128, T], F32)) for i in range(8)]

        sem_names = (["eng_" + e for e in Rec.ENGS] + ["w%d" % i for i in range(NW)] +
                     ["xa", "cos", "sin", "xr0", "xr1", "st0", "st1", "consts", "consts2",
                      "p32_0", "p32_1", "p32_2", "p16_0", "p16_1", "p16_2", "p16_3"])
        sems = {n: es.enter_context(nc.semaphore(n)) for n in sem_names}

        def vcol(c):
            return vec[:, c:c + 1]

        def bcast_cols(col0, cstride, n_mid, n_last):
            base = vec[:, col0:col0 + 1]
            return bass.AP(base.tensor, base.offset, [list(base.ap[0]), [cstride, n_mid], [0, n_last]])

        def mask_bc(mi):
            base = maskb[:, mi, :]
            return bass.AP(base.tensor, base.offset, [list(base.ap[0]), [0, 4], [1, 128]])

        R.dma("sync", lambda e: e.dma_start(out=vec[:], in_=vecs_d), [], ["vec"], "consts")
        R.dma("sync", lambda e: e.dma_start(out=maskb[:], in_=masks_d.rearrange("m p q -> p m q")),
              [], ["maskb"], "consts2")
        R.op("pool", lambda e: e.memset(ones[:], 1.0), [], ["ones"])
        R.op("pool", lambda e: e.memset(onesz[:], 0.0), [], ["onesz"])
        R.op("pool", lambda e: e.memset(onesz[:, 0, 0:64], 1.0), [], ["onesz"])
        R.op("pool", lambda e: e.memset(onesz[:, 1, 64:128], 1.0), [], ["onesz"])
        R.op("pool", lambda e: e.memset(kz[0][:], 0.0), [], [("kz", 0), ("kz", 1), ("kz", 2)])
        R.op("pool", lambda e: e.memset(kz[1][:], 0.0), [], [("kz", 0), ("kz", 1), ("kz", 2)])
        R.op("pool", lambda e: e.memset(vz[0][:], 0.0), [], [("vz", 0), ("vz", 1), ("vz", 2)])
        R.op("pool", lambda e: e.memset(vz[1][:], 0.0), [], [("vz", 0), ("vz", 1), ("vz", 2)])
        R.op("pool", lambda e: e.memset(esink[:], 0.0), [], ["esink"])
        R.op("act", lambda e: e.activation(out=es4[:], in_=vec[:, V_SINK:V_SINK + 4], func=AF.Exp),
             ["vec"], ["es4"])
        for c in range(4):
            R.op("dve", (lambda c: lambda e: e.tensor_scalar(
                out=esink[:, c, :], in0=esink[:, c, :], scalar1=es4[:, c:c + 1], scalar2=None,
                op0=ALU.add))(c), ["es4", "esink"], ["esink"])
        R.op("dve", lambda e: e.tensor_tensor(out=bg2[:], in0=vec[:, V_BPW2:V_BPW2 + 8],
                                               in1=vec[:, V_POST1:V_POST1 + 8], op=ALU.mult),
             ["vec"], ["bg2"])

        st32 = [fbuf, xr[0], xr[1]]
        st32k = ["P32_0", "P32_1", "P32_2"]
        for u in range(NU):
            a = u % 3
            b = u % 4
            n = UNIT_N[u]
            src32 = st32[a][:].rearrange("p k t -> p (k t)")[:, 0:n]
            dst16 = actb[:].rearrange("p k t -> p (k t)")[:, b * UW:b * UW + n]
            R.dma("sync", (lambda u, src32, n: lambda e: e.dma_start(out=src32, in_=wsrc[u][:, 0:n]))(u, src32, n),
                  [], [st32k[a]], "p32_%d" % a)
            if u % 2 == 0:
                R.op("act", (lambda s_, d_: lambda e: e.copy(out=d_, in_=s_))(src32, dst16),
                     [st32k[a]], [("P16", b)])
            else:
                R.op("dve", (lambda s_, d_: lambda e: e.tensor_copy(out=d_, in_=s_))(src32, dst16),
                     [st32k[a]], [("P16", b)])
            R.dma("sync", (lambda u, d_, n: lambda e: e.dma_start(out=wsc[u][:, 0:n], in_=d_))(u, dst16, n),
                  [("P16", b)], [("wsc", u)], "p16_%d" % b)
        R.fence()

        state = {"acc": 0, "w": 0, "sq": 0, "S": 0, "pt": 0, "sg": 0}
        bgq = deque()
        tailq = deque()

        def drain_tail(n=None):
            k = 0
            while tailq and (n is None or k < n):
                tailq.popleft()()
                k += 1

        def drain(n=None):
            k = 0
            while bgq and (n is None or k < n):
                bgq.popleft()()
                k += 1

        def nextacc():
            b = ACC_BANKS[state["acc"] % len(ACC_BANKS)]
            state["acc"] += 1
            return b

        def wload(uid):
            n = UNIT_N[uid]
            slot = state["w"] % NW
            state["w"] += 1
            R.dma("sync", (lambda uid, slot, n: lambda e: e.dma_start(
                out=wring[:, slot, 0:n], in_=wsc[uid][:, 0:n]))(uid, slot, n),
                [("wsc", uid)], [("w", slot)], "w%d" % slot)
            return slot

        def mm(out, lhsT, rhs, start, stop, reads, writes):
            R.op("pe", lambda e: e.matmul(out, lhsT, rhs, start=start, stop=stop), reads, writes)

        def act(out, in_, func, reads, writes, bias=None, scale=None):
            kw = {}
            if bias is not None:
                kw["bias"] = bias
            if scale is not None:
                kw["scale"] = scale
            R.op("act", lambda e: e.activation(out=out, in_=in_, func=func, **kw), reads, writes)

        def tt(eng, out, in0, in1, op, reads, writes):
            R.op(eng, lambda e: e.tensor_tensor(out=out, in0=in0, in1=in1, op=op), reads, writes)

        def stt(eng, out, in0, scalar, in1, op0, op1, reads, writes):
            R.op(eng, lambda e: e.scalar_tensor_tensor(out=out, in0=in0, scalar=scalar, in1=in1,
                                                       op0=op0, op1=op1), reads, writes)

        def ts(eng, out, in0, s1, s2, op0, op1, reads, writes):
            if s2 is None:
                R.op(eng, lambda e: e.tensor_scalar(out=out, in0=in0, scalar1=s1, scalar2=None,
                                                    op0=op0), reads, writes)
            else:
                R.op(eng, lambda e: e.tensor_scalar(out=out, in0=in0, scalar1=s1, scalar2=s2,
                                                    op0=op0, op1=op1), reads, writes)

        def rstd_from(bank, out_tmp):
            o = tmp[:, out_tmp, :]
            act(o, ps[bank][:], AF.Ln, [("ps", bank)], [("tmp", out_tmp)], scale=1.0 / D, bias=EPS)
            act(o, o, AF.Exp, [("tmp", out_tmp)], [("tmp", out_tmp)], scale=-0.5)

        def stats_mm(bank, kc, src_ap, src_key, bias=None):
            i = state["sq"] % 2
            state["sq"] += 1
            act(sq[:, i, :], src_ap, AF.Square, [src_key], [("sq", i)], bias=bias)
            mm(ps[bank][:], ones[:], sq[:, i, :], kc == 0, kc == KC - 1,
               [("sq", i), "ones"], [("ps", bank)])

        def prenorm(src, srckey, gcol, rt, dst, dstkey):
            bank = 3
            for kc in range(KC):
                stats_mm(bank, kc, src[:, kc, :], srckey(kc))
            rstd_from(bank, rt)
            for kc in range(KC):
                stt("dve", dst[:, kc, :], src[:, kc, :], vcol(gcol + kc), tmp[:, rt, :],
                    ALU.mult, ALU.mult, [srckey(kc), ("tmp", rt), "vec"], [dstkey(kc)])

        hkey = lambda kc: ("h", kc)
        mkey = lambda kc: ("mix", kc)

        def proj_chunk(bank, slot, mw, sub, rhs_of, rhs_key, nk):
            for kc in range(nk):
                mm(ps[bank][:], wring[:, slot, kc * mw + sub * 128: kc * mw + sub * 128 + 128],
                   rhs_of(kc), kc == 0, kc == nk - 1, [("w", slot), rhs_key(kc)], [("ps", bank)])

        def postnorm_residual(xslot, gcol, produce, rt, bias_col=None, bgcol=None, ndrain=0, final=False):
            sbank = 3
            banks = {}
            pend = []

            def evac(m):
                bk = banks[m]
                if bias_col is None:
                    act(fbuf[:, m, :], ps[bk][:], AF.Identity, [("ps", bk), "vec"], [("fbuf", m)],
                        scale=vcol(gcol + m))
                    sqb = None
                else:
                    act(fbuf[:, m, :], ps[bk][:], AF.Identity, [("ps", bk), "vec", "bg2"],
                        [("fbuf", m)], scale=vcol(gcol + m), bias=bgcol[:, m:m + 1])
                    sqb = vcol(bias_col + m)
                i = state["sq"] % 2
                state["sq"] += 1
                act(sq[:, i, :], ps[bk][:], AF.Square, [("ps", bk), "vec"], [("sq", i)], bias=sqb)
                pend.append((m, i))

            def flush_one():
                m, i = pend.pop(0)
                mm(ps[sbank][:], ones[:], sq[:, i, :], m == 0, m == KC - 1,
                   [("sq", i), "ones"], [("ps", sbank)])

            for m in range(KC):
                banks[m] = produce(m)
                evac(m)
                if len(pend) > 1:
                    flush_one()
                if m < KC - 3:
                    drain(ndrain)
            while pend:
                flush_one()
            rstd_from(sbank, rt)
            for m in range(KC):
                tt("pool" if final else "dve", fbuf[:, m, :], fbuf[:, m, :], tmp[:, rt, :], ALU.mult,
                   [("fbuf", m), ("tmp", rt)], [("fbuf", m)])
                if m % 4 == 3 and not final:
                    tt("dve", xr[xslot][:, m, :], xr[xslot][:, m, :], fbuf[:, m, :], ALU.add,
                       [("fbuf", m), ("xr", xslot, m)], [("xr", xslot, m)])
                else:
                    tt("pool", xr[xslot][:, m, :], xr[xslot][:, m, :], fbuf[:, m, :], ALU.add,
                       [("fbuf", m), ("xr", xslot, m)], [("xr", xslot, m)])

        def ffn(layer, xslot, rt, ndrain=0, final=False):
            prenorm(xr[xslot], lambda kc: ("xr", xslot, kc), V_FPRE0 + 8 * layer, rt, hb, hkey)
            for j in range(11):
                sg_ = wload(U_G(layer, j))
                su_ = wload(U_U(layer, j))
                for sub in range(2):
                    ch = 2 * j + sub
                    bgt = nextacc()
                    bup = nextacc()
                    proj_chunk(bgt, sg_, 256, sub, lambda kc: hb[:, kc, :], hkey, KC)
                    proj_chunk(bup, su_, 256, sub, lambda kc: hb[:, kc, :], hkey, KC)
                    i = state["sg"] % 2
                    state["sg"] += 1
                    act(sgb[:, i, :], ps[bgt][:], AF.Silu, [("ps", bgt)], [("sg", i)])
                    tt("dve", actb[:, ch, :], sgb[:, i, :], ps[bup][:], ALU.mult,
                       [("sg", i), ("ps", bup)], [("act", ch)])
                    drain(ndrain)

            def produce(m):
                sl = wload(U_D(layer, m))
                bk = nextacc()
                proj_chunk(bk, sl, 128, 0, lambda kc: actb[:, kc, :], lambda kc: ("act", kc), FC)
                return bk
            postnorm_residual(xslot, V_FPOST0 + 8 * layer, produce, rt, ndrain=ndrain, final=final)

        def A_load(s):
            R.dma("act", lambda e: e.dma_start(out=fbuf[:], in_=xT[s]), [],
                  [("fbuf", m) for m in range(KC)], "xa")
            R.dma("act", lambda e: e.dma_start(out=cosb[:], in_=cosT[s]), [], ["cos"], "cos")
            R.dma("act", lambda e: e.dma_start(out=sinb[:], in_=sinT[s]), [], ["sin"], "sin")

        def A_prep(s):
            prenorm(fbuf, lambda kc: ("fbuf", kc), V_PRE0, 0, mix, mkey)

        def A_proj(s, mid_hook=None):
            qs = s % 2
            ks = s % 3
            hr = lambda kc: mix[:, kc, :]
            hk = mkey
            for c in range(5):
                sl = wload(U_IN(c))
                b0 = nextacc(); b1 = nextacc()
                proj_chunk(b0, sl, 256, 0, hr, hk, KC)
                proj_chunk(b1, sl, 256, 1, hr, hk, KC)
                ta = 2 * (c % 2); tb = ta + 1
                tt("dve", fbuf[:, ta, :], ps[b0][:], cosb[:], ALU.mult, [("ps", b0), "cos"], [("fbuf", ta)])
                tt("dve", fbuf[:, tb, :], ps[b1][:], sinb[:], ALU.mult, [("ps", b1), "sin"], [("fbuf", tb)])
                if c < 4:
                    tt("pool", qb_[:, qs, c, :], fbuf[:, ta, :], fbuf[:, tb, :], ALU.add,
                       [("fbuf", ta), ("fbuf", tb)], [("q", qs, c)])
                else:
                    tt("pool", kz[0][0:64, ks, :], fbuf[0:64, ta, :], fbuf[0:64, tb, :], ALU.add,
                       [("fbuf", ta), ("fbuf", tb)], [("kz", ks)])
                    tt("pool", kz[1][64:128, ks, :], fbuf[64:128, ta, :], fbuf[64:128, tb, :], ALU.add,
                       [("fbuf", ta), ("fbuf", tb)], [("kz", ks)])
                if mid_hook is not None:
                    mid_hook(c)
            sl = wload(U_IN(5))
            bv = nextacc()
            for blk in range(4):
                for kc in range(KC):
                    mm(ps[bv][:, blk * 128:(blk + 1) * 128], mix[:, kc, blk * 128:(blk + 1) * 128],
                       wring[:, sl, kc * 256: kc * 256 + 128], kc == 0, kc == KC - 1,
                       [("w", sl), ("mix", kc)], [("ps", bv)])
            pv3 = ps[bv][:].rearrange("p (b d) -> p b d", b=4)
            R.op("act", lambda e: e.copy(out=vz[0][:, ks * 4:ks * 4 + 4, 0:64], in_=pv3[:, :, 0:64]),
                 [("ps", bv)], [("vz", ks)])
            R.op("act", lambda e: e.copy(out=vz[1][:, ks * 4:ks * 4 + 4, 64:128], in_=pv3[:, :, 64:128]),
                 [("ps", bv)], [("vz", ks)])
            us = s % 3

            def u_chunk(sl, mw, sub, gi):
                bk = nextacc()
                proj_chunk(bk, sl, mw, sub, hr, hk, KC)
                R.op("act", lambda e: e.copy(out=ub[:, us, gi, 8:520], in_=ps[bk][:]),
                     [("ps", bk)], [("u", us, "c")])
            u_chunk(sl, 256, 1, 0)
            drain(4)
            sl = wload(U_IN(6))
            u_chunk(sl, 256, 0, 1)
            drain(4)
            u_chunk(sl, 256, 1, 2)
            drain(4)
            sl = wload(U_IN(7))
            u_chunk(sl, 128, 0, 3)
            drain(4)
            halo(ub, "u", s, 8, 8, 520)

        def halo(buf, name, s, hw, c0, c1, RS=3):
            sl = s % RS
            if s == 0:
                R.op("pool", lambda e: e.memset(buf[:, 0, :, c0 - hw:c0], 0.0), [], [(name, 0, "l")])
            if s == NT - 1:
                R.op("pool", lambda e: e.memset(buf[:, sl, :, c1:c1 + hw], 0.0), [], [(name, sl, "r")])
            if s > 0:
                dsl = (s - 1) % RS
                dst = buf[:, dsl, :, c1:c1 + hw]
                src = buf[:, sl, :, c0:c0 + hw]
                if s % 4 == 0:
                    ts("pool", dst, src, vcol(V_FLAG), None, ALU.mult, None,
                       [(name, sl, "c"), "vec"], [(name, dsl, "r")])
                else:
                    R.op("pool", lambda e: e.tensor_copy(out=dst, in_=src),
                         [(name, sl, "c")], [(name, dsl, "r")])
            if s < NT - 1:
                dsl = (s + 1) % RS
                dst2 = buf[:, dsl, :, c0 - hw:c0]
                src2 = buf[:, sl, :, c1 - hw:c1]
                if (s + 1) % 4 == 0:
                    ts("pool", dst2, src2, vcol(V_FLAG), None, ALU.mult, None,
                       [(name, sl, "c"), "vec"], [(name, dsl, "l")])
                else:
                    R.op("pool", lambda e: e.tensor_copy(out=dst2, in_=src2),
                         [(name, sl, "c")], [(name, dsl, "l")])

        CM = 496

        def enqueue_conv_main(t):
            gs = t % 2
            gk = [("g", gs, "c"), ("g", gs, "l")]
            for j in range(31):
                for i in range(KC):
                    def thunk(i=i, j=j):
                        src = gb[:, gs, i, j + 1: j + 1 + CM]
                        wc = vcol(V_WDW + i * 31 + j)
                        if j == 0:
                            ts("dve", ybuf[:, i, 0:CM], src, wc, vcol(V_BDW + i), ALU.mult, ALU.add,
                               gk + ["vec"], [("y", i)])
                        else:
                            stt("dve", ybuf[:, i, 0:CM], src, wc, ybuf[:, i, 0:CM], ALU.mult, ALU.add,
                                gk + ["vec", ("y", i)], [("y", i)])
                    bgq.append(thunk)

        def conv_tail(t):
            gs = t % 2
            gk = [("g", gs, "c"), ("g", gs, "r")]
            ykt = ["ytail"]
            yt = ybuf[:, :, CM:T]
            for j in range(31):
                def thunk(j=j):
                    src = gb[:, gs, :, CM + j + 1: CM + j + 1 + 16]
                    wbc = bcast_cols(V_WDW + j, 31, KC, 16)
                    tsl = j % 4
                    tk = ("ttail", tsl)
                    tt("dve", ttail[:, tsl], src, wbc, ALU.mult, gk + ["vec"], [tk])
                    if j == 0:
                        tt("dve", yt, ttail[:, tsl], bcast_cols(V_BDW, 1, KC, 16), ALU.add, [tk, "vec"],
                           ykt + [("y", i) for i in range(KC)])
                    else:
                        tt("dve", yt, yt, ttail[:, tsl], ALU.add, [tk] + ykt, ykt)
                tailq.append(thunk)

        def stageB(t):
            xs = t % 2
            qs = t % 2
            us = t % 3
            U = ub[:, us]
            fb = fbuf[:].rearrange("p k t -> p (k t)")
            P1 = fb[:, 0:4 * 528].rearrange("p (g c) -> p g c", g=4)
            P2 = fb[:, 4 * 528:7 * 528].rearrange("p (g c) -> p g c", g=3)
            fall = [("fbuf", m) for m in range(KC)]
            ukeys = [("u", us, "c"), ("u", us, "l"), ("u", us, "r")]
            res = [P1[:, 0, 7:519], P2[:, 0, 6:518], P1[:, 2, 4:516], P2[:, 2, 0:512]]

            def pool_p0():
                tt("dve", P1[:, 0:4, 0:527], U[:, :, 0:527], U[:, :, 1:528], ALU.add, ukeys, fall)
                tt("dve", P2[:, 0:3, 0:525], P1[:, 1:4, 0:525], P1[:, 1:4, 2:527], ALU.add, fall, fall)

            def pool_p1():
                tt("dve", P1[:, 2:4, 0:521], P2[:, 1:3, 0:521], P2[:, 1:3, 4:525], ALU.add, fall, fall)
                tt("dve", P2[:, 2:3, 0:513], P1[:, 3:4, 0:513], P1[:, 3:4, 8:521], ALU.add, fall, fall)
                if t % 4 == 0:
                    tb = 0 if t == 0 else 2
                    for gi in range(4):
                        cc = V_CORR + tb * 32 + gi * 8
                        tt("dve", res[gi][:, 0:8], res[gi][:, 0:8], vec[:, cc:cc + 8], ALU.mult,
                           fall + ["vec"], fall)
                if t % 4 == 3:
                    tb = 1 if t == NT - 1 else 3
                    for gi in range(4):
                        cc = V_CORR + tb * 32 + gi * 8
                        tt("dve", res[gi][:, 504:512], res[gi][:, 504:512], vec[:, cc:cc + 8], ALU.mult,
                           fall + ["vec"], fall)

            def pool_stt(gis):
                for gi in gis:
                    w = (2, 4, 8, 16)[gi]
                    stt("dve", hb[:, gi, :], res[gi], 1.0 / w, U[:, gi, 8:520], ALU.mult, ALU.subtract,
                        fall + ukeys, [("h", gi)])
            pool_pieces = [lambda: None, pool_p0, pool_p1, lambda: pool_stt((0, 1, 2, 3))]
            for qb in range(4):
                n = 4 * t + qb
                contribs = []
                for j in (n - 1, n, n + 1):
                    if 0 <= j < 64:
                        for g in range(2):
                            contribs.append((g, j))
                rhs_q = qb_[:, qs, :, qb * 128:(qb + 1) * 128]
                info = []
                bnum, bden = (6, 7)

                def emitS(i):
                    g, j = contribs[i]
                    sbk = 4 + state["S"] % 2
                    state["S"] += 1
                    ksl = (j // 4) % 3
                    ko = (j % 4) * 128
                    mm(ps[sbk][:].rearrange("p (c q) -> p c q", c=4), kz[g][:, ksl, ko:ko + 128], rhs_q,
                       True, True, [("kz", ksl)] + [("q", qs, c) for c in range(4)], [("ps", sbk)])
                    pi = state["pt"] % 3
                    state["pt"] += 1
                    act(ptb[:, pi, :], ps[sbk][:], AF.Exp, [("ps", sbk)], [("pt", pi)], scale=0.125)
                    if j != n:
                        if j < n:
                            mi = 2 if (n % 16 == 0) else 0
                        else:
                            mi = 3 if (n % 16 == 15) else 1
                        p3 = ptb[:, pi, :].rearrange("p (c q) -> p c q", c=4)
                        tt("dve", p3, p3, mask_bc(mi), ALU.mult, [("pt", pi), "maskb"], [("pt", pi)])
                    info.append(pi)

                def emitPV(i):
                    g, j = contribs[i]
                    pi = info[i]
                    vsl = ((j // 4) % 3) * 4 + (j % 4)
                    first = i == 0
                    last = i == len(contribs) - 1
                    mm(ps[bnum][:], vz[g][:, vsl, :], ptb[:, pi, :], first, last,
                       [("vz", (j // 4) % 3), ("pt", pi)], [("ps", bnum)])
                    mm(ps[bden][:], onesz[:, g, :], ptb[:, pi, :], first, last,
                       ["onesz", ("pt", pi)], [("ps", bden)])

                emitS(0)
                for i in range(len(contribs)):
                    if i + 1 < len(contribs):
                        emitS(i + 1)
                    emitPV(i)
                    if i == 2:
                        pool_pieces[qb]()
                tt("dve", tmp[:, 1, :], ps[bden][:], esink[:].rearrange("p c q -> p (c q)"), ALU.add,
                   [("ps", bden), "esink"], [("tmp", 1)])
                R.op("dve", lambda e: e.tensor_copy(out=tmp[:, 2, :], in_=ps[bnum][:]), [("ps", bnum)], [("tmp", 2)])
                act(tmp[:, 1, :], tmp[:, 1, :], AF.Ln, [("tmp", 1)], [("tmp", 1)])
                act(tmp[:, 1, :], tmp[:, 1, :], AF.Exp, [("tmp", 1)], [("tmp", 1)], scale=-1.0)
                tt("dve", mix[:, 0:4, qb * 128:(qb + 1) * 128],
                   tmp[:, 2, :].rearrange("p (c q) -> p c q", c=4),
                   tmp[:, 1, :].rearrange("p (c q) -> p c q", c=4), ALU.mult,
                   [("tmp", 2), ("tmp", 1)], [("mix", c) for c in range(4)])
            R.dma("act", lambda e: e.dma_start(out=xr[xs][:], in_=xT[t]), [],
                  [("xr", xs, m) for m in range(KC)], "xr%d" % xs)
            slp = wload(U_POOL)
            for gi, w in enumerate((2, 4, 8, 16)):
                bk = nextacc()
                mm(ps[bk][:], wring[:, slp, gi * 128:(gi + 1) * 128], hb[:, gi, :], True, True,
                   [("w", slp), ("h", gi)], [("ps", bk)])
                act(mix[:, 4 + gi, :], ps[bk][:], AF.Identity, [("ps", bk), "vec"], [("mix", 4 + gi)],
                    scale=vcol(V_PSCALE + gi))
            wslots = {}

            def produce_out(m):
                if m % 2 == 0:
                    wslots[0] = wload(U_OUT(m // 2))
                bk = nextacc()
                proj_chunk(bk, wslots[0], 256, m % 2, lambda kc: mix[:, kc, :], mkey, KC)
                return bk
            postnorm_residual(xs, V_POST0, produce_out, 0, ndrain=2)
            ffn(0, xs, 0, ndrain=4)
            prenorm(xr[xs], lambda kc: ("xr", xs, kc), V_PRE1, 0, hb, hkey)
            gs = t % 2
            for i in range(KC):
                if i == 1 and t + 2 < NT:
                    A_load(t + 2)
                if i == 4 and t + 2 < NT:
                    A_prep(t + 2)
                sl = wload(U_PW1(i))
                ba = nextacc(); bb = nextacc()
                proj_chunk(ba, sl, 256, 0, lambda kc: hb[:, kc, :], hkey, KC)
                proj_chunk(bb, sl, 256, 1, lambda kc: hb[:, kc, :], hkey, KC)
                si = state["sg"] % 2
                state["sg"] += 1
                act(sgb[:, si, :], ps[bb][:], AF.Sigmoid, [("ps", bb), "vec"], [("sg", si)],
                    bias=vcol(V_BPW1 + 8 + i))
                stt("dve", gb[:, gs, i, 16:528], ps[ba][:], vcol(V_BPW1 + i), sgb[:, si, :],
                    ALU.add, ALU.mult, [("ps", ba), ("sg", si), "vec"], [("g", gs, "c")])
                if i < 4:
                    drain(6)
            drain()
            halo(gb, "g", t, 16, 16, 528, RS=2)

        silu_q = deque()

        def run_silus():
            while silu_q:
                silu_q.popleft()()

        def C_ln(t):
            yk = lambda i: ("y", i)
            bsum = nextacc(); bsq = 3
            for i in range(KC):
                si = state["sg"] % 2
                state["sg"] += 1
                R.op("act", (lambda si, i: lambda e: e.copy(out=sgb[:, si, :], in_=ybuf[:, i, :]))(si, i),
                     [yk(i), "ytail"], [("sg", si)])
                mm(ps[bsum][:], ones[:], sgb[:, si, :], i == 0, i == KC - 1, [("sg", si), "ones"],
                   [("ps", bsum)])
                stats_mm(bsq, i, ybuf[:, i, :], yk(i))
            mu = tmp[:, 1, :]; msq = tmp[:, 2, :]; rs = tmp[:, 3, :]; nmr = tmp[:, 2, :]
            ts("dve", mu, ps[bsum][:], 1.0 / D, None, ALU.mult, None, [("ps", bsum)], [("tmp", 1)])
            tt("dve", msq, mu, mu, ALU.mult, [("tmp", 1)], [("tmp", 2)])
            stt("dve", rs, ps[bsq][:], 1.0 / D, msq, ALU.mult, ALU.subtract, [("ps", bsq), ("tmp", 2)],
                [("tmp", 3)])
            act(rs, rs, AF.Ln, [("tmp", 3)], [("tmp", 3)], bias=EPS)
            act(rs, rs, AF.Exp, [("tmp", 3)], [("tmp", 3)], scale=-0.5)
            stt("dve", nmr, mu, -1.0, rs, ALU.mult, ALU.mult, [("tmp", 1), ("tmp", 3)], [("tmp", 2)])
            for i in (0, 1, 5, 2, 3, 6, 4, 7):
                ne = "pool" if i >= 5 else "dve"
                tt(ne, ybuf[:, i, :], ybuf[:, i, :], rs, ALU.mult, [yk(i), ("tmp", 3)], [yk(i)])
                tt(ne, ybuf[:, i, :], ybuf[:, i, :], nmr, ALU.add, [yk(i), ("tmp", 2)], [yk(i)])
                silu_q.append((lambda i: lambda: act(hb[:, i, :], ybuf[:, i, :], AF.Silu, [("y", i), "vec"],
                                                     [("h", i)], scale=vcol(V_LNG + i),
                                                     bias=vcol(V_LNB + i)))(i))

        def C_main(t):
            xs = t % 2
            wslots = {}

            def produce_pw2(m):
                if m % 2 == 0:
                    wslots[0] = wload(U_PW2(m // 2))
                bk = nextacc()
                proj_chunk(bk, wslots[0], 256, m % 2, lambda kc: hb[:, kc, :], hkey, KC)
                return bk
            postnorm_residual(xs, V_POST1, produce_pw2, 0, bias_col=V_BPW2, bgcol=bg2, ndrain=3)
            ffn(1, xs, 0, ndrain=4, final=True)
            R.dma("pool", lambda e: e.dma_start(out=yT[t], in_=xr[xs][:]),
                  [("xr", xs, m) for m in range(KC)], [("yT", t)], "st%d" % xs)

        A_load(0); A_prep(0); A_proj(0)
        A_load(1); A_prep(1); A_proj(1)
        for s in range(1, NT + 2):
            tb_, tc_ = s - 1, s - 2
            if 0 <= tb_ < NT:
                stageB(tb_)
            has_c = 0 <= tc_ < NT
            if has_c:
                conv_tail(tc_)
                drain_tail(10)

            enq = {"done": False}

            def hook(c, tc_=tc_, tb_=tb_):
                if c < 3:
                    drain_tail(16)
                elif c == 3:
                    drain(); drain_tail()
                    C_ln(tc_)
                else:
                    drain(5)
            if s + 1 < NT:
                A_proj(s + 1, mid_hook=hook if has_c else None)
            elif has_c:
                drain(); drain_tail()
                C_ln(tc_)
            run_silus()
            if 0 <= tb_ < NT and not enq["done"]:
                enqueue_conv_main(tb_)
            if has_c:
                C_main(tc_)
        drain()

        R.finalize()
        lastst = {}
        for o in R.ops["pool"]:
            if o.is_dma and o.sem in ("st0", "st1"):
                lastst[o.sem] = o

        with nc.Block() as block:
            @block.sync
            def _(eng):
                R.emit("sync", eng, sems)
                for o in lastst.values():
                    eng.wait_ge(sems[o.sem], o.semval)

            @block.tensor
            def _(eng):
                R.emit("pe", eng, sems)

            @block.scalar
            def _(eng):
                R.emit("act", eng, sems)

            @block.vector
            def _(eng):
                R.emit("dve", eng, sems)

            @block.gpsimd
            def _(eng):
                R.emit("pool", eng, sems)
    return nc


def _unit_cols(W, cols, kcs):
    sub = W[:, cols]
    mw = sub.shape[1]
    return sub.reshape(kcs, 128, mw).transpose(1, 0, 2).reshape(128, kcs * mw)


def _build_wsrc(inp):
    wsrc = np.zeros((NU, 128, UW), np.float32)
    w_in = inp["w_in"][0]
    hd = 64

    def qcols(c, swap):
        d = np.arange(64)
        dd = (d + 32) % 64 if swap else d
        return np.concatenate([c * hd + dd, (4 + c) * hd + dd])

    def kcols(swap):
        d = np.arange(64)
        dd = (d + 32) % 64 if swap else d
        return np.concatenate([512 + dd, 512 + 64 + dd])
    chunks = []
    for c in range(4):
        chunks.append(qcols(c, False)); chunks.append(qcols(c, True))
    chunks.append(kcols(False)); chunks.append(kcols(True))
    chunks.append(np.arange(640, 768))
    for gi in range(4):
        chunks.append(768 + gi * 128 + np.arange(128))
    for u in range(8):
        cols = np.concatenate(chunks[2 * u: 2 * u + 2])
        a = _unit_cols(w_in, cols, 8)
        wsrc[U_IN(u), :, :a.shape[1]] = a
    wp = inp["w_pool"][0]
    wsrc[U_POOL, :, :512] = wp.transpose(1, 0, 2).reshape(128, 512)
    rowperm = []
    for c in range(4):
        rowperm.extend(list(c * 64 + np.arange(64)))
        rowperm.extend(list((4 + c) * 64 + np.arange(64)))
    rowperm.extend(list(512 + np.arange(512)))
    w_out = inp["w_out"][0][np.array(rowperm), :]
    for i in range(4):
        wsrc[U_OUT(i), :, :2048] = _unit_cols(w_out, np.arange(256 * i, 256 * i + 256), 8)
    for l in range(2):
        wg = inp["ffn_w_gate"][l]; wu = inp["ffn_w_up"][l]; wd = inp["ffn_w_down"][l]
        for j in range(11):
            cols = np.arange(256 * j, 256 * j + 256)
            wsrc[U_G(l, j), :, :2048] = _unit_cols(wg, cols, 8)
            wsrc[U_U(l, j), :, :2048] = _unit_cols(wu, cols, 8)
        for m in range(8):
            wsrc[U_D(l, m), :, :] = _unit_cols(wd, np.arange(128 * m, 128 * m + 128), 22)
    pw1 = inp["conv_w_pw1"][0]
    for i in range(8):
        cols = np.concatenate([np.arange(128 * i, 128 * i + 128), 1024 + np.arange(128 * i, 128 * i + 128)])
        wsrc[U_PW1(i), :, :2048] = _unit_cols(pw1, cols, 8)
    pw2 = inp["conv_w_pw2"][0]
    for i in range(4):
        wsrc[U_PW2(i), :, :2048] = _unit_cols(pw2, np.arange(256 * i, 256 * i + 256), 8)
    return wsrc


def _colvec(v):
    n = v.shape[0] // 128
    return v.reshape(n, 128).T


def _build_vecs(inp, is_prompt):
    vec = np.zeros((128, NV), np.float32)
    vec[:, V_PRE0:V_PRE0 + 8] = _colvec(inp["mix_pre_g"][0])
    vec[:, V_PRE1:V_PRE1 + 8] = _colvec(inp["mix_pre_g"][1])
    vec[:, V_POST0:V_POST0 + 8] = _colvec(inp["mix_post_g"][0])
    vec[:, V_POST1:V_POST1 + 8] = _colvec(inp["mix_post_g"][1])
    vec[:, V_FPRE0:V_FPRE0 + 8] = _colvec(inp["ffn_pre_g"][0])
    vec[:, V_FPRE1:V_FPRE1 + 8] = _colvec(inp["ffn_pre_g"][1])
    vec[:, V_FPOST0:V_FPOST0 + 8] = _colvec(inp["ffn_post_g"][0])
    vec[:, V_FPOST1:V_FPOST1 + 8] = _colvec(inp["ffn_post_g"][1])
    vec[:, V_PSCALE:V_PSCALE + 4] = _colvec(inp["pool_scale"][0])
    vec[:, V_BPW1:V_BPW1 + 16] = _colvec(inp["conv_b_pw1"][0])
    vec[:, V_BDW:V_BDW + 8] = _colvec(inp["conv_b_dw"][0])
    vec[:, V_LNG:V_LNG + 8] = _colvec(inp["conv_ln_g"][0])
    vec[:, V_LNB:V_LNB + 8] = _colvec(inp["conv_ln_b"][0])
    vec[:, V_BPW2:V_BPW2 + 8] = _colvec(inp["conv_b_pw2"][0])
    sink = inp["attn_sink"][0]
    for c in range(4):
        vec[0:64, V_SINK + c] = sink[c]
        vec[64:128, V_SINK + c] = sink[4 + c]
    vec[:, V_FLAG] = 1.0 if is_prompt else 0.0
    wdw = inp["conv_w_dw"][0]
    for i in range(8):
        vec[:, V_WDW + i * 31: V_WDW + (i + 1) * 31] = wdw[:, i * 128:(i + 1) * 128].T
    ledge = np.ones((4, 8), np.float32); redge = np.ones((4, 8), np.float32)
    for gi, w in enumerate((2, 4, 8, 16)):
        half = w // 2
        for i in range(8):
            if i < half:
                ledge[gi, i] = np.float32(w) / np.float32(i + half)
            r = 7 - i
            if r < half - 1:
                redge[gi, i] = np.float32(w) / np.float32(r + 1 + half)
    lint = np.ones((4, 8), np.float32) if is_prompt else ledge
    rint = np.ones((4, 8), np.float32) if is_prompt else redge
    for tb, tab in enumerate((ledge, redge, lint, rint)):
        vec[:, V_CORR + tb * 32: V_CORR + (tb + 1) * 32] = tab.reshape(1, 32)
    return vec


def _build_rope(pos):
    half = 32
    inv_freq = (np.float32(10000.0) ** (-np.arange(0, half, dtype=np.float32) * np.float32(2.0) / np.float32(64))).astype(np.float32)
    ang = pos.astype(np.float32)[:, None] * inv_freq[None, :]
    cos = np.cos(ang).astype(np.float32); sin = np.sin(ang).astype(np.float32)
    p = np.arange(128)
    f = p % 32
    sign = np.where((p % 64) < 32, -1.0, 1.0).astype(np.float32)
    cosT = cos[:, f].T
    sinT = (sin[:, f] * sign[None, :]).T
    S = pos.shape[0]
    cosT = cosT.reshape(128, NT, T).transpose(1, 0, 2)
    sinT = sinT.reshape(128, NT, T).transpose(1, 0, 2)
    return np.ascontiguousarray(cosT), np.ascontiguousarray(sinT)


def _to_featmajor(xc):
    return np.ascontiguousarray(xc.reshape(NT, T, KC, 128).transpose(0, 3, 2, 1))


def _from_featmajor(y):
    return np.ascontiguousarray(y.transpose(0, 3, 2, 1).reshape(NT * T, D))


_NC_CACHE = {}


def kernel(**inputs):
    inp = {k: np.asarray(v) for k, v in inputs.items()}
    xp = inp["x_prompt"].astype(np.float32, copy=False)
    xs = inp["x_sample"].astype(np.float32, copy=False)
    wsrc = _build_wsrc(inp)
    kl = np.arange(128)[:, None]; ql = np.arange(128)[None, :]
    mP = (kl >= ql).astype(np.float32); mN = (kl <= ql).astype(np.float32)
    zero = np.zeros_like(mP)
    in_maps = []
    for c in range(8):
        is_prompt = c < 4
        if is_prompt:
            xc = xp[c]
            pos = np.arange(8192)
        else:
            xc = xs[4 * (c - 4): 4 * (c - 4) + 4].reshape(8192, D)
            pos = np.arange(8192) % 2048
        cosT, sinT = _build_rope(pos)
        masks = np.stack([mP, mN, mP if is_prompt else zero, mN if is_prompt else zero]).astype(ml_dtypes.bfloat16)
        in_maps.append({
            "xT": _to_featmajor(xc),
            "cosT": cosT, "sinT": sinT,
            "wsrc": wsrc,
            "vecs": _build_vecs(inp, is_prompt),
            "masks": masks,
        })
    if "nc" not in _NC_CACHE:
        _NC_CACHE["nc"] = build_program()
    nc = _NC_CACHE["nc"]
    res = run_bass_kernel_spmd(nc, in_maps, core_ids=list(range(8)))
    outs = [np.asarray(r["yT"]) for r in res.results]
    y_prompt = np.stack([_from_featmajor(outs[c]) for c in range(4)]).astype(np.float32)
    ys = [_from_featmajor(outs[c]).reshape(4, 2048, D) for c in range(4, 8)]
    y_sample = np.concatenate(ys, axis=0).astype(np.float32)
    return (y_prompt, y_sample)
```

```python
import numpy as np
import ml_dtypes
import concourse.bass as bass
import concourse.mybir as mybir
from concourse.bass_utils import run_bass_kernel_spmd

F32 = mybir.dt.float32
BF16 = mybir.dt.bfloat16
AF = mybir.ActivationFunctionType
ALU = mybir.AluOpType

D = 1024
T = 512
NT = 16
KC = 8
DFF = 2816
FC = 22
import os
NW = 5
ACC_BANKS = tuple(int(c) for c in '01267')
UW = 2816
NU = 85
EPS = 1e-6
NTILES = NT

V_PRE0, V_PRE1, V_POST0, V_POST1 = 0, 8, 16, 24
V_FPRE0, V_FPRE1, V_FPOST0, V_FPOST1 = 32, 40, 48, 56
V_PSCALE = 64
V_BPW1 = 68
V_BDW = 84
V_LNG = 92
V_LNB = 100
V_BPW2 = 108
V_SINK = 116
V_FLAG = 120
V_WDW = 121
V_CORR = 369
NV = V_CORR + 128


def U_IN(i): return i
U_POOL = 8
def U_OUT(i): return 9 + i
def U_G(l, j): return 13 + l * 30 + 2 * j
def U_U(l, j): return 13 + l * 30 + 2 * j + 1
def U_D(l, m): return 13 + l * 30 + 22 + m
def U_PW1(i): return 73 + i
def U_PW2(i): return 81 + i


UNIT_N = [2048] * NU
UNIT_N[7] = 1024
UNIT_N[8] = 512
for _l in range(2):
    for _m in range(8):
        UNIT_N[13 + _l * 30 + 22 + _m] = 2816


class _Op:
    __slots__ = ("eng", "fn", "deps", "idx", "needs_inc", "sem", "semval", "is_dma")


class Rec:
    ENGS = ("sync", "pe", "act", "dve", "pool")

    def __init__(self):
        self.ops = {e: [] for e in self.ENGS}
        self.lastw = {}
        self.readers = {}
        self.dma_count = {}
        self.fence_deps = {e: [] for e in self.ENGS}

    def _add(self, eng, fn, reads, writes, sem=None):
        o = _Op()
        o.eng = eng; o.fn = fn; o.idx = None; o.needs_inc = False
        o.sem = sem; o.is_dma = sem is not None; o.semval = None
        deps = []
        if self.fence_deps[eng]:
            deps.extend(self.fence_deps[eng]); self.fence_deps[eng] = []
        for k in reads:
            w = self.lastw.get(k)
            if w is not None:
                deps.append(w)
        for k in writes:
            w = self.lastw.get(k)
            if w is not None:
                deps.append(w)
            rd = self.readers.get(k)
            if rd:
                deps.extend(rd.values())
        fl = []
        seen = set()
        for d in deps:
            if id(d) in seen:
                continue
            seen.add(id(d))
            if (not d.is_dma) and (not o.is_dma) and d.eng == eng and eng == "pe":
                continue
            fl.append(d)
        o.deps = fl
        for d in fl:
            d.needs_inc = True
        if o.is_dma:
            c = self.dma_count.get(sem, 0) + 1
            self.dma_count[sem] = c
            o.semval = 16 * c
        rk = ("dma", sem) if o.is_dma else eng
        for k in reads:
            self.readers.setdefault(k, {})[rk] = o
        for k in writes:
            self.lastw[k] = o
            self.readers[k] = {}
        self.ops[eng].append(o)
        return o

    def op(self, eng, fn, reads=(), writes=()):
        return self._add(eng, fn, reads, writes)

    def dma(self, eng, fn, reads, writes, sem):
        return self._add(eng, fn, reads, writes, sem=sem)

    def fence(self):
        deps = []
        for e in self.ENGS:
            for o in reversed(self.ops[e]):
                if not o.is_dma:
                    deps.append(o); break
        lastdma = {}
        for e in self.ENGS:
            for o in self.ops[e]:
                if o.is_dma:
                    lastdma[o.sem] = o
        deps.extend(lastdma.values())
        for e in self.ENGS:
            self.fence_deps[e] = list(deps)

    def finalize(self):
        for e in self.ENGS:
            c = 0
            for o in self.ops[e]:
                if o.is_dma:
                    continue
                if o.needs_inc:
                    c += 1
                    o.idx = c

    def emit(self, eng_name, eng, sems):
        seen = {}
        for o in self.ops[eng_name]:
            for d in o.deps:
                if d.is_dma:
                    key = ("dma", d.sem); val = d.semval; sh = sems[d.sem]
                else:
                    key = d.eng; val = d.idx; sh = sems["eng_" + d.eng]
                if seen.get(key, 0) >= val:
                    continue
                eng.wait_ge(sh, val)
                seen[key] = val
            ins = o.fn(eng)
            if o.is_dma:
                ins.then_inc(sems[o.sem], 16)
            elif o.needs_inc:
                ins.then_inc(sems["eng_" + eng_name], 1)


def build_program():
    nc = bass.Bass("TRN2", target_bir_lowering=False)
    R = Rec()

    xT = nc.dram_tensor("xT", [NT, 128, KC, T], F32, kind="ExternalInput").ap()
    yT = nc.dram_tensor("yT", [NT, 128, KC, T], F32, kind="ExternalOutput").ap()
    cosT = nc.dram_tensor("cosT", [NT, 128, T], F32, kind="ExternalInput").ap()
    sinT = nc.dram_tensor("sinT", [NT, 128, T], F32, kind="ExternalInput").ap()
    wsrc = nc.dram_tensor("wsrc", [NU, 128, UW], F32, kind="ExternalInput").ap()
    vecs_d = nc.dram_tensor("vecs", [128, NV], F32, kind="ExternalInput").ap()
    masks_d = nc.dram_tensor("masks", [4, 128, 128], BF16, kind="ExternalInput").ap()
    wsc = nc.dram_tensor("wsc", [NU, 128, UW], BF16, kind="Internal").ap()

    import contextlib
    from collections import deque
    es = contextlib.ExitStack()

    def sb(name, shape, dt):
        return es.enter_context(nc.sbuf_tensor(name, shape, dt))

    with es:
        fbuf = sb("fbuf", [128, KC, T], F32)
        ybuf = sb("ybuf", [128, KC, T], F32)
        xr = [sb("xr0", [128, KC, T], F32), sb("xr1", [128, KC, T], F32)]
        hb = sb("hb", [128, KC, T], BF16)
        sq = sb("sq", [128, 2, T], BF16)
        kz = [sb("kz0", [128, 3, T], BF16), sb("kz1", [128, 3, T], BF16)]
        vz = [sb("vz0", [128, 12, 128], BF16), sb("vz1", [128, 12, 128], BF16)]
        qb_ = sb("qb", [128, 2, 4, T], BF16)
        ub = sb("ub", [128, 3, 4, 528], BF16)
        mix = sb("mix", [128, KC, T], BF16)
        ptb = sb("ptb", [128, 3, T], BF16)
        actb = sb("actb", [128, FC, T], BF16)
        sgb = sb("sgb", [128, 2, T], BF16)
        gb = sb("gb", [128, 2, KC, 544], BF16)
        wring = sb("wring", [128, NW, UW], BF16)
        cosb = sb("cosb", [128, T], F32)
        sinb = sb("sinb", [128, T], F32)
        tmp = sb("tmp", [128, 4, T], F32)
        vec = sb("vec", [128, NV], F32)
        maskb = sb("maskb", [128, 4, 128], BF16)
        ones = sb("ones", [128, 128], BF16)
        onesz = sb("onesz", [128, 2, 128], BF16)
        esink = sb("esink", [128, 4, 128], F32)
        es4 = sb("es4", [128, 4], F32)
        bg2 = sb("bg2", [128, 8], F32)
        ttail = sb("ttail", [128, 4, KC, 16], F32)
        ps = [es.enter_context(nc.psum_tensor("ps%d" % i, [128, T], F32)) for i in range(8)]

        sem_names = (["eng_" + e for e in Rec.ENGS] + ["w%d" % i for i in range(NW)] +
                     ["xa", "cos", "sin", "xr0", "xr1", "st0", "st1", "consts", "consts2",
                      "p32_0", "p32_1", "p32_2", "p16_0", "p16_1", "p16_2", "p16_3"])
        sems = {n: es.enter_context(nc.semaphore(n)) for n in sem_names}

        def vcol(c):
            return vec[:, c:c + 1]

        def bcast_cols(col0, cstride, n_mid, n_last):
            base = vec[:, col0:col0 + 1]
            return bass.AP(base.tensor, base.offset, [list(base.ap[0]), [cstride, n_mid], [0, n_last]])

        def mask_bc(mi):
            base = maskb[:, mi, :]
            return bass.AP(base.tensor, base.offset, [list(base.ap[0]), [0, 4], [1, 128]])

        R.dma("sync", lambda e: e.dma_start(out=vec[:], in_=vecs_d), [], ["vec"], "consts")
        R.dma("sync", lambda e: e.dma_start(out=maskb[:], in_=masks_d.rearrange("m p q -> p m q")),
              [], ["maskb"], "consts2")
        R.op("pool", lambda e: e.memset(ones[:], 1.0), [], ["ones"])
        R.op("pool", lambda e: e.memset(onesz[:], 0.0), [], ["onesz"])
        R.op("pool", lambda e: e.memset(onesz[:, 0, 0:64], 1.0), [], ["onesz"])
        R.op("pool", lambda e: e.memset(onesz[:, 1, 64:128], 1.0), [], ["onesz"])
        R.op("pool", lambda e: e.memset(kz[0][:], 0.0), [], [("kz", 0), ("kz", 1), ("kz", 2)])
        R.op("pool", lambda e: e.memset(kz[1][:], 0.0), [], [("kz", 0), ("kz", 1), ("kz", 2)])
        R.op("pool", lambda e: e.memset(vz[0][:], 0.0), [], [("vz", 0), ("vz", 1), ("vz", 2)])
        R.op("pool", lambda e: e.memset(vz[1][:], 0.0), [], [("vz", 0), ("vz", 1), ("vz", 2)])
        R.op("pool", lambda e: e.memset(esink[:], 0.0), [], ["esink"])
        R.op("act", lambda e: e.activation(out=es4[:], in_=vec[:, V_SINK:V_SINK + 4], func=AF.Exp),
             ["vec"], ["es4"])
        for c in range(4):
            R.op("dve", (lambda c: lambda e: e.tensor_scalar(
                out=esink[:, c, :], in0=esink[:, c, :], scalar1=es4[:, c:c + 1], scalar2=None,
                op0=ALU.add))(c), ["es4", "esink"], ["esink"])
        R.op("dve", lambda e: e.tensor_tensor(out=bg2[:], in0=vec[:, V_BPW2:V_BPW2 + 8],
                                               in1=vec[:, V_POST1:V_POST1 + 8], op=ALU.mult),
             ["vec"], ["bg2"])

        corder = [u for u in (list(range(0, 43)) + list(range(73, 85)) + list(range(43, 73)))
                  if UNIT_N[u] > 0]
        cpos = {u: i for i, u in enumerate(corder)}
        conv = {"n": 0, "loaded": 0, "done": False}
        st32 = [ybuf, xr[1]]
        st32k = ["P32_0", "P32_1"]
        gflat = gb[:].rearrange("p s k c -> p (s k c)")
        last_cast = {}
        last_store = {}

        def conv_load(i):
            u = corder[i]; a_ = i % 2; n = UNIT_N[u]
            src32 = st32[a_][:].rearrange("p k t -> p (k t)")[:, 0:n]
            R.dma("sync", lambda e: e.dma_start(out=src32, in_=wsrc[u][:, 0:n]), [], [st32k[a_]],
                  "p32_%d" % a_)

        def conv_step(i):
            while conv["loaded"] < min(i + 2, len(corder)):
                conv_load(conv["loaded"]); conv["loaded"] += 1
            u = corder[i]; a_ = i % 2; b_ = i % 3; n = UNIT_N[u]
            src32 = st32[a_][:].rearrange("p k t -> p (k t)")[:, 0:n]
            dst16 = gflat[:, b_ * UW:b_ * UW + n]
            if i % 2 == 0:
                o = R.op("act", lambda e: e.copy(out=dst16, in_=src32), [st32k[a_]], [("P16", b_)])
            else:
                o = R.op("dve", lambda e: e.tensor_copy(out=dst16, in_=src32), [st32k[a_]], [("P16", b_)])
            last_cast[(a_, o.eng)] = o
            o2 = R.dma("sync", lambda e: e.dma_start(out=wsc[u][:, 0:n], in_=dst16),
                       [("P16", b_)], [("wsc", u)], "p16_%d" % b_)
            last_store[b_] = o2

        def conv_ensure(target):
            target = min(target, len(corder))
            while conv["n"] < target:
                conv_step(conv["n"]); conv["n"] += 1

        def conv_finish():
            if conv["done"]:
                return
            conv_ensure(len(corder))
            conv["done"] = True
            for (a_, en), o in last_cast.items():
                keys = [("y", i) for i in range(KC)] + ["ytail"] if a_ == 0 else [("xr", 1, m) for m in range(KC)]
                for k in keys:
                    R.readers.setdefault(k, {})[("cv", a_, en)] = o
                    o.needs_inc = True
            for b_, o in last_store.items():
                for gs_ in range(2):
                    for part in ("c", "l", "r"):
                        R.readers.setdefault(("g", gs_, part), {})[("cvs", b_)] = o

        state = {"acc": 0, "w": 0, "sq": 0, "S": 0, "pt": 0, "sg": 0}
        bgq = deque()
        tailq = deque()

        def drain_tail(n=None):
            k = 0
            while tailq and (n is None or k < n):
                tailq.popleft()()
                k += 1

        def drain(n=None):
            k = 0
            while bgq and (n is None or k < n):
                bgq.popleft()()
                k += 1

        def nextacc():
            b = ACC_BANKS[state["acc"] % len(ACC_BANKS)]
            state["acc"] += 1
            return b

        def wload(uid):
            if not conv["done"]:
                conv_ensure(max(cpos[uid] + 1, conv["n"] + 2))
                if conv["n"] >= len(corder):
                    conv_finish()
            n = UNIT_N[uid]
            slot = state["w"] % NW
            state["w"] += 1
            R.dma("sync", (lambda uid, slot, n: lambda e: e.dma_start(
                out=wring[:, slot, 0:n], in_=wsc[uid][:, 0:n]))(uid, slot, n),
                [("wsc", uid)], [("w", slot)], "w%d" % slot)
            return slot

        def mm(out, lhsT, rhs, start, stop, reads, writes):
            R.op("pe", lambda e: e.matmul(out, lhsT, rhs, start=start, stop=stop), reads, writes)

        def act(out, in_, func, reads, writes, bias=None, scale=None):
            kw = {}
            if bias is not None:
                kw["bias"] = bias
            if scale is not None:
                kw["scale"] = scale
            R.op("act", lambda e: e.activation(out=out, in_=in_, func=func, **kw), reads, writes)

        def tt(eng, out, in0, in1, op, reads, writes):
            R.op(eng, lambda e: e.tensor_tensor(out=out, in0=in0, in1=in1, op=op), reads, writes)

        def stt(eng, out, in0, scalar, in1, op0, op1, reads, writes):
            R.op(eng, lambda e: e.scalar_tensor_tensor(out=out, in0=in0, scalar=scalar, in1=in1,
                                                       op0=op0, op1=op1), reads, writes)

        def ts(eng, out, in0, s1, s2, op0, op1, reads, writes):
            if s2 is None:
                R.op(eng, lambda e: e.tensor_scalar(out=out, in0=in0, scalar1=s1, scalar2=None,
                                                    op0=op0), reads, writes)
            else:
                R.op(eng, lambda e: e.tensor_scalar(out=out, in0=in0, scalar1=s1, scalar2=s2,
                                                    op0=op0, op1=op1), reads, writes)

        def rstd_from(bank, out_tmp):
            o = tmp[:, out_tmp, :]
            act(o, ps[bank][:], AF.Ln, [("ps", bank)], [("tmp", out_tmp)], scale=1.0 / D, bias=EPS)
            act(o, o, AF.Exp, [("tmp", out_tmp)], [("tmp", out_tmp)], scale=-0.5)

        def stats_mm(bank, kc, src_ap, src_key, bias=None):
            i = state["sq"] % 2
            state["sq"] += 1
            act(sq[:, i, :], src_ap, AF.Square, [src_key], [("sq", i)], bias=bias)
            mm(ps[bank][:], ones[:], sq[:, i, :], kc == 0, kc == KC - 1,
               [("sq", i), "ones"], [("ps", bank)])

        def prenorm(src, srckey, gcol, rt, dst, dstkey):
            bank = 3
            for kc in range(KC):
                stats_mm(bank, kc, src[:, kc, :], srckey(kc))
            rstd_from(bank, rt)
            for kc in range(KC):
                stt("dve", dst[:, kc, :], src[:, kc, :], vcol(gcol + kc), tmp[:, rt, :],
                    ALU.mult, ALU.mult, [srckey(kc), ("tmp", rt), "vec"], [dstkey(kc)])

        hkey = lambda kc: ("h", kc)
        mkey = lambda kc: ("mix", kc)

        def proj_chunk(bank, slot, mw, sub, rhs_of, rhs_key, nk):
            for kc in range(nk):
                mm(ps[bank][:], wring[:, slot, kc * mw + sub * 128: kc * mw + sub * 128 + 128],
                   rhs_of(kc), kc == 0, kc == nk - 1, [("w", slot), rhs_key(kc)], [("ps", bank)])

        def postnorm_residual(xslot, gcol, produce, rt, bias_col=None, bgcol=None, ndrain=0, final=False):
            sbank = 3
            banks = {}
            pend = []

            def evac(m):
                bk = banks[m]
                if bias_col is None:
                    act(fbuf[:, m, :], ps[bk][:], AF.Identity, [("ps", bk), "vec"], [("fbuf", m)],
                        scale=vcol(gcol + m))
                    sqb = None
                else:
                    act(fbuf[:, m, :], ps[bk][:], AF.Identity, [("ps", bk), "vec", "bg2"],
                        [("fbuf", m)], scale=vcol(gcol + m), bias=bgcol[:, m:m + 1])
                    sqb = vcol(bias_col + m)
                i = state["sq"] % 2
                state["sq"] += 1
                act(sq[:, i, :], ps[bk][:], AF.Square, [("ps", bk), "vec"], [("sq", i)], bias=sqb)
                pend.append((m, i))

            def flush_one():
                m, i = pend.pop(0)
                mm(ps[sbank][:], ones[:], sq[:, i, :], m == 0, m == KC - 1,
                   [("sq", i), "ones"], [("ps", sbank)])

            for m in range(KC):
                banks[m] = produce(m)
                evac(m)
                if len(pend) > 1:
                    flush_one()
                drain(ndrain)
            while pend:
                flush_one()
            rstd_from(sbank, rt)
            for m in range(KC):
                tt("pool" if final else "dve", fbuf[:, m, :], fbuf[:, m, :], tmp[:, rt, :], ALU.mult,
                   [("fbuf", m), ("tmp", rt)], [("fbuf", m)])
                if m % 4 == 3 and not final:
                    tt("dve", xr[xslot][:, m, :], xr[xslot][:, m, :], fbuf[:, m, :], ALU.add,
                       [("fbuf", m), ("xr", xslot, m)], [("xr", xslot, m)])
                else:
                    tt("pool", xr[xslot][:, m, :], xr[xslot][:, m, :], fbuf[:, m, :], ALU.add,
                       [("fbuf", m), ("xr", xslot, m)], [("xr", xslot, m)])

        def ffn(layer, xslot, rt, ndrain=0, final=False):
            prenorm(xr[xslot], lambda kc: ("xr", xslot, kc), V_FPRE0 + 8 * layer, rt, hb, hkey)
            for j in range(11):
                sg_ = wload(U_G(layer, j))
                su_ = wload(U_U(layer, j))
                for sub in range(2):
                    ch = 2 * j + sub
                    bgt = nextacc()
                    bup = nextacc()
                    proj_chunk(bgt, sg_, 256, sub, lambda kc: hb[:, kc, :], hkey, KC)
                    proj_chunk(bup, su_, 256, sub, lambda kc: hb[:, kc, :], hkey, KC)
                    i = state["sg"] % 2
                    state["sg"] += 1
                    act(sgb[:, i, :], ps[bgt][:], AF.Silu, [("ps", bgt)], [("sg", i)])
                    tt("dve", actb[:, ch, :], sgb[:, i, :], ps[bup][:], ALU.mult,
                       [("sg", i), ("ps", bup)], [("act", ch)])
                    drain(ndrain)

            def produce(m):
                sl = wload(U_D(layer, m))
                bk = nextacc()
                proj_chunk(bk, sl, 128, 0, lambda kc: actb[:, kc, :], lambda kc: ("act", kc), FC)
                return bk
            postnorm_residual(xslot, V_FPOST0 + 8 * layer, produce, rt, ndrain=ndrain, final=final)

        def A_load(s):
            R.dma("act", lambda e: e.dma_start(out=fbuf[:], in_=xT[s]), [],
                  [("fbuf", m) for m in range(KC)], "xa")
            R.dma("act", lambda e: e.dma_start(out=cosb[:], in_=cosT[s]), [], ["cos"], "cos")
            R.dma("act", lambda e: e.dma_start(out=sinb[:], in_=sinT[s]), [], ["sin"], "sin")

        def A_prep(s):
            prenorm(fbuf, lambda kc: ("fbuf", kc), V_PRE0, 0, mix, mkey)

        def A_proj(s, mid_hook=None):
            qs = s % 2
            ks = s % 3
            hr = lambda kc: mix[:, kc, :]
            hk = mkey
            for c in range(5):
                sl = wload(U_IN(c))
                b0 = nextacc(); b1 = nextacc()
                proj_chunk(b0, sl, 256, 0, hr, hk, KC)
                proj_chunk(b1, sl, 256, 1, hr, hk, KC)
                ta = 2 * (c % 2); tb = ta + 1
                tt("dve", fbuf[:, ta, :], ps[b0][:], cosb[:], ALU.mult, [("ps", b0), "cos"], [("fbuf", ta)])
                tt("dve", fbuf[:, tb, :], ps[b1][:], sinb[:], ALU.mult, [("ps", b1), "sin"], [("fbuf", tb)])
                if c < 4:
                    tt("pool", qb_[:, qs, c, :], fbuf[:, ta, :], fbuf[:, tb, :], ALU.add,
                       [("fbuf", ta), ("fbuf", tb)], [("q", qs, c)])
                else:
                    tt("pool", kz[0][0:64, ks, :], fbuf[0:64, ta, :], fbuf[0:64, tb, :], ALU.add,
                       [("fbuf", ta), ("fbuf", tb)], [("kz", ks)])
                    tt("pool", kz[1][64:128, ks, :], fbuf[64:128, ta, :], fbuf[64:128, tb, :], ALU.add,
                       [("fbuf", ta), ("fbuf", tb)], [("kz", ks)])
                if mid_hook is not None:
                    mid_hook(c)
            sl = wload(U_IN(5))
            bv = nextacc()
            for blk in range(4):
                for kc in range(KC):
                    mm(ps[bv][:, blk * 128:(blk + 1) * 128], mix[:, kc, blk * 128:(blk + 1) * 128],
                       wring[:, sl, kc * 256: kc * 256 + 128], kc == 0, kc == KC - 1,
                       [("w", sl), ("mix", kc)], [("ps", bv)])
            pv3 = ps[bv][:].rearrange("p (b d) -> p b d", b=4)
            R.op("act", lambda e: e.copy(out=vz[0][:, ks * 4:ks * 4 + 4, 0:64], in_=pv3[:, :, 0:64]),
                 [("ps", bv)], [("vz", ks)])
            R.op("act", lambda e: e.copy(out=vz[1][:, ks * 4:ks * 4 + 4, 64:128], in_=pv3[:, :, 64:128]),
                 [("ps", bv)], [("vz", ks)])
            us = s % 3

            def u_chunk(sl, mw, sub, gi):
                bk = nextacc()
                proj_chunk(bk, sl, mw, sub, hr, hk, KC)
                R.op("act", lambda e: e.copy(out=ub[:, us, gi, 8:520], in_=ps[bk][:]),
                     [("ps", bk)], [("u", us, "c")])
            u_chunk(sl, 256, 1, 0)
            drain(4)
            sl = wload(U_IN(6))
            u_chunk(sl, 256, 0, 1)
            drain(4)
            u_chunk(sl, 256, 1, 2)
            drain(4)
            sl = wload(U_IN(7))
            u_chunk(sl, 128, 0, 3)
            drain(4)
            halo(ub, "u", s, 8, 8, 520)

        def halo(buf, name, s, hw, c0, c1, RS=3):
            sl = s % RS
            if s == 0:
                R.op("pool", lambda e: e.memset(buf[:, 0, :, c0 - hw:c0], 0.0), [], [(name, 0, "l")])
            if s == NT - 1:
                R.op("pool", lambda e: e.memset(buf[:, sl, :, c1:c1 + hw], 0.0), [], [(name, sl, "r")])
            if s > 0:
                dsl = (s - 1) % RS
                dst = buf[:, dsl, :, c1:c1 + hw]
                src = buf[:, sl, :, c0:c0 + hw]
                if s % 4 == 0:
                    ts("pool", dst, src, vcol(V_FLAG), None, ALU.mult, None,
                       [(name, sl, "c"), "vec"], [(name, dsl, "r")])
                else:
                    R.op("pool", lambda e: e.tensor_copy(out=dst, in_=src),
                         [(name, sl, "c")], [(name, dsl, "r")])
            if s < NT - 1:
                dsl = (s + 1) % RS
                dst2 = buf[:, dsl, :, c0 - hw:c0]
                src2 = buf[:, sl, :, c1 - hw:c1]
                if (s + 1) % 4 == 0:
                    ts("pool", dst2, src2, vcol(V_FLAG), None, ALU.mult, None,
                       [(name, sl, "c"), "vec"], [(name, dsl, "l")])
                else:
                    R.op("pool", lambda e: e.tensor_copy(out=dst2, in_=src2),
                         [(name, sl, "c")], [(name, dsl, "l")])

        CM = 496

        def enqueue_conv_main(t):
            gs = t % 2
            gk = [("g", gs, "c"), ("g", gs, "l")]
            for j in range(31):
                for i in range(KC):
                    def thunk(i=i, j=j):
                        src = gb[:, gs, i, j + 1: j + 1 + CM]
                        wc = vcol(V_WDW + i * 31 + j)
                        if j == 0:
                            ts("dve", ybuf[:, i, 0:CM], src, wc, vcol(V_BDW + i), ALU.mult, ALU.add,
                               gk + ["vec"], [("y", i)])
                        else:
                            stt("dve", ybuf[:, i, 0:CM], src, wc, ybuf[:, i, 0:CM], ALU.mult, ALU.add,
                                gk + ["vec", ("y", i)], [("y", i)])
                    bgq.append(thunk)

        def conv_tail(t):
            gs = t % 2
            gk = [("g", gs, "c"), ("g", gs, "r")]
            ykt = ["ytail"]
            yt = ybuf[:, :, CM:T]
            for j in range(31):
                def thunk(j=j):
                    src = gb[:, gs, :, CM + j + 1: CM + j + 1 + 16]
                    wbc = bcast_cols(V_WDW + j, 31, KC, 16)
                    tsl = j % 4
                    tk = ("ttail", tsl)
                    tt("dve", ttail[:, tsl], src, wbc, ALU.mult, gk + ["vec"], [tk])
                    if j == 0:
                        tt("dve", yt, ttail[:, tsl], bcast_cols(V_BDW, 1, KC, 16), ALU.add, [tk, "vec"],
                           ykt + [("y", i) for i in range(KC)])
                    else:
                        tt("dve", yt, yt, ttail[:, tsl], ALU.add, [tk] + ykt, ykt)
                tailq.append(thunk)

        def stageB(t):
            xs = t % 2
            qs = t % 2
            us = t % 3
            U = ub[:, us]
            fb = fbuf[:].rearrange("p k t -> p (k t)")
            P1 = fb[:, 0:4 * 528].rearrange("p (g c) -> p g c", g=4)
            P2 = fb[:, 4 * 528:7 * 528].rearrange("p (g c) -> p g c", g=3)
            fall = [("fbuf", m) for m in range(KC)]
            ukeys = [("u", us, "c"), ("u", us, "l"), ("u", us, "r")]
            res = [P1[:, 0, 7:519], P2[:, 0, 6:518], P1[:, 2, 4:516], P2[:, 2, 0:512]]

            def pool_p0():
                tt("dve", P1[:, 0:4, 0:527], U[:, :, 0:527], U[:, :, 1:528], ALU.add, ukeys, fall)
                tt("dve", P2[:, 0:3, 0:525], P1[:, 1:4, 0:525], P1[:, 1:4, 2:527], ALU.add, fall, fall)

            def pool_p1():
                tt("dve", P1[:, 2:4, 0:521], P2[:, 1:3, 0:521], P2[:, 1:3, 4:525], ALU.add, fall, fall)
                tt("dve", P2[:, 2:3, 0:513], P1[:, 3:4, 0:513], P1[:, 3:4, 8:521], ALU.add, fall, fall)
                if t % 4 == 0:
                    tb = 0 if t == 0 else 2
                    for gi in range(4):
                        cc = V_CORR + tb * 32 + gi * 8
                        tt("dve", res[gi][:, 0:8], res[gi][:, 0:8], vec[:, cc:cc + 8], ALU.mult,
                           fall + ["vec"], fall)
                if t % 4 == 3:
                    tb = 1 if t == NT - 1 else 3
                    for gi in range(4):
                        cc = V_CORR + tb * 32 + gi * 8
                        tt("dve", res[gi][:, 504:512], res[gi][:, 504:512], vec[:, cc:cc + 8], ALU.mult,
                           fall + ["vec"], fall)

            def pool_stt(gis):
                for gi in gis:
                    w = (2, 4, 8, 16)[gi]
                    stt("dve", hb[:, gi, :], res[gi], 1.0 / w, U[:, gi, 8:520], ALU.mult, ALU.subtract,
                        fall + ukeys, [("h", gi)])
            pool_pieces = [lambda: None, pool_p0, pool_p1, lambda: pool_stt((0, 1, 2, 3))]
            for qb in range(4):
                n = 4 * t + qb
                contribs = []
                for j in (n - 1, n, n + 1):
                    if 0 <= j < 64:
                        for g in range(2):
                            contribs.append((g, j))
                rhs_q = qb_[:, qs, :, qb * 128:(qb + 1) * 128]
                info = []
                bnum, bden = (6, 7)

                def emitS(i):
                    g, j = contribs[i]
                    sbk = 4 + state["S"] % 2
                    state["S"] += 1
                    ksl = (j // 4) % 3
                    ko = (j % 4) * 128
                    mm(ps[sbk][:].rearrange("p (c q) -> p c q", c=4), kz[g][:, ksl, ko:ko + 128], rhs_q,
                       True, True, [("kz", ksl)] + [("q", qs, c) for c in range(4)], [("ps", sbk)])
                    pi = state["pt"] % 3
                    state["pt"] += 1
                    act(ptb[:, pi, :], ps[sbk][:], AF.Exp, [("ps", sbk)], [("pt", pi)], scale=0.125)
                    if j != n:
                        if j < n:
                            mi = 2 if (n % 16 == 0) else 0
                        else:
                            mi = 3 if (n % 16 == 15) else 1
                        p3 = ptb[:, pi, :].rearrange("p (c q) -> p c q", c=4)
                        tt("dve", p3, p3, mask_bc(mi), ALU.mult, [("pt", pi), "maskb"], [("pt", pi)])
                    info.append(pi)

                def emitPV(i):
                    g, j = contribs[i]
                    pi = info[i]
                    vsl = ((j // 4) % 3) * 4 + (j % 4)
                    first = i == 0
                    last = i == len(contribs) - 1
                    mm(ps[bnum][:], vz[g][:, vsl, :], ptb[:, pi, :], first, last,
                       [("vz", (j // 4) % 3), ("pt", pi)], [("ps", bnum)])
                    mm(ps[bden][:], onesz[:, g, :], ptb[:, pi, :], first, last,
                       ["onesz", ("pt", pi)], [("ps", bden)])

                emitS(0)
                for i in range(len(contribs)):
                    if i + 1 < len(contribs):
                        emitS(i + 1)
                    emitPV(i)
                    if i == 2:
                        pool_pieces[qb]()
                tt("dve", tmp[:, 1, :], ps[bden][:], esink[:].rearrange("p c q -> p (c q)"), ALU.add,
                   [("ps", bden), "esink"], [("tmp", 1)])
                R.op("dve", lambda e: e.tensor_copy(out=tmp[:, 2, :], in_=ps[bnum][:]), [("ps", bnum)], [("tmp", 2)])
                act(tmp[:, 1, :], tmp[:, 1, :], AF.Ln, [("tmp", 1)], [("tmp", 1)])
                act(tmp[:, 1, :], tmp[:, 1, :], AF.Exp, [("tmp", 1)], [("tmp", 1)], scale=-1.0)
                tt("dve", mix[:, 0:4, qb * 128:(qb + 1) * 128],
                   tmp[:, 2, :].rearrange("p (c q) -> p c q", c=4),
                   tmp[:, 1, :].rearrange("p (c q) -> p c q", c=4), ALU.mult,
                   [("tmp", 2), ("tmp", 1)], [("mix", c) for c in range(4)])
            R.dma("act", lambda e: e.dma_start(out=xr[xs][:], in_=xT[t]), [],
                  [("xr", xs, m) for m in range(KC)], "xr%d" % xs)
            slp = wload(U_POOL)
            for gi, w in enumerate((2, 4, 8, 16)):
                bk = nextacc()
                mm(ps[bk][:], wring[:, slp, gi * 128:(gi + 1) * 128], hb[:, gi, :], True, True,
                   [("w", slp), ("h", gi)], [("ps", bk)])
                act(mix[:, 4 + gi, :], ps[bk][:], AF.Identity, [("ps", bk), "vec"], [("mix", 4 + gi)],
                    scale=vcol(V_PSCALE + gi))
            wslots = {}

            def produce_out(m):
                if m % 2 == 0:
                    wslots[0] = wload(U_OUT(m // 2))
                bk = nextacc()
                proj_chunk(bk, wslots[0], 256, m % 2, lambda kc: mix[:, kc, :], mkey, KC)
                return bk
            postnorm_residual(xs, V_POST0, produce_out, 0, ndrain=2)
            ffn(0, xs, 0, ndrain=3)
            conv_finish()
            prenorm(xr[xs], lambda kc: ("xr", xs, kc), V_PRE1, 0, hb, hkey)
            gs = t % 2
            for i in range(KC):
                if i == 1 and t + 2 < NT:
                    A_load(t + 2)
                if i == 4 and t + 2 < NT:
                    A_prep(t + 2)
                sl = wload(U_PW1(i))
                ba = nextacc(); bb = nextacc()
                proj_chunk(ba, sl, 256, 0, lambda kc: hb[:, kc, :], hkey, KC)
                proj_chunk(bb, sl, 256, 1, lambda kc: hb[:, kc, :], hkey, KC)
                si = state["sg"] % 2
                state["sg"] += 1
                act(sgb[:, si, :], ps[bb][:], AF.Sigmoid, [("ps", bb), "vec"], [("sg", si)],
                    bias=vcol(V_BPW1 + 8 + i))
                stt("dve", gb[:, gs, i, 16:528], ps[ba][:], vcol(V_BPW1 + i), sgb[:, si, :],
                    ALU.add, ALU.mult, [("ps", ba), ("sg", si), "vec"], [("g", gs, "c")])
                if i < 4:
                    drain(6)
            drain()
            halo(gb, "g", t, 16, 16, 528, RS=2)

        def C_ln(t):
            yk = lambda i: ("y", i)
            bsum = nextacc(); bsq = 3
            for i in range(KC):
                si = state["sg"] % 2
                state["sg"] += 1
                R.op("act", (lambda si, i: lambda e: e.copy(out=sgb[:, si, :], in_=ybuf[:, i, :]))(si, i),
                     [yk(i), "ytail"], [("sg", si)])
                mm(ps[bsum][:], ones[:], sgb[:, si, :], i == 0, i == KC - 1, [("sg", si), "ones"],
                   [("ps", bsum)])
                stats_mm(bsq, i, ybuf[:, i, :], yk(i))
            mu = tmp[:, 1, :]; msq = tmp[:, 2, :]; rs = tmp[:, 3, :]; nmr = tmp[:, 2, :]
            ts("dve", mu, ps[bsum][:], 1.0 / D, None, ALU.mult, None, [("ps", bsum)], [("tmp", 1)])
            tt("dve", msq, mu, mu, ALU.mult, [("tmp", 1)], [("tmp", 2)])
            stt("dve", rs, ps[bsq][:], 1.0 / D, msq, ALU.mult, ALU.subtract, [("ps", bsq), ("tmp", 2)],
                [("tmp", 3)])
            act(rs, rs, AF.Ln, [("tmp", 3)], [("tmp", 3)], bias=EPS)
            act(rs, rs, AF.Exp, [("tmp", 3)], [("tmp", 3)], scale=-0.5)
            stt("dve", nmr, mu, -1.0, rs, ALU.mult, ALU.mult, [("tmp", 1), ("tmp", 3)], [("tmp", 2)])
            for i in (0, 1, 5, 2, 3, 6, 4, 7):
                ne = "pool" if i >= 5 else "dve"
                tt(ne, ybuf[:, i, :], ybuf[:, i, :], rs, ALU.mult, [yk(i), ("tmp", 3)], [yk(i)])
                tt(ne, ybuf[:, i, :], ybuf[:, i, :], nmr, ALU.add, [yk(i), ("tmp", 2)], [yk(i)])
                act(hb[:, i, :], ybuf[:, i, :], AF.Silu, [yk(i), "vec"], [("h", i)],
                    scale=vcol(V_LNG + i), bias=vcol(V_LNB + i))

        def C_main(t):
            xs = t % 2
            wslots = {}

            def produce_pw2(m):
                if m % 2 == 0:
                    wslots[0] = wload(U_PW2(m // 2))
                bk = nextacc()
                proj_chunk(bk, wslots[0], 256, m % 2, lambda kc: hb[:, kc, :], hkey, KC)
                return bk
            postnorm_residual(xs, V_POST1, produce_pw2, 0, bias_col=V_BPW2, bgcol=bg2, ndrain=3)
            ffn(1, xs, 0, ndrain=3, final=True)
            R.dma("pool", lambda e: e.dma_start(out=yT[t], in_=xr[xs][:]),
                  [("xr", xs, m) for m in range(KC)], [("yT", t)], "st%d" % xs)

        A_load(0); A_prep(0); A_proj(0)
        A_load(1); A_prep(1); A_proj(1)
        for s in range(1, NT + 2):
            tb_, tc_ = s - 1, s - 2
            if 0 <= tb_ < NT:
                stageB(tb_)
            has_c = 0 <= tc_ < NT
            if has_c:
                conv_tail(tc_)
                drain_tail(10)

            enq = {"done": False}

            def hook(c, tc_=tc_, tb_=tb_):
                if c < 3:
                    drain_tail(16)
                elif c == 3:
                    drain(); drain_tail()
                    C_ln(tc_)
                    if 0 <= tb_ < NT:
                        enqueue_conv_main(tb_)
                        enq["done"] = True
                else:
                    drain(5)
            if s + 1 < NT:
                A_proj(s + 1, mid_hook=hook if has_c else None)
            elif has_c:
                drain(); drain_tail()
                C_ln(tc_)
            if 0 <= tb_ < NT and not enq["done"]:
                enqueue_conv_main(tb_)
            if has_c:
                C_main(tc_)
        drain()

        R.finalize()
        lastst = {}
        for o in R.ops["pool"]:
            if o.is_dma and o.sem in ("st0", "st1"):
                lastst[o.sem] = o

        with nc.Block() as block:
            @block.sync
            def _(eng):
                R.emit("sync", eng, sems)
                for o in lastst.values():
                    eng.wait_ge(sems[o.sem], o.semval)

            @block.tensor
            def _(eng):
                R.emit("pe", eng, sems)

            @block.scalar
            def _(eng):
                R.emit("act", eng, sems)

            @block.vector
            def _(eng):
                R.emit("dve", eng, sems)

            @block.gpsimd
            def _(eng):
                R.emit("pool", eng, sems)
    return nc


def _unit_cols(W, cols, kcs):
    sub = W[:, cols]
    mw = sub.shape[1]
    return sub.reshape(kcs, 128, mw).transpose(1, 0, 2).reshape(128, kcs * mw)


def _build_wsrc(inp):
    wsrc = np.zeros((NU, 128, UW), np.float32)
    w_in = inp["w_in"][0]
    hd = 64

    def qcols(c, swap):
        d = np.arange(64)
        dd = (d + 32) % 64 if swap else d
        return np.concatenate([c * hd + dd, (4 + c) * hd + dd])

    def kcols(swap):
        d = np.arange(64)
        dd = (d + 32) % 64 if swap else d
        return np.concatenate([512 + dd, 512 + 64 + dd])
    chunks = []
    for c in range(4):
        chunks.append(qcols(c, False)); chunks.append(qcols(c, True))
    chunks.append(kcols(False)); chunks.append(kcols(True))
    chunks.append(np.arange(640, 768))
    for gi in range(4):
        chunks.append(768 + gi * 128 + np.arange(128))
    for u in range(8):
        cols = np.concatenate(chunks[2 * u: 2 * u + 2])
        a = _unit_cols(w_in, cols, 8)
        wsrc[U_IN(u), :, :a.shape[1]] = a
    wp = inp["w_pool"][0]
    wsrc[U_POOL, :, :512] = wp.transpose(1, 0, 2).reshape(128, 512)
    rowperm = []
    for c in range(4):
        rowperm.extend(list(c * 64 + np.arange(64)))
        rowperm.extend(list((4 + c) * 64 + np.arange(64)))
    rowperm.extend(list(512 + np.arange(512)))
    w_out = inp["w_out"][0][np.array(rowperm), :]
    for i in range(4):
        wsrc[U_OUT(i), :, :2048] = _unit_cols(w_out, np.arange(256 * i, 256 * i + 256), 8)
    for l in range(2):
        wg = inp["ffn_w_gate"][l]; wu = inp["ffn_w_up"][l]; wd = inp["ffn_w_down"][l]
        for j in range(11):
            cols = np.arange(256 * j, 256 * j + 256)
            wsrc[U_G(l, j), :, :2048] = _unit_cols(wg, cols, 8)
            wsrc[U_U(l, j), :, :2048] = _unit_cols(wu, cols, 8)
        for m in range(8):
            wsrc[U_D(l, m), :, :] = _unit_cols(wd, np.arange(128 * m, 128 * m + 128), 22)
    pw1 = inp["conv_w_pw1"][0]
    for i in range(8):
        cols = np.concatenate([np.arange(128 * i, 128 * i + 128), 1024 + np.arange(128 * i, 128 * i + 128)])
        wsrc[U_PW1(i), :, :2048] = _unit_cols(pw1, cols, 8)
    pw2 = inp["conv_w_pw2"][0]
    for i in range(4):
        wsrc[U_PW2(i), :, :2048] = _unit_cols(pw2, np.arange(256 * i, 256 * i + 256), 8)
    return wsrc


def _colvec(v):
    n = v.shape[0] // 128
    return v.reshape(n, 128).T


def _build_vecs(inp, is_prompt):
    vec = np.zeros((128, NV), np.float32)
    vec[:, V_PRE0:V_PRE0 + 8] = _colvec(inp["mix_pre_g"][0])
    vec[:, V_PRE1:V_PRE1 + 8] = _colvec(inp["mix_pre_g"][1])
    vec[:, V_POST0:V_POST0 + 8] = _colvec(inp["mix_post_g"][0])
    vec[:, V_POST1:V_POST1 + 8] = _colvec(inp["mix_post_g"][1])
    vec[:, V_FPRE0:V_FPRE0 + 8] = _colvec(inp["ffn_pre_g"][0])
    vec[:, V_FPRE1:V_FPRE1 + 8] = _colvec(inp["ffn_pre_g"][1])
    vec[:, V_FPOST0:V_FPOST0 + 8] = _colvec(inp["ffn_post_g"][0])
    vec[:, V_FPOST1:V_FPOST1 + 8] = _colvec(inp["ffn_post_g"][1])
    vec[:, V_PSCALE:V_PSCALE + 4] = _colvec(inp["pool_scale"][0])
    vec[:, V_BPW1:V_BPW1 + 16] = _colvec(inp["conv_b_pw1"][0])
    vec[:, V_BDW:V_BDW + 8] = _colvec(inp["conv_b_dw"][0])
    vec[:, V_LNG:V_LNG + 8] = _colvec(inp["conv_ln_g"][0])
    vec[:, V_LNB:V_LNB + 8] = _colvec(inp["conv_ln_b"][0])
    vec[:, V_BPW2:V_BPW2 + 8] = _colvec(inp["conv_b_pw2"][0])
    sink = inp["attn_sink"][0]
    for c in range(4):
        vec[0:64, V_SINK + c] = sink[c]
        vec[64:128, V_SINK + c] = sink[4 + c]
    vec[:, V_FLAG] = 1.0 if is_prompt else 0.0
    wdw = inp["conv_w_dw"][0]
    for i in range(8):
        vec[:, V_WDW + i * 31: V_WDW + (i + 1) * 31] = wdw[:, i * 128:(i + 1) * 128].T
    ledge = np.ones((4, 8), np.float32); redge = np.ones((4, 8), np.float32)
    for gi, w in enumerate((2, 4, 8, 16)):
        half = w // 2
        for i in range(8):
            if i < half:
                ledge[gi, i] = np.float32(w) / np.float32(i + half)
            r = 7 - i
            if r < half - 1:
                redge[gi, i] = np.float32(w) / np.float32(r + 1 + half)
    lint = np.ones((4, 8), np.float32) if is_prompt else ledge
    rint = np.ones((4, 8), np.float32) if is_prompt else redge
    for tb, tab in enumerate((ledge, redge, lint, rint)):
        vec[:, V_CORR + tb * 32: V_CORR + (tb + 1) * 32] = tab.reshape(1, 32)
    return vec


def _build_rope(pos):
    half = 32
    inv_freq = (np.float32(10000.0) ** (-np.arange(0, half, dtype=np.float32) * np.float32(2.0) / np.float32(64))).astype(np.float32)
    ang = pos.astype(np.float32)[:, None] * inv_freq[None, :]
    cos = np.cos(ang).astype(np.float32); sin = np.sin(ang).astype(np.float32)
    p = np.arange(128)
    f = p % 32
    sign = np.where((p % 64) < 32, -1.0, 1.0).astype(np.float32)
    cosT = cos[:, f].T
    sinT = (sin[:, f] * sign[None, :]).T
    S = pos.shape[0]
    cosT = cosT.reshape(128, NT, T).transpose(1, 0, 2)
    sinT = sinT.reshape(128, NT, T).transpose(1, 0, 2)
    return np.ascontiguousarray(cosT), np.ascontiguousarray(sinT)


def _to_featmajor(xc):
    return np.ascontiguousarray(xc.reshape(NT, T, KC, 128).transpose(0, 3, 2, 1))


def _from_featmajor(y):
    return np.ascontiguousarray(y.transpose(0, 3, 2, 1).reshape(NT * T, D))


_NC_CACHE = {}


def kernel(**inputs):
    inp = {k: np.asarray(v) for k, v in inputs.items()}
    xp = inp["x_prompt"].astype(np.float32, copy=False)
    xs = inp["x_sample"].astype(np.float32, copy=False)
    wsrc = _build_wsrc(inp)
    kl = np.arange(128)[:, None]; ql = np.arange(128)[None, :]
    mP = (kl >= ql).astype(np.float32); mN = (kl <= ql).astype(np.float32)
    zero = np.zeros_like(mP)
    in_maps = []
    for c in range(8):
        is_prompt = c < 4
        if is_prompt:
            xc = xp[c]
            pos = np.arange(8192)
        else:
            xc = xs[4 * (c - 4): 4 * (c - 4) + 4].reshape(8192, D)
            pos = np.arange(8192) % 2048
        cosT, sinT = _build_rope(pos)
        masks = np.stack([mP, mN, mP if is_prompt else zero, mN if is_prompt else zero]).astype(ml_dtypes.bfloat16)
        in_maps.append({
            "xT": _to_featmajor(xc),
            "cosT": cosT, "sinT": sinT,
            "wsrc": wsrc,
            "vecs": _build_vecs(inp, is_prompt),
            "masks": masks,
        })
    if "nc" not in _NC_CACHE:
        _NC_CACHE["nc"] = build_program()
    nc = _NC_CACHE["nc"]
    res = run_bass_kernel_spmd(nc, in_maps, core_ids=list(range(8)))
    outs = [np.asarray(r["yT"]) for r in res.results]
    y_prompt = np.stack([_from_featmajor(outs[c]) for c in range(4)]).astype(np.float32)
    ys = [_from_featmajor(outs[c]).reshape(4, 2048, D) for c in range(4, 8)]
    y_sample = np.concatenate(ys, axis=0).astype(np.float32)
    return (y_prompt, y_sample)
```

```python
import numpy as np
import ml_dtypes
import concourse.bass as bass
import concourse.mybir as mybir
from concourse.bass_utils import run_bass_kernel_spmd

F32 = mybir.dt.float32
BF16 = mybir.dt.bfloat16
AF = mybir.ActivationFunctionType
ALU = mybir.AluOpType

D = 1024
T = 512
NT = 16
KC = 8
DFF = 2816
FC = 22
import os
NW = 5
ACC_BANKS = tuple(int(c) for c in '01267')
UW = 2816
NU = 85
EPS = 1e-6
NTILES = NT

V_PRE0, V_PRE1, V_POST0, V_POST1 = 0, 8, 16, 24
V_FPRE0, V_FPRE1, V_FPOST0, V_FPOST1 = 32, 40, 48, 56
V_PSCALE = 64
V_BPW1 = 68
V_BDW = 84
V_LNG = 92
V_LNB = 100
V_BPW2 = 108
V_SINK = 116
V_FLAG = 120
V_WDW = 121
V_CORR = 369
NV = V_CORR + 128


def U_IN(i): return i
U_POOL = 8
def U_OUT(i): return 9 + i
def U_G(l, j): return 13 + l * 30 + 2 * j
def U_U(l, j): return 13 + l * 30 + 2 * j + 1
def U_D(l, m): return 13 + l * 30 + 22 + m
def U_PW1(i): return 73 + i
def U_PW2(i): return 81 + i


UNIT_N = [2048] * NU
UNIT_N[7] = 1024
UNIT_N[8] = 512
for _l in range(2):
    for _m in range(8):
        UNIT_N[13 + _l * 30 + 22 + _m] = 2816


class _Op:
    __slots__ = ("eng", "fn", "deps", "idx", "needs_inc", "sem", "semval", "is_dma")


class Rec:
    ENGS = ("sync", "pe", "act", "dve", "pool")

    def __init__(self):
        self.ops = {e: [] for e in self.ENGS}
        self.lastw = {}
        self.readers = {}
        self.dma_count = {}
        self.fence_deps = {e: [] for e in self.ENGS}

    def _add(self, eng, fn, reads, writes, sem=None):
        o = _Op()
        o.eng = eng; o.fn = fn; o.idx = None; o.needs_inc = False
        o.sem = sem; o.is_dma = sem is not None; o.semval = None
        deps = []
        if self.fence_deps[eng]:
            deps.extend(self.fence_deps[eng]); self.fence_deps[eng] = []
        for k in reads:
            w = self.lastw.get(k)
            if w is not None:
                deps.append(w)
        for k in writes:
            w = self.lastw.get(k)
            if w is not None:
                deps.append(w)
            rd = self.readers.get(k)
            if rd:
                deps.extend(rd.values())
        fl = []
        seen = set()
        for d in deps:
            if id(d) in seen:
                continue
            seen.add(id(d))
            if (not d.is_dma) and (not o.is_dma) and d.eng == eng and eng == "pe":
                continue
            fl.append(d)
        o.deps = fl
        for d in fl:
            d.needs_inc = True
        if o.is_dma:
            c = self.dma_count.get(sem, 0) + 1
            self.dma_count[sem] = c
            o.semval = 16 * c
        rk = ("dma", sem) if o.is_dma else eng
        for k in reads:
            self.readers.setdefault(k, {})[rk] = o
        for k in writes:
            self.lastw[k] = o
            self.readers[k] = {}
        self.ops[eng].append(o)
        return o

    def op(self, eng, fn, reads=(), writes=()):
        return self._add(eng, fn, reads, writes)

    def dma(self, eng, fn, reads, writes, sem):
        return self._add(eng, fn, reads, writes, sem=sem)

    def fence(self):
        deps = []
        for e in self.ENGS:
            for o in reversed(self.ops[e]):
                if not o.is_dma:
                    deps.append(o); break
        lastdma = {}
        for e in self.ENGS:
            for o in self.ops[e]:
                if o.is_dma:
                    lastdma[o.sem] = o
        deps.extend(lastdma.values())
        for e in self.ENGS:
            self.fence_deps[e] = list(deps)

    def finalize(self):
        for e in self.ENGS:
            c = 0
            for o in self.ops[e]:
                if o.is_dma:
                    continue
                if o.needs_inc:
                    c += 1
                    o.idx = c

    def emit(self, eng_name, eng, sems):
        seen = {}
        for o in self.ops[eng_name]:
            for d in o.deps:
                if d.is_dma:
                    key = ("dma", d.sem); val = d.semval; sh = sems[d.sem]
                else:
                    key = d.eng; val = d.idx; sh = sems["eng_" + d.eng]
                if seen.get(key, 0) >= val:
                    continue
                eng.wait_ge(sh, val)
                seen[key] = val
            ins = o.fn(eng)
            if o.is_dma:
                ins.then_inc(sems[o.sem], 16)
            elif o.needs_inc:
                ins.then_inc(sems["eng_" + eng_name], 1)


def build_program():
    nc = bass.Bass("TRN2", target_bir_lowering=False)
    R = Rec()

    xT = nc.dram_tensor("xT", [NT, 128, KC, T], F32, kind="ExternalInput").ap()
    yT = nc.dram_tensor("yT", [NT, 128, KC, T], F32, kind="ExternalOutput").ap()
    cosT = nc.dram_tensor("cosT", [NT, 128, T], F32, kind="ExternalInput").ap()
    sinT = nc.dram_tensor("sinT", [NT, 128, T], F32, kind="ExternalInput").ap()
    wsrc = nc.dram_tensor("wsrc", [NU, 128, UW], F32, kind="ExternalInput").ap()
    vecs_d = nc.dram_tensor("vecs", [128, NV], F32, kind="ExternalInput").ap()
    masks_d = nc.dram_tensor("masks", [4, 128, 128], BF16, kind="ExternalInput").ap()
    wsc = nc.dram_tensor("wsc", [NU, 128, UW], BF16, kind="Internal").ap()

    import contextlib
    from collections import deque
    es = contextlib.ExitStack()

    def sb(name, shape, dt):
        return es.enter_context(nc.sbuf_tensor(name, shape, dt))

    with es:
        fbuf = sb("fbuf", [128, KC, T], F32)
        ybuf = sb("ybuf", [128, KC, T], F32)
        xr = [sb("xr0", [128, KC, T], F32), sb("xr1", [128, KC, T], F32)]
        hb = sb("hb", [128, KC, T], BF16)
        sq = sb("sq", [128, 2, T], BF16)
        kz = [sb("kz0", [128, 3, T], BF16), sb("kz1", [128, 3, T], BF16)]
        vz = [sb("vz0", [128, 12, 128], BF16), sb("vz1", [128, 12, 128], BF16)]
        qb_ = sb("qb", [128, 2, 4, T], BF16)
        ub = sb("ub", [128, 3, 4, 528], BF16)
        mix = sb("mix", [128, KC, T], BF16)
        ptb = sb("ptb", [128, 3, T], BF16)
        actb = sb("actb", [128, FC, T], BF16)
        sgb = sb("sgb", [128, 2, T], BF16)
        gb = sb("gb", [128, 2, KC, 544], BF16)
        wring = sb("wring", [128, NW, UW], BF16)
        cosb = sb("cosb", [128, T], F32)
        sinb = sb("sinb", [128, T], F32)
        tmp = sb("tmp", [128, 4, T], F32)
        vec = sb("vec", [128, NV], F32)
        maskb = sb("maskb", [128, 4, 128], BF16)
        ones = sb("ones", [128, 128], BF16)
        onesz = sb("onesz", [128, 2, 128], BF16)
        esink = sb("esink", [128, 4, 128], F32)
        es4 = sb("es4", [128, 4], F32)
        bg2 = sb("bg2", [128, 8], F32)
        ttail = sb("ttail", [128, 4, KC, 16], F32)
        ps = [es.enter_context(nc.psum_tensor("ps%d" % i, [128, T], F32)) for i in range(8)]

        sem_names = (["eng_" + e for e in Rec.ENGS] + ["w%d" % i for i in range(NW)] +
                     ["xa", "cos", "sin", "xr0", "xr1", "st0", "st1", "consts", "consts2",
                      "p32_0", "p32_1", "p32_2", "p16_0", "p16_1", "p16_2", "p16_3"])
        sems = {n: es.enter_context(nc.semaphore(n)) for n in sem_names}

        def vcol(c):
            return vec[:, c:c + 1]

        def bcast_cols(col0, cstride, n_mid, n_last):
            base = vec[:, col0:col0 + 1]
            return bass.AP(base.tensor, base.offset, [list(base.ap[0]), [cstride, n_mid], [0, n_last]])

        def mask_bc(mi):
            base = maskb[:, mi, :]
            return bass.AP(base.tensor, base.offset, [list(base.ap[0]), [0, 4], [1, 128]])

        R.dma("sync", lambda e: e.dma_start(out=vec[:], in_=vecs_d), [], ["vec"], "consts")
        R.dma("sync", lambda e: e.dma_start(out=maskb[:], in_=masks_d.rearrange("m p q -> p m q")),
              [], ["maskb"], "consts2")
        R.op("pool", lambda e: e.memset(ones[:], 1.0), [], ["ones"])
        R.op("pool", lambda e: e.memset(onesz[:], 0.0), [], ["onesz"])
        R.op("pool", lambda e: e.memset(onesz[:, 0, 0:64], 1.0), [], ["onesz"])
        R.op("pool", lambda e: e.memset(onesz[:, 1, 64:128], 1.0), [], ["onesz"])
        R.op("pool", lambda e: e.memset(kz[0][:], 0.0), [], [("kz", 0), ("kz", 1), ("kz", 2)])
        R.op("pool", lambda e: e.memset(kz[1][:], 0.0), [], [("kz", 0), ("kz", 1), ("kz", 2)])
        R.op("pool", lambda e: e.memset(vz[0][:], 0.0), [], [("vz", 0), ("vz", 1), ("vz", 2)])
        R.op("pool", lambda e: e.memset(vz[1][:], 0.0), [], [("vz", 0), ("vz", 1), ("vz", 2)])
        R.op("pool", lambda e: e.memset(esink[:], 0.0), [], ["esink"])
        R.op("act", lambda e: e.activation(out=es4[:], in_=vec[:, V_SINK:V_SINK + 4], func=AF.Exp),
             ["vec"], ["es4"])
        for c in range(4):
            R.op("dve", (lambda c: lambda e: e.tensor_scalar(
                out=esink[:, c, :], in0=esink[:, c, :], scalar1=es4[:, c:c + 1], scalar2=None,
                op0=ALU.add))(c), ["es4", "esink"], ["esink"])
        R.op("dve", lambda e: e.tensor_tensor(out=bg2[:], in0=vec[:, V_BPW2:V_BPW2 + 8],
                                               in1=vec[:, V_POST1:V_POST1 + 8], op=ALU.mult),
             ["vec"], ["bg2"])

        corder = [u for u in (list(range(0, 43)) + list(range(73, 85)) + list(range(43, 73)))
                  if UNIT_N[u] > 0]
        cpos = {u: i for i, u in enumerate(corder)}
        conv = {"n": 0, "loaded": 0, "done": False}
        st32 = [ybuf, xr[1]]
        st32k = ["P32_0", "P32_1"]
        gflat = gb[:].rearrange("p s k c -> p (s k c)")
        last_cast = {}
        last_store = {}

        def conv_load(i):
            u = corder[i]; a_ = i % 2; n = UNIT_N[u]
            src32 = st32[a_][:].rearrange("p k t -> p (k t)")[:, 0:n]
            R.dma("pool", lambda e: e.dma_start(out=src32, in_=wsrc[u][:, 0:n]), [], [st32k[a_]],
                  "p32_%d" % a_)

        def conv_step(i):
            while conv["loaded"] < min(i + 2, len(corder)):
                conv_load(conv["loaded"]); conv["loaded"] += 1
            u = corder[i]; a_ = i % 2; b_ = i % 3; n = UNIT_N[u]
            src32 = st32[a_][:].rearrange("p k t -> p (k t)")[:, 0:n]
            dst16 = gflat[:, b_ * UW:b_ * UW + n]
            if i % 2 == 0:
                o = R.op("act", lambda e: e.copy(out=dst16, in_=src32), [st32k[a_]], [("P16", b_)])
            else:
                o = R.op("dve", lambda e: e.tensor_copy(out=dst16, in_=src32), [st32k[a_]], [("P16", b_)])
            last_cast[(a_, o.eng)] = o
            o2 = R.dma("pool", lambda e: e.dma_start(out=wsc[u][:, 0:n], in_=dst16),
                       [("P16", b_)], [("wsc", u)], "p16_%d" % b_)
            last_store[b_] = o2

        def conv_ensure(target):
            target = min(target, len(corder))
            while conv["n"] < target:
                conv_step(conv["n"]); conv["n"] += 1

        def conv_finish():
            if conv["done"]:
                return
            conv_ensure(len(corder))
            conv["done"] = True
            for (a_, en), o in last_cast.items():
                keys = [("y", i) for i in range(KC)] + ["ytail"] if a_ == 0 else [("xr", 1, m) for m in range(KC)]
                for k in keys:
                    R.readers.setdefault(k, {})[("cv", a_, en)] = o
                    o.needs_inc = True
            for b_, o in last_store.items():
                for gs_ in range(2):
                    for part in ("c", "l", "r"):
                        R.readers.setdefault(("g", gs_, part), {})[("cvs", b_)] = o

        state = {"acc": 0, "w": 0, "sq": 0, "S": 0, "pt": 0, "sg": 0}
        bgq = deque()
        tailq = deque()

        def drain_tail(n=None):
            k = 0
            while tailq and (n is None or k < n):
                tailq.popleft()()
                k += 1

        def drain(n=None):
            k = 0
            while bgq and (n is None or k < n):
                bgq.popleft()()
                k += 1

        def nextacc():
            b = ACC_BANKS[state["acc"] % len(ACC_BANKS)]
            state["acc"] += 1
            return b

        def wload(uid):
            if not conv["done"]:
                conv_ensure(max(cpos[uid] + 1, conv["n"] + 2))
                if conv["n"] >= len(corder):
                    conv_finish()
            n = UNIT_N[uid]
            slot = state["w"] % NW
            state["w"] += 1
            R.dma("sync", (lambda uid, slot, n: lambda e: e.dma_start(
                out=wring[:, slot, 0:n], in_=wsc[uid][:, 0:n]))(uid, slot, n),
                [("wsc", uid)], [("w", slot)], "w%d" % slot)
            return slot

        def mm(out, lhsT, rhs, start, stop, reads, writes):
            R.op("pe", lambda e: e.matmul(out, lhsT, rhs, start=start, stop=stop), reads, writes)

        def act(out, in_, func, reads, writes, bias=None, scale=None):
            kw = {}
            if bias is not None:
                kw["bias"] = bias
            if scale is not None:
                kw["scale"] = scale
            R.op("act", lambda e: e.activation(out=out, in_=in_, func=func, **kw), reads, writes)

        def tt(eng, out, in0, in1, op, reads, writes):
            R.op(eng, lambda e: e.tensor_tensor(out=out, in0=in0, in1=in1, op=op), reads, writes)

        def stt(eng, out, in0, scalar, in1, op0, op1, reads, writes):
            R.op(eng, lambda e: e.scalar_tensor_tensor(out=out, in0=in0, scalar=scalar, in1=in1,
                                                       op0=op0, op1=op1), reads, writes)

        def ts(eng, out, in0, s1, s2, op0, op1, reads, writes):
            if s2 is None:
                R.op(eng, lambda e: e.tensor_scalar(out=out, in0=in0, scalar1=s1, scalar2=None,
                                                    op0=op0), reads, writes)
            else:
                R.op(eng, lambda e: e.tensor_scalar(out=out, in0=in0, scalar1=s1, scalar2=s2,
                                                    op0=op0, op1=op1), reads, writes)

        def rstd_from(bank, out_tmp):
            o = tmp[:, out_tmp, :]
            act(o, ps[bank][:], AF.Ln, [("ps", bank)], [("tmp", out_tmp)], scale=1.0 / D, bias=EPS)
            act(o, o, AF.Exp, [("tmp", out_tmp)], [("tmp", out_tmp)], scale=-0.5)

        def stats_mm(bank, kc, src_ap, src_key, bias=None):
            i = state["sq"] % 2
            state["sq"] += 1
            act(sq[:, i, :], src_ap, AF.Square, [src_key], [("sq", i)], bias=bias)
            mm(ps[bank][:], ones[:], sq[:, i, :], kc == 0, kc == KC - 1,
               [("sq", i), "ones"], [("ps", bank)])

        def prenorm(src, srckey, gcol, rt, dst, dstkey):
            bank = 3
            for kc in range(KC):
                stats_mm(bank, kc, src[:, kc, :], srckey(kc))
            rstd_from(bank, rt)
            for kc in range(KC):
                stt("dve", dst[:, kc, :], src[:, kc, :], vcol(gcol + kc), tmp[:, rt, :],
                    ALU.mult, ALU.mult, [srckey(kc), ("tmp", rt), "vec"], [dstkey(kc)])

        hkey = lambda kc: ("h", kc)
        mkey = lambda kc: ("mix", kc)

        def proj_chunk(bank, slot, mw, sub, rhs_of, rhs_key, nk):
            for kc in range(nk):
                mm(ps[bank][:], wring[:, slot, kc * mw + sub * 128: kc * mw + sub * 128 + 128],
                   rhs_of(kc), kc == 0, kc == nk - 1, [("w", slot), rhs_key(kc)], [("ps", bank)])

        def postnorm_residual(xslot, gcol, produce, rt, bias_col=None, bgcol=None, ndrain=0, final=False):
            sbank = 3
            banks = {}
            pend = []

            def evac(m):
                bk = banks[m]
                if bias_col is None:
                    act(fbuf[:, m, :], ps[bk][:], AF.Identity, [("ps", bk), "vec"], [("fbuf", m)],
                        scale=vcol(gcol + m))
                    sqb = None
                else:
                    act(fbuf[:, m, :], ps[bk][:], AF.Identity, [("ps", bk), "vec", "bg2"],
                        [("fbuf", m)], scale=vcol(gcol + m), bias=bgcol[:, m:m + 1])
                    sqb = vcol(bias_col + m)
                i = state["sq"] % 2
                state["sq"] += 1
                act(sq[:, i, :], ps[bk][:], AF.Square, [("ps", bk), "vec"], [("sq", i)], bias=sqb)
                pend.append((m, i))

            def flush_one():
                m, i = pend.pop(0)
                mm(ps[sbank][:], ones[:], sq[:, i, :], m == 0, m == KC - 1,
                   [("sq", i), "ones"], [("ps", sbank)])

            for m in range(KC):
                banks[m] = produce(m)
                evac(m)
                if len(pend) > 1:
                    flush_one()
                drain(ndrain)
            while pend:
                flush_one()
            rstd_from(sbank, rt)
            for m in range(KC):
                tt("pool" if final else "dve", fbuf[:, m, :], fbuf[:, m, :], tmp[:, rt, :], ALU.mult,
                   [("fbuf", m), ("tmp", rt)], [("fbuf", m)])
                if m % 4 == 3 and not final:
                    tt("dve", xr[xslot][:, m, :], xr[xslot][:, m, :], fbuf[:, m, :], ALU.add,
                       [("fbuf", m), ("xr", xslot, m)], [("xr", xslot, m)])
                else:
                    tt("pool", xr[xslot][:, m, :], xr[xslot][:, m, :], fbuf[:, m, :], ALU.add,
                       [("fbuf", m), ("xr", xslot, m)], [("xr", xslot, m)])

        def ffn(layer, xslot, rt, ndrain=0, final=False):
            prenorm(xr[xslot], lambda kc: ("xr", xslot, kc), V_FPRE0 + 8 * layer, rt, hb, hkey)
            for j in range(11):
                sg_ = wload(U_G(layer, j))
                su_ = wload(U_U(layer, j))
                for sub in range(2):
                    ch = 2 * j + sub
                    bgt = nextacc()
                    bup = nextacc()
                    proj_chunk(bgt, sg_, 256, sub, lambda kc: hb[:, kc, :], hkey, KC)
                    proj_chunk(bup, su_, 256, sub, lambda kc: hb[:, kc, :], hkey, KC)
                    i = state["sg"] % 2
                    state["sg"] += 1
                    act(sgb[:, i, :], ps[bgt][:], AF.Silu, [("ps", bgt)], [("sg", i)])
                    tt("dve", actb[:, ch, :], sgb[:, i, :], ps[bup][:], ALU.mult,
                       [("sg", i), ("ps", bup)], [("act", ch)])
                    drain(ndrain)

            def produce(m):
                sl = wload(U_D(layer, m))
                bk = nextacc()
                proj_chunk(bk, sl, 128, 0, lambda kc: actb[:, kc, :], lambda kc: ("act", kc), FC)
                return bk
            postnorm_residual(xslot, V_FPOST0 + 8 * layer, produce, rt, ndrain=ndrain, final=final)

        def A_load(s):
            R.dma("act", lambda e: e.dma_start(out=fbuf[:], in_=xT[s]), [],
                  [("fbuf", m) for m in range(KC)], "xa")
            R.dma("act", lambda e: e.dma_start(out=cosb[:], in_=cosT[s]), [], ["cos"], "cos")
            R.dma("act", lambda e: e.dma_start(out=sinb[:], in_=sinT[s]), [], ["sin"], "sin")

        def A_prep(s):
            prenorm(fbuf, lambda kc: ("fbuf", kc), V_PRE0, 0, mix, mkey)

        def A_proj(s, mid_hook=None):
            qs = s % 2
            ks = s % 3
            hr = lambda kc: mix[:, kc, :]
            hk = mkey
            for c in range(5):
                sl = wload(U_IN(c))
                b0 = nextacc(); b1 = nextacc()
                proj_chunk(b0, sl, 256, 0, hr, hk, KC)
                proj_chunk(b1, sl, 256, 1, hr, hk, KC)
                ta = 2 * (c % 2); tb = ta + 1
                tt("dve", fbuf[:, ta, :], ps[b0][:], cosb[:], ALU.mult, [("ps", b0), "cos"], [("fbuf", ta)])
                tt("dve", fbuf[:, tb, :], ps[b1][:], sinb[:], ALU.mult, [("ps", b1), "sin"], [("fbuf", tb)])
                if c < 4:
                    tt("pool", qb_[:, qs, c, :], fbuf[:, ta, :], fbuf[:, tb, :], ALU.add,
                       [("fbuf", ta), ("fbuf", tb)], [("q", qs, c)])
                else:
                    tt("pool", kz[0][0:64, ks, :], fbuf[0:64, ta, :], fbuf[0:64, tb, :], ALU.add,
                       [("fbuf", ta), ("fbuf", tb)], [("kz", ks)])
                    tt("pool", kz[1][64:128, ks, :], fbuf[64:128, ta, :], fbuf[64:128, tb, :], ALU.add,
                       [("fbuf", ta), ("fbuf", tb)], [("kz", ks)])
                if mid_hook is not None:
                    mid_hook(c)
            sl = wload(U_IN(5))
            bv = nextacc()
            for blk in range(4):
                for kc in range(KC):
                    mm(ps[bv][:, blk * 128:(blk + 1) * 128], mix[:, kc, blk * 128:(blk + 1) * 128],
                       wring[:, sl, kc * 256: kc * 256 + 128], kc == 0, kc == KC - 1,
                       [("w", sl), ("mix", kc)], [("ps", bv)])
            pv3 = ps[bv][:].rearrange("p (b d) -> p b d", b=4)
            R.op("act", lambda e: e.copy(out=vz[0][:, ks * 4:ks * 4 + 4, 0:64], in_=pv3[:, :, 0:64]),
                 [("ps", bv)], [("vz", ks)])
            R.op("act", lambda e: e.copy(out=vz[1][:, ks * 4:ks * 4 + 4, 64:128], in_=pv3[:, :, 64:128]),
                 [("ps", bv)], [("vz", ks)])
            us = s % 3

            def u_chunk(sl, mw, sub, gi):
                bk = nextacc()
                proj_chunk(bk, sl, mw, sub, hr, hk, KC)
                R.op("act", lambda e: e.copy(out=ub[:, us, gi, 8:520], in_=ps[bk][:]),
                     [("ps", bk)], [("u", us, "c")])
            u_chunk(sl, 256, 1, 0)
            drain(4)
            sl = wload(U_IN(6))
            u_chunk(sl, 256, 0, 1)
            drain(4)
            u_chunk(sl, 256, 1, 2)
            drain(4)
            sl = wload(U_IN(7))
            u_chunk(sl, 128, 0, 3)
            drain(4)
            halo(ub, "u", s, 8, 8, 520)

        def halo(buf, name, s, hw, c0, c1, RS=3):
            sl = s % RS
            if s == 0:
                R.op("pool", lambda e: e.memset(buf[:, 0, :, c0 - hw:c0], 0.0), [], [(name, 0, "l")])
            if s == NT - 1:
                R.op("pool", lambda e: e.memset(buf[:, sl, :, c1:c1 + hw], 0.0), [], [(name, sl, "r")])
            if s > 0:
                dsl = (s - 1) % RS
                dst = buf[:, dsl, :, c1:c1 + hw]
                src = buf[:, sl, :, c0:c0 + hw]
                if s % 4 == 0:
                    ts("pool", dst, src, vcol(V_FLAG), None, ALU.mult, None,
                       [(name, sl, "c"), "vec"], [(name, dsl, "r")])
                else:
                    R.op("pool", lambda e: e.tensor_copy(out=dst, in_=src),
                         [(name, sl, "c")], [(name, dsl, "r")])
            if s < NT - 1:
                dsl = (s + 1) % RS
                dst2 = buf[:, dsl, :, c0 - hw:c0]
                src2 = buf[:, sl, :, c1 - hw:c1]
                if (s + 1) % 4 == 0:
                    ts("pool", dst2, src2, vcol(V_FLAG), None, ALU.mult, None,
                       [(name, sl, "c"), "vec"], [(name, dsl, "l")])
                else:
                    R.op("pool", lambda e: e.tensor_copy(out=dst2, in_=src2),
                         [(name, sl, "c")], [(name, dsl, "l")])

        CM = 496

        def enqueue_conv_main(t):
            gs = t % 2
            gk = [("g", gs, "c"), ("g", gs, "l")]
            for j in range(31):
                for i in range(KC):
                    def thunk(i=i, j=j):
                        src = gb[:, gs, i, j + 1: j + 1 + CM]
                        wc = vcol(V_WDW + i * 31 + j)
                        if j == 0:
                            ts("dve", ybuf[:, i, 0:CM], src, wc, vcol(V_BDW + i), ALU.mult, ALU.add,
                               gk + ["vec"], [("y", i)])
                        else:
                            stt("dve", ybuf[:, i, 0:CM], src, wc, ybuf[:, i, 0:CM], ALU.mult, ALU.add,
                                gk + ["vec", ("y", i)], [("y", i)])
                    bgq.append(thunk)

        def conv_tail(t):
            gs = t % 2
            gk = [("g", gs, "c"), ("g", gs, "r")]
            ykt = ["ytail"]
            yt = ybuf[:, :, CM:T]
            for j in range(31):
                def thunk(j=j):
                    src = gb[:, gs, :, CM + j + 1: CM + j + 1 + 16]
                    wbc = bcast_cols(V_WDW + j, 31, KC, 16)
                    tsl = j % 4
                    tk = ("ttail", tsl)
                    tt("dve", ttail[:, tsl], src, wbc, ALU.mult, gk + ["vec"], [tk])
                    if j == 0:
                        tt("dve", yt, ttail[:, tsl], bcast_cols(V_BDW, 1, KC, 16), ALU.add, [tk, "vec"],
                           ykt + [("y", i) for i in range(KC)])
                    else:
                        tt("dve", yt, yt, ttail[:, tsl], ALU.add, [tk] + ykt, ykt)
                tailq.append(thunk)

        def stageB(t):
            xs = t % 2
            qs = t % 2
            us = t % 3
            U = ub[:, us]
            fb = fbuf[:].rearrange("p k t -> p (k t)")
            P1 = fb[:, 0:4 * 528].rearrange("p (g c) -> p g c", g=4)
            P2 = fb[:, 4 * 528:7 * 528].rearrange("p (g c) -> p g c", g=3)
            fall = [("fbuf", m) for m in range(KC)]
            ukeys = [("u", us, "c"), ("u", us, "l"), ("u", us, "r")]
            res = [P1[:, 0, 7:519], P2[:, 0, 6:518], P1[:, 2, 4:516], P2[:, 2, 0:512]]

            def pool_p0():
                tt("dve", P1[:, 0:4, 0:527], U[:, :, 0:527], U[:, :, 1:528], ALU.add, ukeys, fall)
                tt("dve", P2[:, 0:3, 0:525], P1[:, 1:4, 0:525], P1[:, 1:4, 2:527], ALU.add, fall, fall)

            def pool_p1():
                tt("dve", P1[:, 2:4, 0:521], P2[:, 1:3, 0:521], P2[:, 1:3, 4:525], ALU.add, fall, fall)
                tt("dve", P2[:, 2:3, 0:513], P1[:, 3:4, 0:513], P1[:, 3:4, 8:521], ALU.add, fall, fall)
                if t % 4 == 0:
                    tb = 0 if t == 0 else 2
                    for gi in range(4):
                        cc = V_CORR + tb * 32 + gi * 8
                        tt("dve", res[gi][:, 0:8], res[gi][:, 0:8], vec[:, cc:cc + 8], ALU.mult,
                           fall + ["vec"], fall)
                if t % 4 == 3:
                    tb = 1 if t == NT - 1 else 3
                    for gi in range(4):
                        cc = V_CORR + tb * 32 + gi * 8
                        tt("dve", res[gi][:, 504:512], res[gi][:, 504:512], vec[:, cc:cc + 8], ALU.mult,
                           fall + ["vec"], fall)

            def pool_stt(gis):
                for gi in gis:
                    w = (2, 4, 8, 16)[gi]
                    stt("dve", hb[:, gi, :], res[gi], 1.0 / w, U[:, gi, 8:520], ALU.mult, ALU.subtract,
                        fall + ukeys, [("h", gi)])
            pool_pieces = [lambda: None, pool_p0, pool_p1, lambda: pool_stt((0, 1, 2, 3))]
            for qb in range(4):
                n = 4 * t + qb
                contribs = []
                for j in (n - 1, n, n + 1):
                    if 0 <= j < 64:
                        for g in range(2):
                            contribs.append((g, j))
                rhs_q = qb_[:, qs, :, qb * 128:(qb + 1) * 128]
                info = []
                bnum, bden = (6, 7)

                def emitS(i):
                    g, j = contribs[i]
                    sbk = 4 + state["S"] % 2
                    state["S"] += 1
                    ksl = (j // 4) % 3
                    ko = (j % 4) * 128
                    mm(ps[sbk][:].rearrange("p (c q) -> p c q", c=4), kz[g][:, ksl, ko:ko + 128], rhs_q,
                       True, True, [("kz", ksl)] + [("q", qs, c) for c in range(4)], [("ps", sbk)])
                    pi = state["pt"] % 3
                    state["pt"] += 1
                    act(ptb[:, pi, :], ps[sbk][:], AF.Exp, [("ps", sbk)], [("pt", pi)], scale=0.125)
                    if j != n:
                        if j < n:
                            mi = 2 if (n % 16 == 0) else 0
                        else:
                            mi = 3 if (n % 16 == 15) else 1
                        p3 = ptb[:, pi, :].rearrange("p (c q) -> p c q", c=4)
                        tt("dve", p3, p3, mask_bc(mi), ALU.mult, [("pt", pi), "maskb"], [("pt", pi)])
                    info.append(pi)

                def emitPV(i):
                    g, j = contribs[i]
                    pi = info[i]
                    vsl = ((j // 4) % 3) * 4 + (j % 4)
                    first = i == 0
                    last = i == len(contribs) - 1
                    mm(ps[bnum][:], vz[g][:, vsl, :], ptb[:, pi, :], first, last,
                       [("vz", (j // 4) % 3), ("pt", pi)], [("ps", bnum)])
                    mm(ps[bden][:], onesz[:, g, :], ptb[:, pi, :], first, last,
                       ["onesz", ("pt", pi)], [("ps", bden)])

                emitS(0)
                for i in range(len(contribs)):
                    if i + 1 < len(contribs):
                        emitS(i + 1)
                    emitPV(i)
                    if i == 2:
                        pool_pieces[qb]()
                tt("dve", tmp[:, 1, :], ps[bden][:], esink[:].rearrange("p c q -> p (c q)"), ALU.add,
                   [("ps", bden), "esink"], [("tmp", 1)])
                R.op("dve", lambda e: e.tensor_copy(out=tmp[:, 2, :], in_=ps[bnum][:]), [("ps", bnum)], [("tmp", 2)])
                act(tmp[:, 1, :], tmp[:, 1, :], AF.Ln, [("tmp", 1)], [("tmp", 1)])
                act(tmp[:, 1, :], tmp[:, 1, :], AF.Exp, [("tmp", 1)], [("tmp", 1)], scale=-1.0)
                tt("dve", mix[:, 0:4, qb * 128:(qb + 1) * 128],
                   tmp[:, 2, :].rearrange("p (c q) -> p c q", c=4),
                   tmp[:, 1, :].rearrange("p (c q) -> p c q", c=4), ALU.mult,
                   [("tmp", 2), ("tmp", 1)], [("mix", c) for c in range(4)])
            R.dma("act", lambda e: e.dma_start(out=xr[xs][:], in_=xT[t]), [],
                  [("xr", xs, m) for m in range(KC)], "xr%d" % xs)
            slp = wload(U_POOL)
            for gi, w in enumerate((2, 4, 8, 16)):
                bk = nextacc()
                mm(ps[bk][:], wring[:, slp, gi * 128:(gi + 1) * 128], hb[:, gi, :], True, True,
                   [("w", slp), ("h", gi)], [("ps", bk)])
                act(mix[:, 4 + gi, :], ps[bk][:], AF.Identity, [("ps", bk), "vec"], [("mix", 4 + gi)],
                    scale=vcol(V_PSCALE + gi))
            wslots = {}

            def produce_out(m):
                if m % 2 == 0:
                    wslots[0] = wload(U_OUT(m // 2))
                bk = nextacc()
                proj_chunk(bk, wslots[0], 256, m % 2, lambda kc: mix[:, kc, :], mkey, KC)
                return bk
            postnorm_residual(xs, V_POST0, produce_out, 0, ndrain=2)
            ffn(0, xs, 0, ndrain=3)
            conv_finish()
            prenorm(xr[xs], lambda kc: ("xr", xs, kc), V_PRE1, 0, hb, hkey)
            gs = t % 2
            for i in range(KC):
                if i == 1 and t + 2 < NT:
                    A_load(t + 2)
                if i == 4 and t + 2 < NT:
                    A_prep(t + 2)
                sl = wload(U_PW1(i))
                ba = nextacc(); bb = nextacc()
                proj_chunk(ba, sl, 256, 0, lambda kc: hb[:, kc, :], hkey, KC)
                proj_chunk(bb, sl, 256, 1, lambda kc: hb[:, kc, :], hkey, KC)
                si = state["sg"] % 2
                state["sg"] += 1
                act(sgb[:, si, :], ps[bb][:], AF.Sigmoid, [("ps", bb), "vec"], [("sg", si)],
                    bias=vcol(V_BPW1 + 8 + i))
                stt("dve", gb[:, gs, i, 16:528], ps[ba][:], vcol(V_BPW1 + i), sgb[:, si, :],
                    ALU.add, ALU.mult, [("ps", ba), ("sg", si), "vec"], [("g", gs, "c")])
                if i < 4:
                    drain(6)
            drain()
            halo(gb, "g", t, 16, 16, 528, RS=2)

        def C_ln(t):
            yk = lambda i: ("y", i)
            bsum = nextacc(); bsq = 3
            for i in range(KC):
                si = state["sg"] % 2
                state["sg"] += 1
                R.op("act", (lambda si, i: lambda e: e.copy(out=sgb[:, si, :], in_=ybuf[:, i, :]))(si, i),
                     [yk(i), "ytail"], [("sg", si)])
                mm(ps[bsum][:], ones[:], sgb[:, si, :], i == 0, i == KC - 1, [("sg", si), "ones"],
                   [("ps", bsum)])
                stats_mm(bsq, i, ybuf[:, i, :], yk(i))
            mu = tmp[:, 1, :]; msq = tmp[:, 2, :]; rs = tmp[:, 3, :]; nmr = tmp[:, 2, :]
            ts("dve", mu, ps[bsum][:], 1.0 / D, None, ALU.mult, None, [("ps", bsum)], [("tmp", 1)])
            tt("dve", msq, mu, mu, ALU.mult, [("tmp", 1)], [("tmp", 2)])
            stt("dve", rs, ps[bsq][:], 1.0 / D, msq, ALU.mult, ALU.subtract, [("ps", bsq), ("tmp", 2)],
                [("tmp", 3)])
            act(rs, rs, AF.Ln, [("tmp", 3)], [("tmp", 3)], bias=EPS)
            act(rs, rs, AF.Exp, [("tmp", 3)], [("tmp", 3)], scale=-0.5)
            stt("dve", nmr, mu, -1.0, rs, ALU.mult, ALU.mult, [("tmp", 1), ("tmp", 3)], [("tmp", 2)])
            for i in (0, 1, 5, 2, 3, 6, 4, 7):
                ne = "pool" if i >= 5 else "dve"
                tt(ne, ybuf[:, i, :], ybuf[:, i, :], rs, ALU.mult, [yk(i), ("tmp", 3)], [yk(i)])
                tt(ne, ybuf[:, i, :], ybuf[:, i, :], nmr, ALU.add, [yk(i), ("tmp", 2)], [yk(i)])
                act(hb[:, i, :], ybuf[:, i, :], AF.Silu, [yk(i), "vec"], [("h", i)],
                    scale=vcol(V_LNG + i), bias=vcol(V_LNB + i))

        def C_main(t):
            xs = t % 2
            wslots = {}

            def produce_pw2(m):
                if m % 2 == 0:
                    wslots[0] = wload(U_PW2(m // 2))
                bk = nextacc()
                proj_chunk(bk, wslots[0], 256, m % 2, lambda kc: hb[:, kc, :], hkey, KC)
                return bk
            postnorm_residual(xs, V_POST1, produce_pw2, 0, bias_col=V_BPW2, bgcol=bg2, ndrain=3)
            ffn(1, xs, 0, ndrain=3, final=True)
            R.dma("pool", lambda e: e.dma_start(out=yT[t], in_=xr[xs][:]),
                  [("xr", xs, m) for m in range(KC)], [("yT", t)], "st%d" % xs)

        A_load(0); A_prep(0); A_proj(0)
        A_load(1); A_prep(1); A_proj(1)
        for s in range(1, NT + 2):
            tb_, tc_ = s - 1, s - 2
            if 0 <= tb_ < NT:
                stageB(tb_)
            has_c = 0 <= tc_ < NT
            if has_c:
                conv_tail(tc_)
                drain_tail(10)

            enq = {"done": False}

            def hook(c, tc_=tc_, tb_=tb_):
                if c < 3:
                    drain_tail(16)
                elif c == 3:
                    drain(); drain_tail()
                    C_ln(tc_)
                    if 0 <= tb_ < NT:
                        enqueue_conv_main(tb_)
                        enq["done"] = True
                else:
                    drain(5)
            if s + 1 < NT:
                A_proj(s + 1, mid_hook=hook if has_c else None)
            elif has_c:
                drain(); drain_tail()
                C_ln(tc_)
            if 0 <= tb_ < NT and not enq["done"]:
                enqueue_conv_main(tb_)
            if has_c:
                C_main(tc_)
        drain()

        R.finalize()
        lastst = {}
        for o in R.ops["pool"]:
            if o.is_dma and o.sem in ("st0", "st1"):
                lastst[o.sem] = o

        with nc.Block() as block:
            @block.sync
            def _(eng):
                R.emit("sync", eng, sems)
                for o in lastst.values():
                    eng.wait_ge(sems[o.sem], o.semval)

            @block.tensor
            def _(eng):
                R.emit("pe", eng, sems)

            @block.scalar
            def _(eng):
                R.emit("act", eng, sems)

            @block.vector
            def _(eng):
                R.emit("dve", eng, sems)

            @block.gpsimd
            def _(eng):
                R.emit("pool", eng, sems)
    return nc


def _unit_cols(W, cols, kcs):
    sub = W[:, cols]
    mw = sub.shape[1]
    return sub.reshape(kcs, 128, mw).transpose(1, 0, 2).reshape(128, kcs * mw)


def _build_wsrc(inp):
    wsrc = np.zeros((NU, 128, UW), np.float32)
    w_in = inp["w_in"][0]
    hd = 64

    def qcols(c, swap):
        d = np.arange(64)
        dd = (d + 32) % 64 if swap else d
        return np.concatenate([c * hd + dd, (4 + c) * hd + dd])

    def kcols(swap):
        d = np.arange(64)
        dd = (d + 32) % 64 if swap else d
        return np.concatenate([512 + dd, 512 + 64 + dd])
    chunks = []
    for c in range(4):
        chunks.append(qcols(c, False)); chunks.append(qcols(c, True))
    chunks.append(kcols(False)); chunks.append(kcols(True))
    chunks.append(np.arange(640, 768))
    for gi in range(4):
        chunks.append(768 + gi * 128 + np.arange(128))
    for u in range(8):
        cols = np.concatenate(chunks[2 * u: 2 * u + 2])
        a = _unit_cols(w_in, cols, 8)
        wsrc[U_IN(u), :, :a.shape[1]] = a
    wp = inp["w_pool"][0]
    wsrc[U_POOL, :, :512] = wp.transpose(1, 0, 2).reshape(128, 512)
    rowperm = []
    for c in range(4):
        rowperm.extend(list(c * 64 + np.arange(64)))
        rowperm.extend(list((4 + c) * 64 + np.arange(64)))
    rowperm.extend(list(512 + np.arange(512)))
    w_out = inp["w_out"][0][np.array(rowperm), :]
    for i in range(4):
        wsrc[U_OUT(i), :, :2048] = _unit_cols(w_out, np.arange(256 * i, 256 * i + 256), 8)
    for l in range(2):
        wg = inp["ffn_w_gate"][l]; wu = inp["ffn_w_up"][l]; wd = inp["ffn_w_down"][l]
        for j in range(11):
            cols = np.arange(256 * j, 256 * j + 256)
            wsrc[U_G(l, j), :, :2048] = _unit_cols(wg, cols, 8)
            wsrc[U_U(l, j), :, :2048] = _unit_cols(wu, cols, 8)
        for m in range(8):
            wsrc[U_D(l, m), :, :] = _unit_cols(wd, np.arange(128 * m, 128 * m + 128), 22)
    pw1 = inp["conv_w_pw1"][0]
    for i in range(8):
        cols = np.concatenate([np.arange(128 * i, 128 * i + 128), 1024 + np.arange(128 * i, 128 * i + 128)])
        wsrc[U_PW1(i), :, :2048] = _unit_cols(pw1, cols, 8)
    pw2 = inp["conv_w_pw2"][0]
    for i in range(4):
        wsrc[U_PW2(i), :, :2048] = _unit_cols(pw2, np.arange(256 * i, 256 * i + 256), 8)
    return wsrc


def _colvec(v):
    n = v.shape[0] // 128
    return v.reshape(n, 128).T


def _build_vecs(inp, is_prompt):
    vec = np.zeros((128, NV), np.float32)
    vec[:, V_PRE0:V_PRE0 + 8] = _colvec(inp["mix_pre_g"][0])
    vec[:, V_PRE1:V_PRE1 + 8] = _colvec(inp["mix_pre_g"][1])
    vec[:, V_POST0:V_POST0 + 8] = _colvec(inp["mix_post_g"][0])
    vec[:, V_POST1:V_POST1 + 8] = _colvec(inp["mix_post_g"][1])
    vec[:, V_FPRE0:V_FPRE0 + 8] = _colvec(inp["ffn_pre_g"][0])
    vec[:, V_FPRE1:V_FPRE1 + 8] = _colvec(inp["ffn_pre_g"][1])
    vec[:, V_FPOST0:V_FPOST0 + 8] = _colvec(inp["ffn_post_g"][0])
    vec[:, V_FPOST1:V_FPOST1 + 8] = _colvec(inp["ffn_post_g"][1])
    vec[:, V_PSCALE:V_PSCALE + 4] = _colvec(inp["pool_scale"][0])
    vec[:, V_BPW1:V_BPW1 + 16] = _colvec(inp["conv_b_pw1"][0])
    vec[:, V_BDW:V_BDW + 8] = _colvec(inp["conv_b_dw"][0])
    vec[:, V_LNG:V_LNG + 8] = _colvec(inp["conv_ln_g"][0])
    vec[:, V_LNB:V_LNB + 8] = _colvec(inp["conv_ln_b"][0])
    vec[:, V_BPW2:V_BPW2 + 8] = _colvec(inp["conv_b_pw2"][0])
    sink = inp["attn_sink"][0]
    for c in range(4):
        vec[0:64, V_SINK + c] = sink[c]
        vec[64:128, V_SINK + c] = sink[4 + c]
    vec[:, V_FLAG] = 1.0 if is_prompt else 0.0
    wdw = inp["conv_w_dw"][0]
    for i in range(8):
        vec[:, V_WDW + i * 31: V_WDW + (i + 1) * 31] = wdw[:, i * 128:(i + 1) * 128].T
    ledge = np.ones((4, 8), np.float32); redge = np.ones((4, 8), np.float32)
    for gi, w in enumerate((2, 4, 8, 16)):
        half = w // 2
        for i in range(8):
            if i < half:
                ledge[gi, i] = np.float32(w) / np.float32(i + half)
            r = 7 - i
            if r < half - 1:
                redge[gi, i] = np.float32(w) / np.float32(r + 1 + half)
    lint = np.ones((4, 8), np.float32) if is_prompt else ledge
    rint = np.ones((4, 8), np.float32) if is_prompt else redge
    for tb, tab in enumerate((ledge, redge, lint, rint)):
        vec[:, V_CORR + tb * 32: V_CORR + (tb + 1) * 32] = tab.reshape(1, 32)
    return vec


def _build_rope(pos):
    half = 32
    inv_freq = (np.float32(10000.0) ** (-np.arange(0, half, dtype=np.float32) * np.float32(2.0) / np.float32(64))).astype(np.float32)
    ang = pos.astype(np.float32)[:, None] * inv_freq[None, :]
    cos = np.cos(ang).astype(np.float32); sin = np.sin(ang).astype(np.float32)
    p = np.arange(128)
    f = p % 32
    sign = np.where((p % 64) < 32, -1.0, 1.0).astype(np.float32)
    cosT = cos[:, f].T
    sinT = (sin[:, f] * sign[None, :]).T
    S = pos.shape[0]
    cosT = cosT.reshape(128, NT, T).transpose(1, 0, 2)
    sinT = sinT.reshape(128, NT, T).transpose(1, 0, 2)
    return np.ascontiguousarray(cosT), np.ascontiguousarray(sinT)


def _to_featmajor(xc):
    return np.ascontiguousarray(xc.reshape(NT, T, KC, 128).transpose(0, 3, 2, 1))


def _from_featmajor(y):
    return np.ascontiguousarray(y.transpose(0, 3, 2, 1).reshape(NT * T, D))


_NC_CACHE = {}


def kernel(**inputs):
    inp = {k: np.asarray(v) for k, v in inputs.items()}
    xp = inp["x_prompt"].astype(np.float32, copy=False)
    xs = inp["x_sample"].astype(np.float32, copy=False)
    wsrc = _build_wsrc(inp)
    kl = np.arange(128)[:, None]; ql = np.arange(128)[None, :]
    mP = (kl >= ql).astype(np.float32); mN = (kl <= ql).astype(np.float32)
    zero = np.zeros_like(mP)
    in_maps = []
    for c in range(8):
        is_prompt = c < 4
        if is_prompt:
            xc = xp[c]
            pos = np.arange(8192)
        else:
            xc = xs[4 * (c - 4): 4 * (c - 4) + 4].reshape(8192, D)
            pos = np.arange(8192) % 2048
        cosT, sinT = _build_rope(pos)
        masks = np.stack([mP, mN, mP if is_prompt else zero, mN if is_prompt else zero]).astype(ml_dtypes.bfloat16)
        in_maps.append({
            "xT": _to_featmajor(xc),
            "cosT": cosT, "sinT": sinT,
            "wsrc": wsrc,
            "vecs": _build_vecs(inp, is_prompt),
            "masks": masks,
        })
    if "nc" not in _NC_CACHE:
        _NC_CACHE["nc"] = build_program()
    nc = _NC_CACHE["nc"]
    res = run_bass_kernel_spmd(nc, in_maps, core_ids=list(range(8)))
    outs = [np.asarray(r["yT"]) for r in res.results]
    y_prompt = np.stack([_from_featmajor(outs[c]) for c in range(4)]).astype(np.float32)
    ys = [_from_featmajor(outs[c]).reshape(4, 2048, D) for c in range(4, 8)]
    y_sample = np.concatenate(ys, axis=0).astype(np.float32)
    return (y_prompt, y_sample)
```
